# Optimizing a Trainium2 kernel written in Bass

```python
import math
import jax, jax.numpy as jnp
from jax import lax
import numpy as np

D_MODEL = 1024
BATCH = 2
SEQ = 8192
DEPTH = 1

HEAD_DIM = 64
DIL_GROUPS = ((128, 1), (512, 4), (2048, 16))
DIL_HEADS_PER_GROUP = 4
N_DIL_HEADS = DIL_HEADS_PER_GROUP * len(DIL_GROUPS)
N_SB_HEADS = 8
DIL_WIDTH = N_DIL_HEADS * HEAD_DIM
DIL_OUT_WIDTH = DIL_HEADS_PER_GROUP * HEAD_DIM
SB_WIDTH = N_SB_HEADS * HEAD_DIM
D_FF = 4 * D_MODEL
BLOCK = 128
RMS_EPS = 1e-6
NEG_INF = -1e30
IN_COLS = 3 * DIL_WIDTH + 3 * SB_WIDTH + 2 * D_MODEL
SPLITS = (DIL_WIDTH, 2 * DIL_WIDTH, 3 * DIL_WIDTH,
          3 * DIL_WIDTH + SB_WIDTH, 3 * DIL_WIDTH + 2 * SB_WIDTH, 3 * DIL_WIDTH + 3 * SB_WIDTH,
          3 * DIL_WIDTH + 3 * SB_WIDTH + D_MODEL)

kernel_name = "hybrid_dilated_stickbreaking_gated_block"


def rmsnorm(x, g):
    xf = x.astype(jnp.float32)
    y = xf * lax.rsqrt(jnp.mean(xf * xf, axis=-1, keepdims=True) + RMS_EPS)
    return (y * g.astype(jnp.float32)).astype(x.dtype)


def alibi_slopes(n):
    return jnp.exp2(-8.0 * jnp.arange(1, n + 1, dtype=jnp.float32) / n)


def dilated_window_group(q, k, v, slopes, window, dilation):
    b, s, h, dh = q.shape
    n_steps = window // dilation
    nb = -(-s // (dilation * BLOCK))
    sub_len = nb * BLOCK
    s_pad = sub_len * dilation

    def to_blocks(t):
        t = jnp.pad(t, ((0, 0), (0, s_pad - s), (0, 0), (0, 0)))
        t = t.reshape(b, sub_len, dilation, h, dh)
        t = t.transpose(0, 2, 3, 1, 4)
        return t.reshape(b, dilation, h, nb, BLOCK, dh)

    qb, kb, vb = to_blocks(q), to_blocks(k), to_blocks(v)

    def with_prev(t):
        prev = jnp.pad(t[:, :, :, :-1], ((0, 0), (0, 0), (0, 0), (1, 0), (0, 0), (0, 0)))
        return jnp.concatenate([prev, t], axis=4)

    kk, vv = with_prev(kb), with_prev(vb)
    scores = jnp.einsum('brhnqd,brhnkd->brhnqk', qb, kk).astype(jnp.float32) / math.sqrt(dh)
    qi = jnp.arange(BLOCK)[:, None]
    kj = jnp.arange(2 * BLOCK)[None, :]
    steps = qi + BLOCK - kj
    key_sub_idx = jnp.arange(nb)[:, None, None] * BLOCK + kj[None] - BLOCK
    valid = (steps >= 0) & (steps <= n_steps) & (key_sub_idx >= 0)
    bias = -slopes[:, None, None].astype(jnp.float32) * (steps * dilation).astype(jnp.float32)
    logits = scores + bias[None, None, :, None]
    logits = jnp.where(valid[None, None, None], logits, NEG_INF)
    lse = jax.nn.logsumexp(logits, axis=-1)
    p = jnp.exp(logits - lse[..., None])
    o = jnp.einsum('brhnqk,brhnkd->brhnqd', p.astype(v.dtype), vv)

    def from_blocks(t):
        extra = t.shape[5:]
        t = t.reshape((b, dilation, h, sub_len) + extra)
        t = jnp.moveaxis(t, 3, 1)
        t = t.reshape((b, s_pad, h) + extra)
        return t[:, :s]

    return from_blocks(o), from_blocks(lse)


def dilated_attention(q, k, v):
    b, s = q.shape[:2]
    slopes = alibi_slopes(N_DIL_HEADS)
    outs, lses = [], []
    for g, (window, dilation) in enumerate(DIL_GROUPS):
        sl = slice(g * DIL_HEADS_PER_GROUP, (g + 1) * DIL_HEADS_PER_GROUP)
        o, l = dilated_window_group(q[:, :, sl], k[:, :, sl], v[:, :, sl], slopes[sl], window, dilation)
        outs.append(o)
        lses.append(l)
    o_all = jnp.stack(outs, axis=0).astype(jnp.float32)
    w = jax.nn.softmax(jnp.stack(lses, axis=0), axis=0)
    out = jnp.sum(w[..., None] * o_all, axis=0).astype(q.dtype)
    return out.reshape(b, s, DIL_OUT_WIDTH)


def stick_breaking_attention(q, k, v):
    b, s, h, dh = q.shape
    nb = s // BLOCK
    scale = 1.0 / math.sqrt(dh)
    kt = k.transpose(0, 2, 1, 3)
    vt = v.transpose(0, 2, 1, 3)
    q_blocks = q.transpose(0, 2, 1, 3).reshape(b, h, nb, BLOCK, dh).transpose(2, 0, 1, 3, 4)
    key_pos = jnp.arange(s)

    def one_block(args):
        i, q_blk = args
        z = jnp.einsum('bhqd,bhkd->bhqk', q_blk, kt).astype(jnp.float32) * scale
        q_pos = i * BLOCK + jnp.arange(BLOCK)
        causal = key_pos[None, :] < q_pos[:, None]
        log_one_minus = jnp.where(causal, jax.nn.log_sigmoid(-z), 0.0)
        suffix = lax.cumsum(log_one_minus, axis=3, reverse=True) - log_one_minus
        log_a = jax.nn.log_sigmoid(z) + suffix
        a = jnp.where(causal, jnp.exp(log_a), 0.0)
        return jnp.einsum('bhqk,bhkd->bhqd', a.astype(vt.dtype), vt)

    o_blocks = lax.map(one_block, (jnp.arange(nb), q_blocks))
    return o_blocks.transpose(1, 0, 3, 2, 4).reshape(b, s, h * dh)


def setup_inputs(seed: int = 0) -> dict:
    key = jax.random.key(seed)
    ks = jax.random.split(key, 12)
    f32 = jnp.float32
    x = jax.random.normal(ks[0], (BATCH, SEQ, D_MODEL), f32)
    norm_mix_g = 1.0 + 0.02 * jax.random.normal(ks[1], (DEPTH, D_MODEL), f32)
    w_in = jax.random.normal(ks[2], (DEPTH, D_MODEL, IN_COLS), f32) * D_MODEL ** -0.5
    b_gate = 0.02 * jax.random.normal(ks[3], (DEPTH, 2 * D_MODEL), f32)
    w_up_dil = jax.random.normal(ks[4], (DEPTH, DIL_OUT_WIDTH, D_MODEL), f32) * DIL_OUT_WIDTH ** -0.5
    w_up_sb = jax.random.normal(ks[5], (DEPTH, SB_WIDTH, D_MODEL), f32) * SB_WIDTH ** -0.5
    w_out = jax.random.normal(ks[6], (DEPTH, D_MODEL, D_MODEL), f32) * D_MODEL ** -0.5
    norm_mlp_g = 1.0 + 0.02 * jax.random.normal(ks[7], (DEPTH, D_MODEL), f32)
    w_mlp_in = jax.random.normal(ks[8], (DEPTH, D_MODEL, D_FF), f32) * D_MODEL ** -0.5
    w_mlp_out = jax.random.normal(ks[9], (DEPTH, D_FF, D_MODEL), f32) * D_FF ** -0.5
    norm_final_g = 1.0 + 0.02 * jax.random.normal(ks[10], (D_MODEL,), f32)
    return {"x": x, "norm_mix_g": norm_mix_g, "w_in": w_in, "b_gate": b_gate,
            "w_up_dil": w_up_dil, "w_up_sb": w_up_sb, "w_out": w_out,
            "norm_mlp_g": norm_mlp_g, "w_mlp_in": w_mlp_in, "w_mlp_out": w_mlp_out,
            "norm_final_g": norm_final_g}


def reference(x, norm_mix_g, w_in, b_gate, w_up_dil, w_up_sb, w_out,
              norm_mlp_g, w_mlp_in, w_mlp_out, norm_final_g):
    b, s, _ = x.shape
    for layer in range(DEPTH):
        h = rmsnorm(x, norm_mix_g[layer])
        proj = h @ w_in[layer]
        q_a, k_a, v_a, q_b, k_b, v_b, gl_a, gl_b = jnp.split(proj, SPLITS, axis=-1)
        heads_a = lambda t: t.reshape(b, s, N_DIL_HEADS, HEAD_DIM)
        heads_b = lambda t: t.reshape(b, s, N_SB_HEADS, HEAD_DIM)
        o_a = dilated_attention(heads_a(q_a), heads_a(k_a), heads_a(v_a))
        o_b = stick_breaking_attention(heads_b(q_b), heads_b(k_b), heads_b(v_b))
        bg_a, bg_b = jnp.split(b_gate[layer], 2)
        g_a = jax.nn.sigmoid(gl_a + bg_a)
        g_b = jax.nn.sigmoid(gl_b + bg_b)
        merged = g_a * (o_a @ w_up_dil[layer]) + g_b * (o_b @ w_up_sb[layer])
        x = x + merged @ w_out[layer]
        h2 = rmsnorm(x, norm_mlp_g[layer])
        x = x + jnp.square(jax.nn.relu(h2 @ w_mlp_in[layer])) @ w_mlp_out[layer]
    return rmsnorm(x, norm_final_g)
```

```python
import numpy as np
import ml_dtypes
import concourse.bass as bass
import concourse.mybir as mybir
from concourse.bass_utils import run_bass_kernel_spmd

F32 = mybir.dt.float32
BF16 = mybir.dt.bfloat16
AF = mybir.ActivationFunctionType
ALU = mybir.AluOpType

S = 8192
D = 1024
NCORES = 8
TOK_OWN = 2048
EPS = 1e-6
SEM_CAP = 2000
XMOD = 64

OWN_TILES = ["QA", "KA", "VA", "QB", "KB", "VS", "QC", "KC", "VG"]
OWN_W = {"QA": 128, "KA": 128, "VA": 128, "QB": 128, "KB": 128, "VS": 128, "QC": 64, "KC": 64, "VG": 64}
OWN_OFF = {}
_o = 0
for _n in OWN_TILES:
    OWN_OFF[_n] = _o
    _o += OWN_W[_n]
OWN_COLS = _o


class Op:
    __slots__ = ("eng", "fn", "dma", "deps", "signal", "num", "idx", "inc")

    def __init__(self, eng, fn, dma, inc=16):
        self.eng, self.fn, self.dma = eng, fn, dma
        self.inc = inc
        self.deps = []
        self.signal = False
        self.num = None
        self.idx = None


class Sched:
    def __init__(self):
        self.ops = []
        self.lastw = {}
        self.readers = {}
        self.floor = []
        self.last_by_src = {}
        self.enabled = True
        import os as _os
        self.maxops = int(_os.environ.get("KMAXOPS", "100000000"))

    @staticmethod
    def _src(op):
        return ("dma", op.dma) if op.dma is not None else ("eng", op.eng)

    def add(self, eng, fn, reads=(), writes=(), dma=None, inc=16):
        op = Op(eng, fn, dma, inc)
        if not self.enabled or len(self.ops) >= self.maxops:
            return op
        op.idx = len(self.ops)
        deps = {}
        for d in self.floor:
            deps[id(d)] = d
        for k in reads:
            w = self.lastw.get(k)
            if w is not None:
                deps[id(w)] = w
        for k in writes:
            w = self.lastw.get(k)
            if w is not None:
                deps[id(w)] = w
            for r in self.readers.get(k, {}).values():
                deps[id(r)] = r
        for k in reads:
            self.readers.setdefault(k, {})[self._src(op)] = op
        for k in writes:
            self.lastw[k] = op
            self.readers[k] = {}
        out = []
        for d in deps.values():
            if d is op:
                continue
            if d.dma is None and d.eng == "pe" and eng == "pe" and dma is None:
                continue
            out.append(d)
        op.deps = out
        self.ops.append(op)
        self.last_by_src[self._src(op)] = op
        return op

    def barrier(self):
        self.floor = list(self.last_by_src.values())
        self.lastw = {}
        self.readers = {}

    def prepare(self, nc, semstack):
        for op in self.ops:
            for d in op.deps:
                d.signal = True
        for d in self.last_by_src.values():
            d.signal = True
        counters = {}
        for op in self.ops:
            if op.dma is not None:
                k = ("dma", op.dma)
                counters[k] = counters.get(k, 0) + 1
                op.num = counters[k]
            elif op.signal:
                k = ("eng", op.eng)
                counters[k] = counters.get(k, 0) + 1
                op.num = counters[k]
        sems = {}
        for k, n in counters.items():
            if k[0] == "dma":
                sems[k] = [semstack.enter_context(nc.semaphore("d_%s" % str(k[1])))]
            else:
                ns = (n + SEM_CAP - 1) // SEM_CAP
                sems[k] = [semstack.enter_context(nc.semaphore("e_%s_%d" % (k[1], i))) for i in range(ns)]
        self.sems = sems

    def emit(self, nc, block):
        sems = self.sems

        def semval(op):
            k = Sched._src(op)
            if k[0] == "dma":
                return sems[k][0], op.num * op.inc
            n = op.num - 1
            return sems[k][n // SEM_CAP], n % SEM_CAP + 1

        def run(engname, e):
            waited = {}
            for op in self.ops:
                if op.eng != engname:
                    continue
                need = {}
                for d in op.deps:
                    k = Sched._src(d)
                    if d.num > need.get(k, (0, None))[0]:
                        need[k] = (d.num, d)
                for k, (n, d) in need.items():
                    if waited.get(k, 0) >= n:
                        continue
                    waited[k] = n
                    s, v = semval(d)
                    e.wait_ge(s, v)
                ins = op.fn(e)
                if op.dma is not None:
                    s, _ = semval(op)
                    ins.then_inc(s, op.inc)
                elif op.signal:
                    s, _ = semval(op)
                    ins.then_inc(s, 1)

        final = [op for op in self.last_by_src.values()]

        def runfinal(e):
            for d in final:
                s, v = semval(d) if d.num is not None else (None, None)
                if s is not None:
                    e.wait_ge(s, v)

        @block.tensor
        def _(e):
            run("pe", e)

        @block.scalar
        def _(e):
            run("act", e)

        @block.vector
        def _(e):
            run("dve", e)

        @block.gpsimd
        def _(e):
            run("pool", e)

        @block.sync
        def _(e):
            run("sp", e)
            runfinal(e)


def build_nc(debug=False, sb_limit=None, phases=9, ntA=16, tlist=None, lite=False):
    import contextlib

    nc = bass.Bass("TRN2", target_bir_lowering=False)
    dt = nc.dram_tensor
    x_full = dt("x_full", [S, D], F32, kind="ExternalInput").ap()
    x_own = dt("x_own", [TOK_OWN, D], F32, kind="ExternalInput").ap()
    w_own = dt("w_own", [D, OWN_COLS], F32, kind="ExternalInput").ap()
    if lite:
        _real_dt = dt

        def dt(name, shape, dtype, kind=None):
            if kind == "ExternalInput" and name in ("w_gate", "w_ud", "w_us", "w_o", "w_1", "w_2"):
                shape = [128, 8]
            return _real_dt(name, shape, dtype, kind=kind) if kind else _real_dt(name, shape, dtype)
    w_gate = dt("w_gate", [D, 2 * D], F32, kind="ExternalInput").ap()
    w_ud = dt("w_ud", [256, D], F32, kind="ExternalInput").ap()
    w_us = dt("w_us", [512, D], F32, kind="ExternalInput").ap()
    w_o = dt("w_o", [D, D], F32, kind="ExternalInput").ap()
    w_1 = dt("w_1", [D, 4 * D], F32, kind="ExternalInput").ap()
    w_2 = dt("w_2", [4 * D, D], F32, kind="ExternalInput").ap()
    vecs = dt("vecs", [128, 32], F32, kind="ExternalInput").ap()
    gfin = dt("gfin", [128, D], F32, kind="ExternalInput").ap()
    cmask = dt("cmask", [128, 4 * 512], BF16, kind="ExternalInput").ap()
    cmat = dt("cmat", [128, 4 * 128], BF16, kind="ExternalInput").ap()
    dmask = dt("dmask", [128, 3 * 256], F32, kind="ExternalInput").ap()
    selm = dt("selm", [128, 64], F32, kind="ExternalInput").ap()
    out = dt("y_out", [TOK_OWN, D], F32, kind="ExternalOutput").ap()
    o_locq = [dt("o_loc%d" % q, [192, TOK_OWN], BF16) for q in range(4)]
    o_all = dt("o_all", [4, 4 * 192, TOK_OWN], BF16)
    o_mine = dt("o_mine", [4 * 192, TOK_OWN], BF16)
    if debug:
        dbg = dt("dbg", [4 * 192, S], BF16, kind="ExternalOutput").ap()

    sc = Sched()
    es = contextlib.ExitStack()
    with es:
        def sb(name, shape, dtype, stack=es):
            return stack.enter_context(nc.sbuf_tensor(name, shape, dtype))

        def ps(name, shape, dtype, stack=es):
            return stack.enter_context(nc.psum_tensor(name, shape, dtype))

        vec_sb = sb("vec_sb", [128, 32], F32)
        cmat_sb = sb("cmat_sb", [128, 512], BF16)
        ident = cmat_sb[:, 0:128]
        negtri = cmat_sb[:, 128:256]
        negones = cmat_sb[:, 256:384]
        neglow = cmat_sb[:, 384:512]
        sc.add("sp", lambda e: e.dma_start(out=vec_sb[:], in_=vecs[:, :]), writes=["vec"], dma="c0a")
        sc.add("sp", lambda e: e.dma_start(out=cmat_sb[:], in_=cmat[:, :]), writes=["cmat"], dma="c0b")
        g1 = vec_sb[:, 0:8]
        g2 = vec_sb[:, 8:16]
        bg = vec_sb[:, 16:32]

        def rmsnorm_T(xsrc_tiles, ntt, hT_dst, gain, keyp, scratch, psT, tag):
            ss, lnv, rstd, junk, xn = scratch
            for tt, (xa, rk) in enumerate(xsrc_tiles):
                sc.add("act", lambda e, xa=xa, tt=tt: e.activation(out=junk[:], in_=xa, func=AF.Square,
                                                                   accum_out=ss[:, tt:tt + 1]),
                       reads=[rk], writes=[tag + "junk", tag + "ss%d" % tt])
            sskeys = [tag + "ss%d" % tt for tt in range(ntt)]
            sc.add("act", lambda e: e.activation(out=lnv[:, 0:ntt], in_=ss[:, 0:ntt], func=AF.Ln,
                                                 scale=1.0 / D, bias=eps_sb[:, 0:1]),
                   reads=sskeys + ["eps"], writes=[tag + "lnv"])
            sc.add("act", lambda e: e.activation(out=rstd[:, 0:ntt], in_=lnv[:, 0:ntt], func=AF.Exp, scale=-0.5),
                   reads=[tag + "lnv"], writes=[tag + "rstd"])
            for tt, (xa, rk) in enumerate(xsrc_tiles):
                if tt % 2 == 0:
                    sc.add("dve", lambda e, xa=xa, tt=tt: e.tensor_scalar(out=xn[:, tt, :], in0=xa,
                                                                         scalar1=rstd[:, tt:tt + 1], scalar2=None,
                                                                         op0=ALU.mult),
                           reads=[rk, tag + "rstd"], writes=[tag + "xn%d" % tt])
                else:
                    sc.add("act", lambda e, xa=xa, tt=tt: e.activation(out=xn[:, tt, :], in_=xa, func=AF.Copy,
                                                                       scale=rstd[:, tt:tt + 1]),
                           reads=[rk, tag + "rstd"], writes=[tag + "xn%d" % tt])
            for c in range(8):
                pb = psT[c % 2]
                pk = tag + "psT%d" % (c % 2)
                for tt in range(ntt):
                    sc.add("pe", lambda e, pb=pb, tt=tt, c=c: e.transpose(out=pb[:, tt * 128:(tt + 1) * 128],
                                                                         in_=xn[:, tt, c * 128:(c + 1) * 128],
                                                                         identity=ident),
                           reads=[tag + "xn%d" % tt, "cmat"], writes=[pk])
                dst, dk = hT_dst(c)
                if c % 2 == 0:
                    sc.add("act", lambda e, pb=pb, dst=dst, c=c: e.activation(out=dst, in_=pb[:, 0:ntt * 128],
                                                                              func=AF.Copy, scale=gain[:, c:c + 1]),
                           reads=[pk, "vec"], writes=[dk])
                else:
                    sc.add("dve", lambda e, pb=pb, dst=dst, c=c: e.tensor_scalar(out=dst, in0=pb[:, 0:ntt * 128],
                                                                                 scalar1=gain[:, c:c + 1],
                                                                                 scalar2=None, op0=ALU.mult),
                           reads=[pk, "vec"], writes=[dk])

        eps_sb = sb("eps_sb", [128, 1], F32)
        sc.add("dve", lambda e: e.memset(eps_sb[:], EPS), writes=["eps"])

        def mk_gather(q):
            def f(e):
                return e.collective_compute("AllGather", ALU.bypass, replica_groups=[[0, 1, 2, 3], [4, 5, 6, 7]],
                                            ins=[o_locq[q].ap().opt()], outs=[o_all.ap()[q].opt()])
            return f

        def add_gather(q):
            olk = ["o_loc_a%d" % ch for ch in range(4 * q, 4 * q + 4)] + \
                  ["o_loc_b%d_%d" % (orow, qt) for orow in (64, 128) for qt in range(4 * q, 4 * q + 4)]
            sc.add("pool", mk_gather(q), reads=olk, writes=["o_all%d" % q], dma="cc", inc=1)

        AB = contextlib.ExitStack()
        with AB:
            QB = sb("QB", [128, S], BF16, AB)
            KB = sb("KB", [128, S], BF16, AB)
            QC = sb("QC", [64, S], BF16, AB)
            KC = sb("KC", [64, S], BF16, AB)
            Vs = sb("Vs", [128, 64, 128], BF16, AB)
            DIL = contextlib.ExitStack()
            with DIL:
                QA = sb("QA", [128, S], BF16, DIL)
                KA = sb("KA", [128, S], BF16, DIL)
                Vd = [sb("Vd%d" % g, [128, 64, 66], BF16, DIL) for g in range(3)]
                for g in range(3):
                    sc.add("pool", lambda e, g=g: e.memset(Vd[g][:, :, 64:65], 1.0), writes=["Vd%d_ones" % g])

                PA = contextlib.ExitStack()
                with PA:
                    Wown = sb("Wown", [128, 8, OWN_COLS], BF16, PA)
                    xs = [sb("xs%d" % i, [128, 2, D], F32, PA) for i in range(2)]
                    xn = sb("xnA", [128, 2, D], BF16, PA)
                    hT = [sb("hTA%d" % i, [128, 8, 512], BF16, PA) for i in range(2)]
                    junk = sb("junkA", [128, D], BF16, PA)
                    ss = sb("ssA", [128, 4], F32, PA)
                    lnv = sb("lnvA", [128, 4], F32, PA)
                    rstd = sb("rstdA", [128, 4], F32, PA)
                    vtA = sb("vtA", [128, 512], BF16, PA)
                    vtS = sb("vtS", [128, 512], BF16, PA)
                    vt3 = sb("vt3", [128, 2048], BF16, PA)
                    sc.add("dve", lambda e: e.memset(vt3[64:128, :], 0.0), writes=["vt3"])
                    psT = [ps("psTA%d" % i, [128, 1024], BF16, PA) for i in range(2)]
                    psP = [ps("psPA%d" % i, [128, 512], F32, PA) for i in range(3)]
                    psV = [ps("psVA%d" % i, [128, 1024], BF16, PA) for i in range(2)]

                    wv = w_own.rearrange("(c p) n -> p c n", p=128)
                    for c0 in range(0, 8, 2):
                        sc.add("pool", lambda e, c0=c0: e.dma_start(out=Wown[:, c0:c0 + 2, :], in_=wv[:, c0:c0 + 2, :]),
                               writes=["Wown%d" % c0], dma="wown")
                    wkeys = ["Wown%d" % c0 for c0 in range(0, 8, 2)]
                    xv = x_full.rearrange("(n tt p) d -> n p tt d", tt=2, p=128)

                    pcount = [0]

                    def proj_tile(t, name, hTt, hk):
                        w = OWN_W[name]
                        off = OWN_OFF[name]
                        i = pcount[0] % 3
                        pcount[0] += 1
                        pp = psP[i]
                        pk = "psPA%d" % i
                        for c in range(8):
                            sc.add("pe", lambda e, c=c, pp=pp: e.matmul(pp[0:w, :], lhsT=Wown[:, c, off:off + w],
                                                                       rhs=hTt[:, c, :], start=(c == 0), stop=(c == 7)),
                                   reads=wkeys + [hk], writes=[pk])
                        return pp, pk

                    def deint(ap_rows, d):
                        return ap_rows.rearrange("p (l r) -> p r l", r=d)

                    for ti, t in enumerate(tlist if tlist is not None else range(ntA)):
                        hTt = hT[ti % 2]
                        hk = "hT%d" % (ti % 2)
                        for half in range(2):
                            n = 2 * t + half
                            xsl = xs[n % 2]
                            xk = "xs%d" % (n % 2)
                            sc.add("sp", lambda e, xsl=xsl, n=n: e.dma_start(out=xsl[:], in_=xv[n]),
                                   writes=[xk], dma=xk)
                            rmsnorm_T([(xsl[:, 0, :], xk), (xsl[:, 1, :], xk)], 2,
                                      lambda c, half=half, hTt=hTt, hk=hk: (hTt[:, c, half * 256:(half + 1) * 256], hk),
                                      g1, None, (ss, lnv, rstd, junk, xn), psT, "A")
                        tsl = slice(t * 512, (t + 1) * 512)
                        pp, pk = proj_tile(t, "QA", hTt, hk)
                        sc.add("act", lambda e, pp=pp, tsl=tsl: e.activation(out=QA[0:64, tsl], in_=pp[0:64, :],
                                                                             func=AF.Copy, scale=0.125),
                               reads=[pk], writes=["QA"])
                        sc.add("dve", lambda e, pp=pp, t=t: e.tensor_scalar(
                            out=QA[64:128, :].rearrange("p (r n l) -> p r n l", r=4, n=16)[:, :, t, :],
                            in0=deint(pp[64:128, :], 4), scalar1=0.125, scalar2=None, op0=ALU.mult),
                            reads=[pk], writes=["QA"])
                        pp, pk = proj_tile(t, "KA", hTt, hk)
                        sc.add("act", lambda e, pp=pp, tsl=tsl: e.activation(out=KA[0:64, tsl], in_=pp[0:64, :],
                                                                             func=AF.Copy), reads=[pk], writes=["KA"])
                        sc.add("dve", lambda e, pp=pp, t=t: e.tensor_copy(
                            out=KA[64:128, :].rearrange("p (r n l) -> p r n l", r=4, n=16)[:, :, t, :],
                            in_=deint(pp[64:128, :], 4)), reads=[pk], writes=["KA"])
                        pp, pk = proj_tile(t, "VA", hTt, hk)
                        sc.add("act", lambda e, pp=pp: e.activation(out=vtA[0:64, :], in_=pp[0:64, :], func=AF.Copy),
                               reads=[pk], writes=["vtA"])
                        sc.add("dve", lambda e, pp=pp: e.tensor_copy(
                            out=vtA[64:128, :].rearrange("p (r l) -> p r l", r=4), in_=deint(pp[64:128, :], 4)),
                            reads=[pk], writes=["vtA"])
                        pv = psV[0]
                        for bi in range(4):
                            sc.add("pe", lambda e, bi=bi, pv=pv: e.transpose(out=pv[:, bi * 128:(bi + 1) * 128],
                                                                            in_=vtA[:, bi * 128:(bi + 1) * 128],
                                                                            identity=ident),
                                   reads=["vtA", "cmat"], writes=["psVA0"])
                        for bi in range(4):
                            sc.add("act", lambda e, pv=pv, t=t, bi=bi: e.activation(
                                out=Vd[0][:, 4 * t + bi, 0:64], in_=pv[:, bi * 128:bi * 128 + 64], func=AF.Copy),
                                reads=["psVA0"], writes=["Vd0"])
                            sc.add("act", lambda e, pv=pv, t=t, bi=bi: e.activation(
                                out=Vd[1][:, bi * 16 + t, 0:64], in_=pv[:, bi * 128 + 64:bi * 128 + 128], func=AF.Copy),
                                reads=["psVA0"], writes=["Vd1"])
                        nn, qq = t // 4, t % 4
                        pp, pk = proj_tile(t, "QB", hTt, hk)
                        sc.add("dve", lambda e, pp=pp, nn=nn, qq=qq: e.tensor_scalar(
                            out=QB[0:64, :].rearrange("p (r n i) -> p r n i", r=16, n=4)[:, :, nn, 32 * qq:32 * qq + 32],
                            in0=deint(pp[0:64, :], 16), scalar1=0.125, scalar2=None, op0=ALU.mult),
                            reads=[pk], writes=["QB"])
                        sc.add("act", lambda e, pp=pp, tsl=tsl: e.activation(out=QB[64:128, tsl], in_=pp[64:128, :],
                                                                             func=AF.Copy, scale=0.125),
                               reads=[pk], writes=["QB"])
                        pp, pk = proj_tile(t, "KB", hTt, hk)
                        sc.add("dve", lambda e, pp=pp, nn=nn, qq=qq: e.tensor_copy(
                            out=KB[0:64, :].rearrange("p (r n i) -> p r n i", r=16, n=4)[:, :, nn, 32 * qq:32 * qq + 32],
                            in_=deint(pp[0:64, :], 16)), reads=[pk], writes=["KB"])
                        sc.add("act", lambda e, pp=pp, tsl=tsl: e.activation(out=KB[64:128, tsl], in_=pp[64:128, :],
                                                                             func=AF.Copy), reads=[pk], writes=["KB"])
                        pp, pk = proj_tile(t, "VS", hTt, hk)
                        sc.add("act", lambda e, pp=pp: e.activation(out=vtS[:, :], in_=pp[:, :], func=AF.Copy),
                               reads=[pk], writes=["vtS"])
                        pv = psV[1]
                        for bi in range(4):
                            sc.add("pe", lambda e, bi=bi, pv=pv: e.transpose(out=pv[:, bi * 128:(bi + 1) * 128],
                                                                            in_=vtS[:, bi * 128:(bi + 1) * 128],
                                                                            identity=ident),
                                   reads=["vtS", "cmat"], writes=["psVA1"])
                        sc.add("dve", lambda e, pv=pv, t=t: e.tensor_copy(
                            out=Vs[:, 4 * t:4 * t + 4, :].rearrange("p b d -> p (b d)"), in_=pv[:, 0:512]),
                            reads=["psVA1"], writes=["Vs"])
                        pp, pk = proj_tile(t, "QC", hTt, hk)
                        sc.add("act", lambda e, pp=pp, tsl=tsl: e.activation(out=QC[0:64, tsl], in_=pp[0:64, :],
                                                                             func=AF.Copy, scale=0.125),
                               reads=[pk], writes=["QC"])
                        pp, pk = proj_tile(t, "KC", hTt, hk)
                        sc.add("dve", lambda e, pp=pp, tsl=tsl: e.tensor_copy(out=KC[0:64, tsl], in_=pp[0:64, :]),
                               reads=[pk], writes=["KC"])
                        pp, pk = proj_tile(t, "VG", hTt, hk)
                        sc.add("act", lambda e, pp=pp, qq=qq: e.activation(
                            out=vt3[0:64, :].rearrange("p (r i) -> p r i", r=16)[:, :, 32 * qq:32 * qq + 32],
                            in_=deint(pp[0:64, :], 16), func=AF.Copy), reads=[pk], writes=["vt3"])
                        if qq == 3:
                            for r in range(16):
                                pv = psV[r // 8]
                                pvk = "psVA%d" % (r // 8)
                                rr = r % 8
                                sc.add("pe", lambda e, r=r, rr=rr, pv=pv: e.transpose(out=pv[:, rr * 128:(rr + 1) * 128],
                                                                                      in_=vt3[:, r * 128:(r + 1) * 128],
                                                                                      identity=ident),
                                       reads=["vt3", "cmat"], writes=[pvk])
                            for r in range(16):
                                pv = psV[r // 8]
                                pvk = "psVA%d" % (r // 8)
                                rr = r % 8
                                if False:
                                    sc.add("act", lambda e, pv=pv, nn=nn, r=r, rr=rr: e.activation(
                                        out=Vd[2][:, r * 4 + nn, 0:64], in_=pv[:, rr * 128:rr * 128 + 64], func=AF.Copy),
                                        reads=[pvk], writes=["Vd2"])
                                else:
                                    sc.add("dve", lambda e, pv=pv, nn=nn, r=r, rr=rr: e.tensor_copy(
                                        out=Vd[2][:, r * 4 + nn, 0:64], in_=pv[:, rr * 128:rr * 128 + 64]),
                                        reads=[pvk], writes=["Vd2"])
                sc.barrier()
                if phases < 2:
                    sc.enabled = False

                PD = contextlib.ExitStack()
                with PD:
                    acc = sb("acc", [65, S], F32, PD)
                    dm = sb("dm", [128, 768], F32, PD)
                    sel_sb = sb("sel_sb", [128, 64], F32, PD)
                    sc.add("sp", lambda e: e.dma_start(out=dm[:], in_=dmask[:, :]), writes=["dm"], dma="c1a")
                    sc.add("sp", lambda e: e.dma_start(out=sel_sb[:], in_=selm[:, :]), writes=["sel"], dma="c1b")
                    pex = [sb("pex%d" % i, [128, 512], F32, PD) for i in range(2)]
                    pbf = [sb("pbf%d" % i, [128, 512], BF16, PD) for i in range(2)]
                    rec = sb("rec", [64, 512], F32, PD)
                    oa = [sb("oa%d" % i, [64, 512], BF16, PD) for i in range(2)]
                    psS = [ps("psS%d" % i, [128, 512], F32, PD) for i in range(2)]
                    psO = [ps("psOd%d" % i, [128, 512], F32, PD) for i in range(2)]
                    psD = ps("psDd", [128, 512], F32, PD)
                    groups = [
                        (0, 1, QA, KA, 0),
                        (1, 4, QA, KA, 64),
                        (2, 16, QB, KB, 0),
                    ]
                    qkey = {0: "QA", 1: "QA", 2: "QB"}
                    kkey = {0: "KA", 1: "KA", 2: "KB"}
                    pairno = 0
                    import os as _os3
                    _dg = _os3.environ.get("DILG")
                    for (g, d, Qt, Kt, ro) in groups:
                        if _dg is not None and str(g) not in _dg:
                            continue
                        nb = 64 // d
                        rows = slice(ro, ro + 64)
                        if d == 1:
                            banks = [[(0, n0 + i) for i in range(4)] for n0 in range(0, 64, 4)]
                        else:
                            banks = [[(r0 + i, n) for i in range(4)] for n in range(nb) for r0 in range(0, d, 4)]
                        for bank in banks:
                            po = psO[pairno % 2]
                            pok = "psOd%d" % (pairno % 2)
                            pairno += 1
                            for half in range(2):
                                blks = bank[2 * half:2 * half + 2]
                                i2 = (pairno * 2 + half) % 2
                                pS = psS[i2]
                                psk = "psS%d" % i2
                                for bi, (r, n) in enumerate(blks):
                                    blk = r * nb + n
                                    qsl = slice(blk * 128, (blk + 1) * 128)
                                    pblk = blk - 1 if n > 0 else blk
                                    ksl_p = slice(pblk * 128, (pblk + 1) * 128)
                                    sc.add("pe", lambda e, pS=pS, bi=bi, ksl_p=ksl_p, qsl=qsl, Kt=Kt, Qt=Qt, rows=rows: e.matmul(
                                        pS[:, bi * 256:bi * 256 + 128], lhsT=Kt[rows, ksl_p], rhs=Qt[rows, qsl],
                                        start=True, stop=True), reads=[qkey[g], kkey[g]], writes=[psk])
                                    sc.add("pe", lambda e, pS=pS, bi=bi, qsl=qsl, Kt=Kt, Qt=Qt, rows=rows: e.matmul(
                                        pS[:, bi * 256 + 128:bi * 256 + 256], lhsT=Kt[rows, qsl], rhs=Qt[rows, qsl],
                                        start=True, stop=True), reads=[qkey[g], kkey[g]], writes=[psk])
                                pe_ = pex[i2]
                                pb_ = pbf[i2]
                                sc.add("act", lambda e, pe_=pe_, pS=pS: e.activation(out=pe_[:], in_=pS[:, :], func=AF.Exp),
                                       reads=[psk], writes=["pex%d" % i2])
                                for bi in range(2):
                                    sc.add("dve", lambda e, pe_=pe_, pb_=pb_, bi=bi, g=g: e.tensor_tensor(
                                        out=pb_[:, bi * 256:(bi + 1) * 256], in0=pe_[:, bi * 256:(bi + 1) * 256],
                                        in1=dm[:, g * 256:(g + 1) * 256], op=ALU.mult),
                                        reads=["pex%d" % i2, "dm"], writes=["pbf%d" % i2])
                                for bi, (r, n) in enumerate(blks):
                                    blk = r * nb + n
                                    slot = 2 * half + bi
                                    osl = po[0:65, slot * 128:(slot + 1) * 128]
                                    if n > 0:
                                        sc.add("pe", lambda e, osl=osl, pb_=pb_, bi=bi, blk=blk, g=g: e.matmul(
                                            osl, lhsT=Vd[g][:, blk - 1, 0:65], rhs=pb_[:, bi * 256:bi * 256 + 128],
                                            start=True, stop=False),
                                            reads=["pbf%d" % i2, "Vd%d" % g, "Vd%d_ones" % g], writes=[pok])
                                    sc.add("pe", lambda e, osl=osl, pb_=pb_, bi=bi, blk=blk, g=g, n=n: e.matmul(
                                        osl, lhsT=Vd[g][:, blk, 0:65], rhs=pb_[:, bi * 256 + 128:bi * 256 + 256],
                                        start=(n == 0), stop=True),
                                        reads=["pbf%d" % i2, "Vd%d" % g, "Vd%d_ones" % g], writes=[pok])
                            r0, n0 = bank[0]
                            if d == 1:
                                dst = acc[0:65, n0 * 128:(n0 + 4) * 128]
                                sc.add("act", lambda e, dst=dst, po=po: e.activation(out=dst, in_=po[0:65, :], func=AF.Copy),
                                       reads=[pok], writes=["acc"])
                            else:
                                base = 128 * n0 * d + r0
                                span = acc[0:65, 128 * n0 * d:128 * (n0 + 1) * d].rearrange("p (i r) -> p r i", r=d)
                                dst = span[:, r0:r0 + 4, :]
                                src = po[0:65, :].rearrange("p (r i) -> p r i", r=4)
                                sc.add("dve", lambda e, dst=dst, src=src: e.tensor_tensor(out=dst, in0=dst, in1=src, op=ALU.add),
                                       reads=[pok, "acc"], writes=["acc"])
                    for ch in range(16):
                        csl = slice(ch * 512, (ch + 1) * 512)
                        sc.add("pe", lambda e, csl=csl: e.matmul(psD[0:64, :], lhsT=sel_sb[0:65, :], rhs=acc[0:65, csl],
                                                                 start=True, stop=True),
                               reads=["acc", "sel"], writes=["psDd"])
                        sc.add("dve", lambda e: e.reciprocal(out=rec[:], in_=psD[0:64, :]), reads=["psDd"], writes=["rec"])
                        oo = oa[ch % 2]
                        sc.add("dve", lambda e, oo=oo, csl=csl: e.tensor_tensor(out=oo[:], in0=acc[0:64, csl], in1=rec[:],
                                                                                op=ALU.mult),
                               reads=["acc", "rec"], writes=["oa%d" % (ch % 2)])
                        sc.add("sp", lambda e, oo=oo, ch=ch: e.dma_start(
                            out=o_locq[ch // 4].ap()[0:64, (ch % 4) * 512:(ch % 4 + 1) * 512], in_=oo[:]),
                               reads=["oa%d" % (ch % 2)], writes=["o_loc_a%d" % ch], dma="oa%d" % (ch % 2))
                sc.barrier()
                if phases < 3:
                    sc.enabled = False

            PS_ = contextlib.ExitStack()
            with PS_:
                NB = 4
                cm = sb("cm", [128, 2048], BF16, PS_)
                sc.add("sp", lambda e: e.dma_start(out=cm[:], in_=cmask[:, :]), writes=["cm"], dma="c2")
                Eb = [sb("Eb%d" % i, [128, 512], F32, PS_) for i in range(NB)]
                Lb = [sb("Lb%d" % i, [128, 512], BF16, PS_) for i in range(NB)]
                Ab = [sb("Ab%d" % i, [128, 512], BF16, PS_) for i in range(NB)]
                LsF = sb("LsF", [128, 512], F32, PS_)
                LsB = [sb("LsB%d" % i, [128, 512], BF16, PS_) for i in range(4)]
                ost = [sb("ost%d" % i, [64, 512], BF16, PS_) for i in range(2)]
                psZ = [ps("psZ%d" % i, [128, 512], F32, PS_) for i in range(2)]
                psB = [ps("psB%d" % i, [128, 512], F32, PS_) for i in range(2)]
                psOs = [ps("psOs%d" % i, [128, 512], F32, PS_) for i in range(2)]
                heads = [
                    (QB, KB, slice(64, 128), "QB", "KB", slice(0, 64), 64),
                    (QC, KC, slice(0, 64), "QC", "KC", slice(64, 128), 128),
                ]
                pairs = []
                qn = 0
                for qt in range(16):
                    kmax = 4 * qt + 3
                    kmin = 0 if sb_limit is None else max(0, kmax - sb_limit + 1)
                    for kb in range(kmax, kmin - 1, -1):
                        for hi, hd in enumerate(heads):
                            pairs.append(dict(h=hd, hi=hi, qt=qt, kb=kb, first=(kb == kmax), last=(kb == kmin),
                                              u=(kb - 4 * qt) if kb >= 4 * qt else None, qn=hi))
                Xb = [sb("Xb%d" % i, [128, 512], F32, PS_) for i in range(2)]

                def stage1(i, p):
                    Qt, Kt, rows, qk, kk, vcols, orow = p["h"]
                    qsl = slice(p["qt"] * 512, (p["qt"] + 1) * 512)
                    ksl = slice(p["kb"] * 128, (p["kb"] + 1) * 128)
                    z = psZ[i % 2]
                    zk = "psZ%d" % (i % 2)
                    E, L = Eb[i % NB], Lb[i % NB]
                    ek, lk = "Eb%d" % (i % NB), "Lb%d" % (i % NB)
                    sc.add("pe", lambda e: e.matmul(z[:, :], lhsT=Kt[rows, ksl], rhs=Qt[rows, qsl], start=True, stop=True),
                           reads=[qk, kk], writes=[zk])
                    sc.add("act", lambda e: e.activation(out=E[:], in_=z[:, :], func=AF.Exp), reads=[zk], writes=[ek])
                    if p["u"] is not None:
                        u = p["u"]
                        sc.add("dve", lambda e: e.tensor_tensor(out=E[:], in0=E[:], in1=cm[:, u * 512:(u + 1) * 512],
                                                                op=ALU.mult), reads=[ek, "cm"], writes=[ek])
                    sc.add("act", lambda e: e.activation(out=L[:], in_=E[:], func=AF.Ln, bias=1.0), reads=[ek], writes=[lk])

                def stage2(i, p):
                    Qt, Kt, rows, qk, kk, vcols, orow = p["h"]
                    qsl = slice(p["qt"] * 512, (p["qt"] + 1) * 512)
                    kb = p["kb"]
                    bq = psB[i % 2]
                    bk = "psB%d" % (i % 2)
                    E, L, A = Eb[i % NB], Lb[i % NB], Ab[i % NB]
                    ek, lk, ak = "Eb%d" % (i % NB), "Lb%d" % (i % NB), "Ab%d" % (i % NB)
                    X = Xb[i % 2]
                    xk_ = "Xb%d" % (i % 2)
                    first, last = p["first"], p["last"]
                    bq = psB[p["hi"]]
                    bk = "psB%d" % p["hi"]
                    sc.add("pe", lambda e: e.matmul(bq[:, :], lhsT=negtri, rhs=L[:], start=first, stop=False,
                                                    skip_group_check=True),
                           reads=[lk, "cmat"], writes=[bk])
                    sc.add("act", lambda e: e.activation(out=X[:], in_=bq[:, :], func=AF.Exp), reads=[bk], writes=[xk_])
                    sc.add("dve", lambda e: e.tensor_tensor(out=A[:], in0=E[:], in1=X[:], op=ALU.mult),
                           reads=[ek, xk_], writes=[ak])

                def stage2b(i, p):
                    L = Lb[i % NB]
                    lk = "Lb%d" % (i % NB)
                    bq = psB[p["hi"]]
                    bk = "psB%d" % p["hi"]
                    if not p["last"]:
                        sc.add("pe", lambda e: e.matmul(bq[:, :], lhsT=neglow, rhs=L[:], start=False, stop=False,
                                                        skip_group_check=True),
                               reads=[lk, "cmat"], writes=[bk])

                def stage3(i, p):
                    Qt, Kt, rows, qk, kk, vcols, orow = p["h"]
                    qsl = slice(p["qt"] * 512, (p["qt"] + 1) * 512)
                    kb = p["kb"]
                    A = Ab[i % NB]
                    ak = "Ab%d" % (i % NB)
                    first, last = p["first"], p["last"]
                    po = psOs[p["qn"] % 2]
                    pok = "psOs%d" % (p["qn"] % 2)
                    sc.add("pe", lambda e: e.matmul(po[0:64, :], lhsT=Vs[:, kb, vcols], rhs=A[:], start=first, stop=last,
                                                    skip_group_check=True),
                           reads=[ak, "Vs"], writes=[pok])
                    if last:
                        oo = ost[p["qn"] % 2]
                        ook = "ost%d" % (p["qn"] % 2)
                        sc.add("act", lambda e: e.activation(out=oo[:], in_=po[0:64, :], func=AF.Copy), reads=[pok], writes=[ook])
                        qt_ = p["qt"]
                        sc.add("sp", lambda e: e.dma_start(
                            out=o_locq[qt_ // 4].ap()[orow:orow + 64, (qt_ % 4) * 512:(qt_ % 4 + 1) * 512], in_=oo[:]),
                               reads=[ook], writes=["o_loc_b%d_%d" % (orow, qt_)], dma=ook)

                cnt = 0
                for p in pairs:
                    p["cnt"] = cnt
                    cnt += 1
                DEPTH = 2
                for i in range(len(pairs) + 2 * DEPTH):
                    if i < len(pairs):
                        stage1(i, pairs[i])
                    if 0 <= i - DEPTH < len(pairs):
                        stage2(i - DEPTH, pairs[i - DEPTH])
                    if 0 <= i - 2 * DEPTH < len(pairs):
                        p3 = pairs[i - 2 * DEPTH]
                        stage3(i - 2 * DEPTH, p3)
                    if 0 <= i - DEPTH < len(pairs):
                        stage2b(i - DEPTH, pairs[i - DEPTH])
                    if 0 <= i - 2 * DEPTH < len(pairs):
                        if phases >= 4 and p3["last"] and p3["hi"] == 1 and p3["qt"] % 4 == 3:
                            add_gather(p3["qt"] // 4)
            sc.barrier()
            if phases < 4:
                sc.enabled = False

        oallk = ["o_all%d" % q for q in range(4)]
        if debug:
            for q in range(4):
                sc.add("pool", lambda e, q=q: e.dma_start(out=dbg[:, q * TOK_OWN:(q + 1) * TOK_OWN], in_=o_all.ap()[q]),
                       reads=oallk, writes=["dbg%d" % q], dma="dbg")

        if phases < 5:
            sc.enabled = False
        PC = contextlib.ExitStack()
        with PC:
            xr = sb("xr", [128, 16, D], F32, PC)
            hT2 = sb("hT2", [128, 8, TOK_OWN], BF16, PC)
            junk = sb("junkC", [128, D], BF16, PC)
            ss = sb("ssC", [128, 4], F32, PC)
            lnv = sb("lnvC", [128, 4], F32, PC)
            rstd = sb("rstdC", [128, 4], F32, PC)
            xn = sb("xnC", [128, 4, D], BF16, PC)
            xo = x_own.rearrange("(tt p) d -> p tt d", p=128)
            for q4 in range(4):
                sc.add("sp", lambda e, q4=q4: e.dma_start(out=xr[:, 4 * q4:4 * q4 + 4, :], in_=xo[:, 4 * q4:4 * q4 + 4, :]),
                       writes=["xr%d" % q4], dma="xr%d" % q4)
            psT = [ps("psTC%d" % i, [128, 1024], BF16, PC) for i in range(2)]
            psM = [ps("psMC%d" % i, [128, 512], F32, PC) for i in range(6)]
            pmc = [0]

            def nextps():
                i = pmc[0] % 6
                pmc[0] += 1
                return psM[i], "psMC%d" % i

            C1 = contextlib.ExitStack()
            with C1:
                Wg = sb("Wg", [128, 8, 2 * D], BF16, C1)
                Wud = sb("Wud", [128, 2, D], BF16, C1)
                Wus = sb("Wus", [128, 4, D], BF16, C1)
                Wo = sb("Wo", [128, 8, D], BF16, C1)
                oA = sb("oA", [128, 2, 512], BF16, C1)
                oB = sb("oB", [128, 4, 512], BF16, C1)
                hT1 = sb("hT1", [128, 8, 512], BF16, C1)
                Ga = sb("Ga", [128, 512], F32, C1)
                Gb = sb("Gb", [128, 512], F32, C1)
                t1 = sb("t1", [128, 512], F32, C1)
                t2 = sb("t2", [128, 512], F32, C1)
                mg = sb("mg", [128, 8, 512], BF16, C1)
                wgv = w_gate.rearrange("(c p) n -> p c n", p=128)
                for c0 in range(0, 8, 2):
                    sc.add("pool", lambda e, c0=c0: e.dma_start(out=Wg[:, c0:c0 + 2, :], in_=wgv[:, c0:c0 + 2, :]),
                           writes=["Wg%d" % c0], dma="wg")
                wgk = ["Wg%d" % c0 for c0 in range(0, 8, 2)]
                sc.add("pool", lambda e: e.dma_start(out=Wud[:], in_=w_ud.rearrange("(c p) n -> p c n", p=128)),
                       writes=["Wud"], dma="wud")
                sc.add("pool", lambda e: e.dma_start(out=Wus[:], in_=w_us.rearrange("(c p) n -> p c n", p=128)),
                       writes=["Wus"], dma="wus")
                wov = w_o.rearrange("(c p) n -> p c n", p=128)
                for c0 in range(0, 8, 4):
                    sc.add("pool", lambda e, c0=c0: e.dma_start(out=Wo[:, c0:c0 + 4, :], in_=wov[:, c0:c0 + 4, :]),
                           writes=["Wo%d" % c0], dma="wo")
                wok = ["Wo0", "Wo4"]
                jq_cache = {}

                def get_jq(e):
                    if "jq" not in jq_cache:
                        pid = e.partition_id()
                        jq_cache["jq"] = bass.ds(pid % 4, 1)
                    return jq_cache["jq"]

                omv = o_mine.ap()

                def mk_om(r):
                    def f(e):
                        jq = get_jq(e)
                        return e.dma_start(out=omv[r * 192:(r + 1) * 192, :].rearrange("(o f) t -> o f t", o=1),
                                           in_=o_all.ap()[jq, r * 192:(r + 1) * 192, :])
                    return f

                for r in range(4):
                    sc.add("pool", mk_om(r), reads=oallk, writes=["o_mine%d" % r], dma="omine")

                def mk_oa(cc, hh, T):
                    def f(e):
                        r = 2 * cc + hh
                        return e.dma_start(out=oA[hh * 64:(hh + 1) * 64, cc, :],
                                           in_=omv[r * 192:r * 192 + 64, T * 512:(T + 1) * 512])
                    return f

                def mk_ob(cc, T):
                    def f(e):
                        return e.dma_start(out=oB[:, cc, :], in_=omv[cc * 192 + 64:cc * 192 + 192, T * 512:(T + 1) * 512])
                    return f

                for T in range(4):
                    tsl = slice(T * 512, (T + 1) * 512)
                    for cc in range(2):
                        for hh in range(2):
                            sc.add("sp", mk_oa(cc, hh, T), reads=["o_mine%d" % r_ for r_ in range(4)], writes=["oA%d" % (2 * cc + hh)], dma="oAg")
                    for cc in range(4):
                        sc.add("sp", mk_ob(cc, T), reads=["o_mine%d" % r_ for r_ in range(4)], writes=["oB%d" % cc], dma="oBg")
                    rmsnorm_T([(xr[:, 4 * T + tt, :], "xr%d" % T) for tt in range(4)], 4,
                              lambda c: (hT1[:, c, :], "hT1"), g1, None, (ss, lnv, rstd, junk, xn), psT, "C")
                    for m in range(8):
                        msl = slice(m * 128, (m + 1) * 128)
                        pga, pgak = nextps()
                        for c in range(8):
                            sc.add("pe", lambda e, c=c, pga=pga, msl=msl: e.matmul(pga[:, :], lhsT=Wg[:, c, msl], rhs=hT1[:, c, :],
                                                                                  start=(c == 0), stop=(c == 7)),
                                   reads=wgk + ["hT1"], writes=[pgak])
                        pgb, pgbk = nextps()
                        msl2 = slice(D + m * 128, D + (m + 1) * 128)
                        for c in range(8):
                            sc.add("pe", lambda e, c=c, pgb=pgb, msl2=msl2: e.matmul(pgb[:, :], lhsT=Wg[:, c, msl2], rhs=hT1[:, c, :],
                                                                                    start=(c == 0), stop=(c == 7)),
                                   reads=wgk + ["hT1"], writes=[pgbk])
                        pua, puak = nextps()
                        for c in range(2):
                            sc.add("pe", lambda e, c=c, pua=pua, msl=msl, tsl=tsl: e.matmul(pua[:, :], lhsT=Wud[:, c, msl],
                                                                                           rhs=oA[:, c, :],
                                                                                           start=(c == 0), stop=(c == 1)),
                                   reads=["Wud", "oA0", "oA1", "oA2", "oA3"], writes=[puak])
                        pub, pubk = nextps()
                        for c in range(4):
                            sc.add("pe", lambda e, c=c, pub=pub, msl=msl, tsl=tsl: e.matmul(pub[:, :], lhsT=Wus[:, c, msl],
                                                                                           rhs=oB[:, c, :],
                                                                                           start=(c == 0), stop=(c == 3)),
                                   reads=["Wus", "oB0", "oB1", "oB2", "oB3"], writes=[pubk])
                        sc.add("act", lambda e, pga=pga, m=m: e.activation(out=Ga[:], in_=pga[:, :], func=AF.Sigmoid,
                                                                           bias=bg[:, m:m + 1]),
                               reads=[pgak, "vec"], writes=["Ga"])
                        sc.add("act", lambda e, pgb=pgb, m=m: e.activation(out=Gb[:], in_=pgb[:, :], func=AF.Sigmoid,
                                                                           bias=bg[:, 8 + m:9 + m]),
                               reads=[pgbk, "vec"], writes=["Gb"])
                        sc.add("dve", lambda e, pua=pua: e.tensor_tensor(out=t1[:], in0=Ga[:], in1=pua[:, :], op=ALU.mult),
                               reads=["Ga", puak], writes=["t1"])
                        sc.add("dve", lambda e, pub=pub: e.tensor_tensor(out=t2[:], in0=Gb[:], in1=pub[:, :], op=ALU.mult),
                               reads=["Gb", pubk], writes=["t2"])
                        sc.add("pool", lambda e, m=m: e.tensor_tensor(out=mg[:, m, :], in0=t1[:], in1=t2[:], op=ALU.add),
                               reads=["t1", "t2"], writes=["mg%d" % m])
                    mgk = ["mg%d" % m for m in range(8)]
                    for tb in range(4):
                        for half in range(2):
                            py, pyk = nextps()
                            hsl = slice(half * 512, (half + 1) * 512)
                            for c in range(8):
                                sc.add("pe", lambda e, c=c, py=py, tb=tb, hsl=hsl: e.matmul(
                                    py[:, :], lhsT=mg[:, c, tb * 128:(tb + 1) * 128], rhs=Wo[:, c, hsl],
                                    start=(c == 0), stop=(c == 7)), reads=mgk + wok, writes=[pyk])
                            xa = xr[:, 4 * T + tb, hsl]
                            sc.add("dve", lambda e, xa=xa, py=py: e.tensor_tensor(out=xa, in0=xa, in1=py[:, :], op=ALU.add),
                                   reads=[pyk, "xr%d" % T], writes=["xr%d" % T])
                    rmsnorm_T([(xr[:, 4 * T + tt, :], "xr%d" % T) for tt in range(4)], 4,
                              lambda c, tsl=tsl: (hT2[:, c, tsl], "hT2_%d" % T), g2, None,
                              (ss, lnv, rstd, junk, xn), psT, "C")
            sc.barrier()
            C2 = contextlib.ExitStack()
            with C2:
                gfb = sb("gfb", [128, D], F32, C2)
                sc.add("sp", lambda e: e.dma_start(out=gfb[:], in_=gfin[:, :]), writes=["gfb"], dma="c3")
                W1q = [sb("W1q%d" % i, [128, 8, D], BF16, C2) for i in range(2)]
                W2q = [sb("W2q%d" % i, [128, 8, D], BF16, C2) for i in range(2)]
                uT = [sb("uT%d" % i, [128, 8, 512], BF16, C2) for i in range(2)]
                rl = [sb("rl%d" % i, [128, 512], F32, C2) for i in range(2)]
                yo = [sb("yo%d" % i, [128, D], F32, C2) for i in range(2)]
                w1v = w_1.rearrange("(c p) n -> p c n", p=128)
                w2v = w_2.rearrange("(c p) n -> p c n", p=128)
                it = 0
                rc = 0
                for qf in range(4):
                    s_ = qf % 2
                    sc.add("pool", lambda e, qf=qf, s_=s_: e.dma_start(out=W1q[s_][:], in_=w1v[:, :, qf * D:(qf + 1) * D]),
                           writes=["W1q%d" % s_], dma="w1q%d" % s_)
                    sc.add("pool", lambda e, qf=qf, s_=s_: e.dma_start(out=W2q[s_][:], in_=w2v[:, 8 * qf:8 * qf + 8, :]),
                           writes=["W2q%d" % s_], dma="w2q%d" % s_)
                    for T in range(4):
                        tsl = slice(T * 512, (T + 1) * 512)
                        u = uT[it % 2]
                        uk = "uT%d" % (it % 2)
                        it += 1
                        for m in range(8):
                            pu, puk = nextps()
                            for c in range(8):
                                sc.add("pe", lambda e, c=c, pu=pu, m=m, s_=s_, tsl=tsl: e.matmul(
                                    pu[:, :], lhsT=W1q[s_][:, c, m * 128:(m + 1) * 128], rhs=hT2[:, c, tsl],
                                    start=(c == 0), stop=(c == 7)), reads=["W1q%d" % s_, "hT2_%d" % T], writes=[puk])
                            r_ = rl[rc % 2]
                            rk = "rl%d" % (rc % 2)
                            rc += 1
                            sc.add("act", lambda e, r_=r_, pu=pu: e.activation(out=r_[:], in_=pu[:, :], func=AF.Relu),
                                   reads=[puk], writes=[rk])
                            sc.add("pool", lambda e, r_=r_, u=u, m=m: e.tensor_tensor(out=u[:, m, :], in0=r_[:], in1=r_[:],
                                                                                    op=ALU.mult),
                                   reads=[rk], writes=[uk + "_%d" % m])
                        uks = [uk + "_%d" % m for m in range(8)]
                        for tb in range(4):
                            for half in range(2):
                                py, pyk = nextps()
                                hsl = slice(half * 512, (half + 1) * 512)
                                for m in range(8):
                                    sc.add("pe", lambda e, m=m, py=py, u=u, tb=tb, hsl=hsl, s_=s_: e.matmul(
                                        py[:, :], lhsT=u[:, m, tb * 128:(tb + 1) * 128], rhs=W2q[s_][:, m, hsl],
                                        start=(m == 0), stop=(m == 7)), reads=uks + ["W2q%d" % s_], writes=[pyk])
                                xa = xr[:, 4 * T + tb, hsl]
                                sc.add("dve", lambda e, xa=xa, py=py: e.tensor_tensor(out=xa, in0=xa, in1=py[:, :], op=ALU.add),
                                       reads=[pyk, "xr%d" % T], writes=["xr%d" % T])
                ov = out.rearrange("(tt p) d -> p tt d", p=128)
                for T in range(4):
                    for tt in range(4):
                        sc.add("act", lambda e, T=T, tt=tt: e.activation(out=junk[:], in_=xr[:, 4 * T + tt, :], func=AF.Square,
                                                                         accum_out=ss[:, tt:tt + 1]),
                               reads=["xr%d" % T], writes=["Fjunk", "Fss%d" % tt])
                    sc.add("act", lambda e: e.activation(out=lnv[:, 0:4], in_=ss[:, 0:4], func=AF.Ln, scale=1.0 / D,
                                                         bias=eps_sb[:, 0:1]),
                           reads=["Fss%d" % tt for tt in range(4)] + ["eps"], writes=["Flnv"])
                    sc.add("act", lambda e: e.activation(out=rstd[:, 0:4], in_=lnv[:, 0:4], func=AF.Exp, scale=-0.5),
                           reads=["Flnv"], writes=["Frstd"])
                    for tt in range(4):
                        y = yo[tt % 2]
                        yk = "yo%d" % (tt % 2)
                        sc.add("dve", lambda e, T=T, tt=tt, y=y: e.scalar_tensor_tensor(
                            out=y[:], in0=xr[:, 4 * T + tt, :], scalar=rstd[:, tt:tt + 1], in1=gfb[:],
                            op0=ALU.mult, op1=ALU.mult), reads=["xr%d" % T, "Frstd", "gfb"], writes=[yk])
                        sc.add("sp", lambda e, T=T, tt=tt, y=y: e.dma_start(out=ov[:, 4 * T + tt, :], in_=y[:]),
                               reads=[yk], writes=["out"], dma=yk)

        semstack = contextlib.ExitStack()
        with semstack:
            sc.prepare(nc, semstack)
            with nc.Block() as block:
                sc.emit(nc, block)
    return nc


def _constants(j):
    bf = ml_dtypes.bfloat16
    ident = np.eye(128, dtype=np.float32)
    jj = np.arange(128)[:, None]
    kk = np.arange(128)[None, :]
    negtri = np.where(jj >= kk, -1.0, 0.0).astype(np.float32)
    negones = -np.ones((128, 128), np.float32)
    neglow = np.where(jj < kk, -1.0, 0.0).astype(np.float32)
    cmat = np.concatenate([ident, negtri, negones, neglow], axis=1).astype(bf)
    p = np.arange(128)[:, None]
    c = np.arange(512)[None, :]
    cm = np.concatenate([(128 * u + p < c).astype(np.float32) for u in range(4)], axis=1).astype(bf)
    kq = np.arange(128)[:, None].astype(np.float64)
    qq = np.arange(128)[None, :].astype(np.float64)
    dms = []
    for g, d in enumerate((1, 4, 16)):
        slope = 2.0 ** (-8.0 * (4 * g + j + 1) / 12.0)
        sp = qq + 128 - kq
        prev = np.where(sp <= 128, np.exp(-slope * d * sp), 0.0)
        scur = qq - kq
        cur = np.where(scur >= 0, np.exp(-slope * d * scur), 0.0)
        dms.append(np.concatenate([prev, cur], axis=1))
    dmask = np.concatenate(dms, axis=1).astype(np.float32)
    sel = np.zeros((128, 64), np.float32)
    sel[64, :] = 1.0
    return cmat, cm, dmask, sel


def _own_cols(j):
    def dq(g):
        return (4 * g + j) * 64
    cols = {}
    qa, ka, va = 0, 768, 1536
    qb, kb, vb = 2304, 2816, 3328
    s0, s1 = 2 * j, 2 * j + 1
    r = lambda o: list(range(o, o + 64))
    cols["QA"] = r(qa + dq(0)) + r(qa + dq(1))
    cols["KA"] = r(ka + dq(0)) + r(ka + dq(1))
    cols["VA"] = r(va + dq(0)) + r(va + dq(1))
    cols["QB"] = r(qa + dq(2)) + r(qb + s0 * 64)
    cols["KB"] = r(ka + dq(2)) + r(kb + s0 * 64)
    cols["VS"] = r(vb + s0 * 64) + r(vb + s1 * 64)
    cols["QC"] = r(qb + s1 * 64)
    cols["KC"] = r(kb + s1 * 64)
    cols["VG"] = r(va + dq(2))
    idx = []
    for n in OWN_TILES:
        idx += cols[n]
    return np.array(idx)


_NC_CACHE = {}


def kernel(x, norm_mix_g, w_in, b_gate, w_up_dil, w_up_sb, w_out, norm_mlp_g, w_mlp_in, w_mlp_out, norm_final_g,
           _debug=False, _sb_limit=None, _phases=9):
    x = np.asarray(x, np.float32)
    w_in0 = np.asarray(w_in, np.float32)[0]
    key = (_debug, _sb_limit, _phases)
    if key not in _NC_CACHE:
        _NC_CACHE[key] = build_nc(debug=_debug, sb_limit=_sb_limit, phases=_phases)
    nc = _NC_CACHE[key]
    vecs = np.concatenate([
        np.asarray(norm_mix_g, np.float32)[0].reshape(8, 128).T,
        np.asarray(norm_mlp_g, np.float32)[0].reshape(8, 128).T,
        np.asarray(b_gate, np.float32)[0].reshape(16, 128).T], axis=1)
    vecs = np.ascontiguousarray(vecs)
    gfin = np.ascontiguousarray(np.broadcast_to(np.asarray(norm_final_g, np.float32)[None, :], (128, D)))
    w_gate = np.ascontiguousarray(w_in0[:, 3840:5888])
    shared = {
        "w_gate": w_gate,
        "w_ud": np.ascontiguousarray(np.asarray(w_up_dil, np.float32)[0]),
        "w_us": np.ascontiguousarray(np.asarray(w_up_sb, np.float32)[0]),
        "w_o": np.ascontiguousarray(np.asarray(w_out, np.float32)[0]),
        "w_1": np.ascontiguousarray(np.asarray(w_mlp_in, np.float32)[0]),
        "w_2": np.ascontiguousarray(np.asarray(w_mlp_out, np.float32)[0]),
        "vecs": vecs, "gfin": gfin,
    }
    in_maps = []
    for c in range(NCORES):
        b, j = c // 4, c % 4
        cmat, cm, dmask, sel = _constants(j)
        m = dict(shared)
        m["x_full"] = np.ascontiguousarray(x[b])
        m["x_own"] = np.ascontiguousarray(x[b, j * TOK_OWN:(j + 1) * TOK_OWN])
        m["w_own"] = np.ascontiguousarray(w_in0[:, _own_cols(j)])
        m["cmat"] = cmat
        m["cmask"] = cm
        m["dmask"] = dmask
        m["selm"] = sel
        in_maps.append(m)
    res = run_bass_kernel_spmd(nc, in_maps, core_ids=list(range(NCORES)))
    outp = np.empty((2, S, D), np.float32)
    for c in range(NCORES):
        b, j = c // 4, c % 4
        outp[b, j * TOK_OWN:(j + 1) * TOK_OWN] = np.asarray(res.results[c]["y_out"], np.float32)
    if _debug:
        return outp, [np.asarray(res.results[c]["dbg"]) for c in range(NCORES)]
    return outp
```

```python
import numpy as np
import ml_dtypes
import concourse.bass as bass
import concourse.mybir as mybir
from concourse.bass_utils import run_bass_kernel_spmd

F32 = mybir.dt.float32
BF16 = mybir.dt.bfloat16
AF = mybir.ActivationFunctionType
ALU = mybir.AluOpType

S = 8192
D = 1024
NCORES = 8
TOK_OWN = 2048
EPS = 1e-6
SEM_CAP = 2000
XMOD = 64

OWN_TILES = ["QA", "KA", "VA", "QB", "KB", "VS", "QC", "KC", "VG"]
OWN_W = {"QA": 128, "KA": 128, "VA": 128, "QB": 128, "KB": 128, "VS": 128, "QC": 64, "KC": 64, "VG": 64}
OWN_OFF = {}
_o = 0
for _n in OWN_TILES:
    OWN_OFF[_n] = _o
    _o += OWN_W[_n]
OWN_COLS = _o


class Op:
    __slots__ = ("eng", "fn", "dma", "deps", "signal", "num", "idx", "inc")

    def __init__(self, eng, fn, dma, inc=16):
        self.eng, self.fn, self.dma = eng, fn, dma
        self.inc = inc
        self.deps = []
        self.signal = False
        self.num = None
        self.idx = None


class Sched:
    def __init__(self):
        self.ops = []
        self.lastw = {}
        self.readers = {}
        self.floor = []
        self.last_by_src = {}
        self.enabled = True
        import os as _os
        self.maxops = int(_os.environ.get("KMAXOPS", "100000000"))

    @staticmethod
    def _src(op):
        return ("dma", op.dma) if op.dma is not None else ("eng", op.eng)

    def add(self, eng, fn, reads=(), writes=(), dma=None, inc=16):
        op = Op(eng, fn, dma, inc)
        if not self.enabled or len(self.ops) >= self.maxops:
            return op
        op.idx = len(self.ops)
        deps = {}
        for d in self.floor:
            deps[id(d)] = d
        for k in reads:
            w = self.lastw.get(k)
            if w is not None:
                deps[id(w)] = w
        for k in writes:
            w = self.lastw.get(k)
            if w is not None:
                deps[id(w)] = w
            for r in self.readers.get(k, {}).values():
                deps[id(r)] = r
        for k in reads:
            self.readers.setdefault(k, {})[self._src(op)] = op
        for k in writes:
            self.lastw[k] = op
            self.readers[k] = {}
        out = []
        for d in deps.values():
            if d is op:
                continue
            if d.dma is None and d.eng == "pe" and eng == "pe" and dma is None:
                continue
            out.append(d)
        op.deps = out
        self.ops.append(op)
        self.last_by_src[self._src(op)] = op
        return op

    def barrier(self):
        self.floor = list(self.last_by_src.values())
        self.lastw = {}
        self.readers = {}

    def prepare(self, nc, semstack):
        for op in self.ops:
            for d in op.deps:
                d.signal = True
        for d in self.last_by_src.values():
            d.signal = True
        counters = {}
        for op in self.ops:
            if op.dma is not None:
                k = ("dma", op.dma)
                counters[k] = counters.get(k, 0) + 1
                op.num = counters[k]
            elif op.signal:
                k = ("eng", op.eng)
                counters[k] = counters.get(k, 0) + 1
                op.num = counters[k]
        sems = {}
        for k, n in counters.items():
            if k[0] == "dma":
                sems[k] = [semstack.enter_context(nc.semaphore("d_%s" % str(k[1])))]
            else:
                ns = (n + SEM_CAP - 1) // SEM_CAP
                sems[k] = [semstack.enter_context(nc.semaphore("e_%s_%d" % (k[1], i))) for i in range(ns)]
        self.sems = sems

    def emit(self, nc, block):
        sems = self.sems

        def semval(op):
            k = Sched._src(op)
            if k[0] == "dma":
                return sems[k][0], op.num * op.inc
            n = op.num - 1
            return sems[k][n // SEM_CAP], n % SEM_CAP + 1

        def run(engname, e):
            waited = {}
            for op in self.ops:
                if op.eng != engname:
                    continue
                need = {}
                for d in op.deps:
                    k = Sched._src(d)
                    if d.num > need.get(k, (0, None))[0]:
                        need[k] = (d.num, d)
                for k, (n, d) in need.items():
                    if waited.get(k, 0) >= n:
                        continue
                    waited[k] = n
                    s, v = semval(d)
                    e.wait_ge(s, v)
                ins = op.fn(e)
                if op.dma is not None:
                    s, _ = semval(op)
                    ins.then_inc(s, op.inc)
                elif op.signal:
                    s, _ = semval(op)
                    ins.then_inc(s, 1)

        final = [op for op in self.last_by_src.values()]

        def runfinal(e):
            for d in final:
                s, v = semval(d) if d.num is not None else (None, None)
                if s is not None:
                    e.wait_ge(s, v)

        @block.tensor
        def _(e):
            run("pe", e)

        @block.scalar
        def _(e):
            run("act", e)

        @block.vector
        def _(e):
            run("dve", e)

        @block.gpsimd
        def _(e):
            run("pool", e)

        @block.sync
        def _(e):
            run("sp", e)
            runfinal(e)


def build_nc(debug=False, sb_limit=None, phases=9, ntA=16, tlist=None, lite=False):
    import contextlib

    nc = bass.Bass("TRN2", target_bir_lowering=False)
    dt = nc.dram_tensor
    x_full = dt("x_full", [S, D], F32, kind="ExternalInput").ap()
    x_own = dt("x_own", [TOK_OWN, D], F32, kind="ExternalInput").ap()
    w_own = dt("w_own", [D, OWN_COLS], F32, kind="ExternalInput").ap()
    if lite:
        _real_dt = dt

        def dt(name, shape, dtype, kind=None):
            if kind == "ExternalInput" and name in ("w_gate", "w_ud", "w_us", "w_o", "w_1", "w_2"):
                shape = [128, 8]
            return _real_dt(name, shape, dtype, kind=kind) if kind else _real_dt(name, shape, dtype)
    w_gate = dt("w_gate", [D, 2 * D], F32, kind="ExternalInput").ap()
    w_ud = dt("w_ud", [256, D], F32, kind="ExternalInput").ap()
    w_us = dt("w_us", [512, D], F32, kind="ExternalInput").ap()
    w_o = dt("w_o", [D, D], F32, kind="ExternalInput").ap()
    w_1 = dt("w_1", [D, 4 * D], F32, kind="ExternalInput").ap()
    w_2 = dt("w_2", [4 * D, D], F32, kind="ExternalInput").ap()
    vecs = dt("vecs", [128, 32], F32, kind="ExternalInput").ap()
    gfin = dt("gfin", [128, D], F32, kind="ExternalInput").ap()
    cmask = dt("cmask", [128, 4 * 512], BF16, kind="ExternalInput").ap()
    cmat = dt("cmat", [128, 3 * 128], BF16, kind="ExternalInput").ap()
    dmask = dt("dmask", [128, 3 * 256], F32, kind="ExternalInput").ap()
    selm = dt("selm", [128, 64], F32, kind="ExternalInput").ap()
    out = dt("y_out", [TOK_OWN, D], F32, kind="ExternalOutput").ap()
    o_locq = [dt("o_loc%d" % q, [192, TOK_OWN], BF16) for q in range(4)]
    o_all = dt("o_all", [4, 4 * 192, TOK_OWN], BF16)
    o_mine = dt("o_mine", [4 * 192, TOK_OWN], BF16)
    if debug:
        dbg = dt("dbg", [4 * 192, S], BF16, kind="ExternalOutput").ap()

    sc = Sched()
    es = contextlib.ExitStack()
    with es:
        def sb(name, shape, dtype, stack=es):
            return stack.enter_context(nc.sbuf_tensor(name, shape, dtype))

        def ps(name, shape, dtype, stack=es):
            return stack.enter_context(nc.psum_tensor(name, shape, dtype))

        vec_sb = sb("vec_sb", [128, 32], F32)
        cmat_sb = sb("cmat_sb", [128, 384], BF16)
        ident = cmat_sb[:, 0:128]
        negtri = cmat_sb[:, 128:256]
        negones = cmat_sb[:, 256:384]
        sc.add("sp", lambda e: e.dma_start(out=vec_sb[:], in_=vecs[:, :]), writes=["vec"], dma="c0a")
        sc.add("sp", lambda e: e.dma_start(out=cmat_sb[:], in_=cmat[:, :]), writes=["cmat"], dma="c0b")
        g1 = vec_sb[:, 0:8]
        g2 = vec_sb[:, 8:16]
        bg = vec_sb[:, 16:32]

        def rmsnorm_T(xsrc_tiles, ntt, hT_dst, gain, keyp, scratch, psT, tag, part=None):
            ss, lnv, rstd, junk, xn = scratch
            for tt, (xa, rk) in enumerate(xsrc_tiles if part in (None, "stats") else []):
                sc.add("act", lambda e, xa=xa, tt=tt: e.activation(out=junk[:], in_=xa, func=AF.Square,
                                                                   accum_out=ss[:, tt:tt + 1]),
                       reads=[rk], writes=[tag + "junk", tag + "ss%d" % tt])
            sskeys = [tag + "ss%d" % tt for tt in range(ntt)]
            if part in (None, "stats"):
                sc.add("act", lambda e: e.activation(out=lnv[:, 0:ntt], in_=ss[:, 0:ntt], func=AF.Ln,
                                                     scale=1.0 / D, bias=eps_sb[:, 0:1]),
                       reads=sskeys + ["eps"], writes=[tag + "lnv"])
                sc.add("act", lambda e: e.activation(out=rstd[:, 0:ntt], in_=lnv[:, 0:ntt], func=AF.Exp, scale=-0.5),
                       reads=[tag + "lnv"], writes=[tag + "rstd"])
            for tt, (xa, rk) in enumerate(xsrc_tiles if part in (None, "stats") else []):
                if tt % 2 == 0:
                    sc.add("dve", lambda e, xa=xa, tt=tt: e.tensor_scalar(out=xn[:, tt, :], in0=xa,
                                                                         scalar1=rstd[:, tt:tt + 1], scalar2=None,
                                                                         op0=ALU.mult),
                           reads=[rk, tag + "rstd"], writes=[tag + "xn%d" % tt])
                else:
                    sc.add("act", lambda e, xa=xa, tt=tt: e.activation(out=xn[:, tt, :], in_=xa, func=AF.Copy,
                                                                       scale=rstd[:, tt:tt + 1]),
                           reads=[rk, tag + "rstd"], writes=[tag + "xn%d" % tt])
            for c in range(8 if part in (None, "trans") else 0):
                pb = psT[c % 2]
                pk = tag + "psT%d" % (c % 2)
                for tt in range(ntt):
                    sc.add("pe", lambda e, pb=pb, tt=tt, c=c: e.transpose(out=pb[:, tt * 128:(tt + 1) * 128],
                                                                         in_=xn[:, tt, c * 128:(c + 1) * 128],
                                                                         identity=ident),
                           reads=[tag + "xn%d" % tt, "cmat"], writes=[pk])
                dst, dk = hT_dst(c)
                if c % 2 == 0:
                    sc.add("act", lambda e, pb=pb, dst=dst, c=c: e.activation(out=dst, in_=pb[:, 0:ntt * 128],
                                                                              func=AF.Copy, scale=gain[:, c:c + 1]),
                           reads=[pk, "vec"], writes=[dk])
                else:
                    sc.add("dve", lambda e, pb=pb, dst=dst, c=c: e.tensor_scalar(out=dst, in0=pb[:, 0:ntt * 128],
                                                                                 scalar1=gain[:, c:c + 1],
                                                                                 scalar2=None, op0=ALU.mult),
                           reads=[pk, "vec"], writes=[dk])

        eps_sb = sb("eps_sb", [128, 1], F32)
        sc.add("dve", lambda e: e.memset(eps_sb[:], EPS), writes=["eps"])

        def mk_gather(q):
            def f(e):
                return e.collective_compute("AllGather", ALU.bypass, replica_groups=[[0, 1, 2, 3], [4, 5, 6, 7]],
                                            ins=[o_locq[q].ap().opt()], outs=[o_all.ap()[q].opt()])
            return f

        def add_gather(q):
            olk = ["o_loc_a%d" % ch for ch in range(4 * q, 4 * q + 4)] + \
                  ["o_loc_b%d_%d" % (orow, qt) for orow in (64, 128) for qt in range(4 * q, 4 * q + 4)]
            sc.add("pool", mk_gather(q), reads=olk, writes=["o_all%d" % q], dma="cc", inc=1)

        AB = contextlib.ExitStack()
        with AB:
            QB = sb("QB", [128, S], BF16, AB)
            KB = sb("KB", [128, S], BF16, AB)
            QC = sb("QC", [64, S], BF16, AB)
            KC = sb("KC", [64, S], BF16, AB)
            Vs = sb("Vs", [128, 64, 128], BF16, AB)
            DIL = contextlib.ExitStack()
            with DIL:
                QA = sb("QA", [128, S], BF16, DIL)
                KA = sb("KA", [128, S], BF16, DIL)
                Vd = [sb("Vd%d" % g, [128, 64, 66], BF16, DIL) for g in range(3)]
                for g in range(3):
                    sc.add("pool", lambda e, g=g: e.memset(Vd[g][:, :, 64:65], 1.0), writes=["Vd%d_ones" % g])

                PA = contextlib.ExitStack()
                with PA:
                    Wown = sb("Wown", [128, 8, OWN_COLS], BF16, PA)
                    xs = [sb("xs%d" % i, [128, 2, D], F32, PA) for i in range(2)]
                    xn = sb("xnA", [128, 2, D], BF16, PA)
                    hT = [sb("hTA%d" % i, [128, 8, 512], BF16, PA) for i in range(2)]
                    junk = sb("junkA", [128, D], BF16, PA)
                    ss = sb("ssA", [128, 4], F32, PA)
                    lnv = sb("lnvA", [128, 4], F32, PA)
                    rstd = sb("rstdA", [128, 4], F32, PA)
                    vtA = sb("vtA", [128, 512], BF16, PA)
                    vtS = sb("vtS", [128, 512], BF16, PA)
                    vt3 = sb("vt3", [128, 2048], BF16, PA)
                    sc.add("dve", lambda e: e.memset(vt3[64:128, :], 0.0), writes=["vt3"])
                    psT = [ps("psTA%d" % i, [128, 1024], BF16, PA) for i in range(2)]
                    psP = [ps("psPA%d" % i, [128, 512], F32, PA) for i in range(3)]
                    psV = [ps("psVA%d" % i, [128, 1024], BF16, PA) for i in range(2)]

                    wv = w_own.rearrange("(c p) n -> p c n", p=128)
                    for c0 in range(0, 8, 2):
                        sc.add("pool", lambda e, c0=c0: e.dma_start(out=Wown[:, c0:c0 + 2, :], in_=wv[:, c0:c0 + 2, :]),
                               writes=["Wown%d" % c0], dma="wown")
                    wkeys = ["Wown%d" % c0 for c0 in range(0, 8, 2)]
                    xv = x_full.rearrange("(n tt p) d -> n p tt d", tt=2, p=128)

                    pcount = [0]

                    def proj_tile(t, name, hTt, hk):
                        w = OWN_W[name]
                        off = OWN_OFF[name]
                        i = pcount[0] % 3
                        pcount[0] += 1
                        pp = psP[i]
                        pk = "psPA%d" % i
                        for c in range(8):
                            sc.add("pe", lambda e, c=c, pp=pp: e.matmul(pp[0:w, :], lhsT=Wown[:, c, off:off + w],
                                                                       rhs=hTt[:, c, :], start=(c == 0), stop=(c == 7)),
                                   reads=wkeys + [hk], writes=[pk])
                        return pp, pk

                    def deint(ap_rows, d):
                        return ap_rows.rearrange("p (l r) -> p r l", r=d)

                    def norm_half(ti, t, half, part):
                        hTt = hT[ti % 2]
                        hk = "hT%d" % (ti % 2)
                        n = 2 * t + half
                        xsl = xs[n % 2]
                        xk = "xs%d" % (n % 2)
                        if part == "stats":
                            sc.add("sp", lambda e, xsl=xsl, n=n: e.dma_start(out=xsl[:], in_=xv[n]),
                                   writes=[xk], dma=xk)
                        rmsnorm_T([(xsl[:, 0, :], xk), (xsl[:, 1, :], xk)], 2,
                                  lambda c, half=half, hTt=hTt, hk=hk: (hTt[:, c, half * 256:(half + 1) * 256], hk),
                                  g1, None, (ss, lnv, rstd, junk, xn), psT, "A", part=part)

                    def proj_part1(ti, t):
                        hTt = hT[ti % 2]
                        hk = "hT%d" % (ti % 2)
                        tsl = slice(t * 512, (t + 1) * 512)
                        pp, pk = proj_tile(t, "QA", hTt, hk)
                        sc.add("act", lambda e, pp=pp, tsl=tsl: e.activation(out=QA[0:64, tsl], in_=pp[0:64, :],
                                                                             func=AF.Copy, scale=0.125),
                               reads=[pk], writes=["QA"])
                        sc.add("dve", lambda e, pp=pp, t=t: e.tensor_scalar(
                            out=QA[64:128, :].rearrange("p (r n l) -> p r n l", r=4, n=16)[:, :, t, :],
                            in0=deint(pp[64:128, :], 4), scalar1=0.125, scalar2=None, op0=ALU.mult),
                            reads=[pk], writes=["QA"])
                        pp, pk = proj_tile(t, "KA", hTt, hk)
                        sc.add("act", lambda e, pp=pp, tsl=tsl: e.activation(out=KA[0:64, tsl], in_=pp[0:64, :],
                                                                             func=AF.Copy), reads=[pk], writes=["KA"])
                        sc.add("dve", lambda e, pp=pp, t=t: e.tensor_copy(
                            out=KA[64:128, :].rearrange("p (r n l) -> p r n l", r=4, n=16)[:, :, t, :],
                            in_=deint(pp[64:128, :], 4)), reads=[pk], writes=["KA"])
                        pp, pk = proj_tile(t, "VA", hTt, hk)
                        sc.add("act", lambda e, pp=pp: e.activation(out=vtA[0:64, :], in_=pp[0:64, :], func=AF.Copy),
                               reads=[pk], writes=["vtA"])
                        sc.add("dve", lambda e, pp=pp: e.tensor_copy(
                            out=vtA[64:128, :].rearrange("p (r l) -> p r l", r=4), in_=deint(pp[64:128, :], 4)),
                            reads=[pk], writes=["vtA"])
                        pv = psV[0]
                        for bi in range(4):
                            sc.add("pe", lambda e, bi=bi, pv=pv: e.transpose(out=pv[:, bi * 128:(bi + 1) * 128],
                                                                            in_=vtA[:, bi * 128:(bi + 1) * 128],
                                                                            identity=ident),
                                   reads=["vtA", "cmat"], writes=["psVA0"])
                        for bi in range(4):
                            sc.add("act", lambda e, pv=pv, t=t, bi=bi: e.activation(
                                out=Vd[0][:, 4 * t + bi, 0:64], in_=pv[:, bi * 128:bi * 128 + 64], func=AF.Copy),
                                reads=["psVA0"], writes=["Vd0"])
                            sc.add("act", lambda e, pv=pv, t=t, bi=bi: e.activation(
                                out=Vd[1][:, bi * 16 + t, 0:64], in_=pv[:, bi * 128 + 64:bi * 128 + 128], func=AF.Copy),
                                reads=["psVA0"], writes=["Vd1"])

                    def proj_part2(ti, t):
                        hTt = hT[ti % 2]
                        hk = "hT%d" % (ti % 2)
                        tsl = slice(t * 512, (t + 1) * 512)
                        nn, qq = t // 4, t % 4
                        pp, pk = proj_tile(t, "QB", hTt, hk)
                        sc.add("dve", lambda e, pp=pp, nn=nn, qq=qq: e.tensor_scalar(
                            out=QB[0:64, :].rearrange("p (r n i) -> p r n i", r=16, n=4)[:, :, nn, 32 * qq:32 * qq + 32],
                            in0=deint(pp[0:64, :], 16), scalar1=0.125, scalar2=None, op0=ALU.mult),
                            reads=[pk], writes=["QB"])
                        sc.add("act", lambda e, pp=pp, tsl=tsl: e.activation(out=QB[64:128, tsl], in_=pp[64:128, :],
                                                                             func=AF.Copy, scale=0.125),
                               reads=[pk], writes=["QB"])
                        pp, pk = proj_tile(t, "KB", hTt, hk)
                        sc.add("dve", lambda e, pp=pp, nn=nn, qq=qq: e.tensor_copy(
                            out=KB[0:64, :].rearrange("p (r n i) -> p r n i", r=16, n=4)[:, :, nn, 32 * qq:32 * qq + 32],
                            in_=deint(pp[0:64, :], 16)), reads=[pk], writes=["KB"])
                        sc.add("act", lambda e, pp=pp, tsl=tsl: e.activation(out=KB[64:128, tsl], in_=pp[64:128, :],
                                                                             func=AF.Copy), reads=[pk], writes=["KB"])
                        pp, pk = proj_tile(t, "VS", hTt, hk)
                        sc.add("act", lambda e, pp=pp: e.activation(out=vtS[:, :], in_=pp[:, :], func=AF.Copy),
                               reads=[pk], writes=["vtS"])
                        pv = psV[1]
                        for bi in range(4):
                            sc.add("pe", lambda e, bi=bi, pv=pv: e.transpose(out=pv[:, bi * 128:(bi + 1) * 128],
                                                                            in_=vtS[:, bi * 128:(bi + 1) * 128],
                                                                            identity=ident),
                                   reads=["vtS", "cmat"], writes=["psVA1"])
                        sc.add("dve", lambda e, pv=pv, t=t: e.tensor_copy(
                            out=Vs[:, 4 * t:4 * t + 4, :].rearrange("p b d -> p (b d)"), in_=pv[:, 0:512]),
                            reads=["psVA1"], writes=["Vs"])
                        pp, pk = proj_tile(t, "QC", hTt, hk)
                        sc.add("act", lambda e, pp=pp, tsl=tsl: e.activation(out=QC[0:64, tsl], in_=pp[0:64, :],
                                                                             func=AF.Copy, scale=0.125),
                               reads=[pk], writes=["QC"])
                        pp, pk = proj_tile(t, "KC", hTt, hk)
                        sc.add("dve", lambda e, pp=pp, tsl=tsl: e.tensor_copy(out=KC[0:64, tsl], in_=pp[0:64, :]),
                               reads=[pk], writes=["KC"])
                        pp, pk = proj_tile(t, "VG", hTt, hk)
                        sc.add("act", lambda e, pp=pp, qq=qq: e.activation(
                            out=vt3[0:64, :].rearrange("p (r i) -> p r i", r=16)[:, :, 32 * qq:32 * qq + 32],
                            in_=deint(pp[0:64, :], 16), func=AF.Copy), reads=[pk], writes=["vt3"])
                        if qq == 3:
                            for r in range(16):
                                pv = psV[r // 8]
                                pvk = "psVA%d" % (r // 8)
                                rr = r % 8
                                sc.add("pe", lambda e, r=r, rr=rr, pv=pv: e.transpose(out=pv[:, rr * 128:(rr + 1) * 128],
                                                                                      in_=vt3[:, r * 128:(r + 1) * 128],
                                                                                      identity=ident),
                                       reads=["vt3", "cmat"], writes=[pvk])
                            for r in range(16):
                                pv = psV[r // 8]
                                pvk = "psVA%d" % (r // 8)
                                rr = r % 8
                                if False:
                                    sc.add("act", lambda e, pv=pv, nn=nn, r=r, rr=rr: e.activation(
                                        out=Vd[2][:, r * 4 + nn, 0:64], in_=pv[:, rr * 128:rr * 128 + 64], func=AF.Copy),
                                        reads=[pvk], writes=["Vd2"])
                                else:
                                    sc.add("dve", lambda e, pv=pv, nn=nn, r=r, rr=rr: e.tensor_copy(
                                        out=Vd[2][:, r * 4 + nn, 0:64], in_=pv[:, rr * 128:rr * 128 + 64]),
                                        reads=[pvk], writes=["Vd2"])

                    tl_ = list(tlist if tlist is not None else range(ntA))
                    for half in range(2):
                        norm_half(0, tl_[0], half, "stats")
                        norm_half(0, tl_[0], half, "trans")
                    for ti, t in enumerate(tl_):
                        nxt = ti + 1 < len(tl_)
                        if nxt:
                            norm_half(ti + 1, tl_[ti + 1], 0, "stats")
                        proj_part1(ti, t)
                        if nxt:
                            norm_half(ti + 1, tl_[ti + 1], 0, "trans")
                            norm_half(ti + 1, tl_[ti + 1], 1, "stats")
                        proj_part2(ti, t)
                        if nxt:
                            norm_half(ti + 1, tl_[ti + 1], 1, "trans")
                sc.barrier()
                if phases < 2:
                    sc.enabled = False

                PD = contextlib.ExitStack()
                with PD:
                    acc = sb("acc", [65, S], F32, PD)
                    dm = sb("dm", [128, 768], F32, PD)
                    sel_sb = sb("sel_sb", [128, 64], F32, PD)
                    sc.add("sp", lambda e: e.dma_start(out=dm[:], in_=dmask[:, :]), writes=["dm"], dma="c1a")
                    sc.add("sp", lambda e: e.dma_start(out=sel_sb[:], in_=selm[:, :]), writes=["sel"], dma="c1b")
                    pex = [sb("pex%d" % i, [128, 512], F32, PD) for i in range(2)]
                    pbf = [sb("pbf%d" % i, [128, 512], BF16, PD) for i in range(2)]
                    rec = sb("rec", [64, 512], F32, PD)
                    oa = [sb("oa%d" % i, [64, 512], BF16, PD) for i in range(2)]
                    psS = [ps("psS%d" % i, [128, 512], F32, PD) for i in range(2)]
                    psO = [ps("psOd%d" % i, [128, 512], F32, PD) for i in range(2)]
                    psD = ps("psDd", [128, 512], F32, PD)
                    groups = [
                        (0, 1, QA, KA, 0),
                        (1, 4, QA, KA, 64),
                        (2, 16, QB, KB, 0),
                    ]
                    qkey = {0: "QA", 1: "QA", 2: "QB"}
                    kkey = {0: "KA", 1: "KA", 2: "KB"}
                    pairno = 0
                    import os as _os3
                    _dg = _os3.environ.get("DILG")
                    for (g, d, Qt, Kt, ro) in groups:
                        if _dg is not None and str(g) not in _dg:
                            continue
                        nb = 64 // d
                        rows = slice(ro, ro + 64)
                        if d == 1:
                            banks = [[(0, n0 + i) for i in range(4)] for n0 in range(0, 64, 4)]
                        else:
                            banks = [[(r0 + i, n) for i in range(4)] for n in range(nb) for r0 in range(0, d, 4)]
                        for bank in banks:
                            po = psO[pairno % 2]
                            pok = "psOd%d" % (pairno % 2)
                            pairno += 1
                            for half in range(2):
                                blks = bank[2 * half:2 * half + 2]
                                i2 = (pairno * 2 + half) % 2
                                pS = psS[i2]
                                psk = "psS%d" % i2
                                for bi, (r, n) in enumerate(blks):
                                    blk = r * nb + n
                                    qsl = slice(blk * 128, (blk + 1) * 128)
                                    pblk = blk - 1 if n > 0 else blk
                                    ksl_p = slice(pblk * 128, (pblk + 1) * 128)
                                    sc.add("pe", lambda e, pS=pS, bi=bi, ksl_p=ksl_p, qsl=qsl, Kt=Kt, Qt=Qt, rows=rows: e.matmul(
                                        pS[:, bi * 256:bi * 256 + 128], lhsT=Kt[rows, ksl_p], rhs=Qt[rows, qsl],
                                        start=True, stop=True), reads=[qkey[g], kkey[g]], writes=[psk])
                                    sc.add("pe", lambda e, pS=pS, bi=bi, qsl=qsl, Kt=Kt, Qt=Qt, rows=rows: e.matmul(
                                        pS[:, bi * 256 + 128:bi * 256 + 256], lhsT=Kt[rows, qsl], rhs=Qt[rows, qsl],
                                        start=True, stop=True), reads=[qkey[g], kkey[g]], writes=[psk])
                                pe_ = pex[i2]
                                pb_ = pbf[i2]
                                sc.add("act", lambda e, pe_=pe_, pS=pS: e.activation(out=pe_[:], in_=pS[:, :], func=AF.Exp),
                                       reads=[psk], writes=["pex%d" % i2])
                                for bi in range(2):
                                    sc.add("dve", lambda e, pe_=pe_, pb_=pb_, bi=bi, g=g: e.tensor_tensor(
                                        out=pb_[:, bi * 256:(bi + 1) * 256], in0=pe_[:, bi * 256:(bi + 1) * 256],
                                        in1=dm[:, g * 256:(g + 1) * 256], op=ALU.mult),
                                        reads=["pex%d" % i2, "dm"], writes=["pbf%d" % i2])
                                for bi, (r, n) in enumerate(blks):
                                    blk = r * nb + n
                                    slot = 2 * half + bi
                                    osl = po[0:65, slot * 128:(slot + 1) * 128]
                                    if n > 0:
                                        sc.add("pe", lambda e, osl=osl, pb_=pb_, bi=bi, blk=blk, g=g: e.matmul(
                                            osl, lhsT=Vd[g][:, blk - 1, 0:65], rhs=pb_[:, bi * 256:bi * 256 + 128],
                                            start=True, stop=False),
                                            reads=["pbf%d" % i2, "Vd%d" % g, "Vd%d_ones" % g], writes=[pok])
                                    sc.add("pe", lambda e, osl=osl, pb_=pb_, bi=bi, blk=blk, g=g, n=n: e.matmul(
                                        osl, lhsT=Vd[g][:, blk, 0:65], rhs=pb_[:, bi * 256 + 128:bi * 256 + 256],
                                        start=(n == 0), stop=True),
                                        reads=["pbf%d" % i2, "Vd%d" % g, "Vd%d_ones" % g], writes=[pok])
                            r0, n0 = bank[0]
                            if d == 1:
                                dst = acc[0:65, n0 * 128:(n0 + 4) * 128]
                                sc.add("act", lambda e, dst=dst, po=po: e.activation(out=dst, in_=po[0:65, :], func=AF.Copy),
                                       reads=[pok], writes=["acc"])
                            else:
                                base = 128 * n0 * d + r0
                                span = acc[0:65, 128 * n0 * d:128 * (n0 + 1) * d].rearrange("p (i r) -> p r i", r=d)
                                dst = span[:, r0:r0 + 4, :]
                                src = po[0:65, :].rearrange("p (r i) -> p r i", r=4)
                                sc.add("dve", lambda e, dst=dst, src=src: e.tensor_tensor(out=dst, in0=dst, in1=src, op=ALU.add),
                                       reads=[pok, "acc"], writes=["acc"])
                    for ch in range(16):
                        csl = slice(ch * 512, (ch + 1) * 512)
                        sc.add("pe", lambda e, csl=csl: e.matmul(psD[0:64, :], lhsT=sel_sb[0:65, :], rhs=acc[0:65, csl],
                                                                 start=True, stop=True),
                               reads=["acc", "sel"], writes=["psDd"])
                        sc.add("dve", lambda e: e.reciprocal(out=rec[:], in_=psD[0:64, :]), reads=["psDd"], writes=["rec"])
                        oo = oa[ch % 2]
                        sc.add("dve", lambda e, oo=oo, csl=csl: e.tensor_tensor(out=oo[:], in0=acc[0:64, csl], in1=rec[:],
                                                                                op=ALU.mult),
                               reads=["acc", "rec"], writes=["oa%d" % (ch % 2)])
                        sc.add("sp", lambda e, oo=oo, ch=ch: e.dma_start(
                            out=o_locq[ch // 4].ap()[0:64, (ch % 4) * 512:(ch % 4 + 1) * 512], in_=oo[:]),
                               reads=["oa%d" % (ch % 2)], writes=["o_loc_a%d" % ch], dma="oa%d" % (ch % 2))
                sc.barrier()
                if phases < 3:
                    sc.enabled = False

            PS_ = contextlib.ExitStack()
            with PS_:
                NB = 4
                cm = sb("cm", [128, 2048], BF16, PS_)
                sc.add("sp", lambda e: e.dma_start(out=cm[:], in_=cmask[:, :]), writes=["cm"], dma="c2")
                Eb = [sb("Eb%d" % i, [128, 512], F32, PS_) for i in range(NB)]
                Lb = [sb("Lb%d" % i, [128, 512], BF16, PS_) for i in range(NB)]
                Ab = [sb("Ab%d" % i, [128, 512], BF16, PS_) for i in range(NB)]
                LsF = sb("LsF", [128, 512], F32, PS_)
                LsB = [sb("LsB%d" % i, [128, 512], BF16, PS_) for i in range(4)]
                ost = [sb("ost%d" % i, [64, 512], BF16, PS_) for i in range(2)]
                psZ = [ps("psZ%d" % i, [128, 512], F32, PS_) for i in range(2)]
                psB = [ps("psB%d" % i, [128, 512], F32, PS_) for i in range(2)]
                psOs = [ps("psOs%d" % i, [128, 512], F32, PS_) for i in range(2)]
                heads = [
                    (QB, KB, slice(64, 128), "QB", "KB", slice(0, 64), 64),
                    (QC, KC, slice(0, 64), "QC", "KC", slice(64, 128), 128),
                ]
                pairs = []
                qn = 0
                for qt in range(16):
                    for hi, hd in enumerate(heads):
                        kmax = 4 * qt + 3
                        kmin = 0 if sb_limit is None else max(0, kmax - sb_limit + 1)
                        for kb in range(kmax, kmin - 1, -1):
                            pairs.append(dict(h=hd, hi=hi, qt=qt, kb=kb, first=(kb == kmax), last=(kb == kmin),
                                              u=(kb - 4 * qt) if kb >= 4 * qt else None, qn=qn))
                        qn += 1

                Xb = [sb("Xb%d" % i, [128, 512], F32, PS_) for i in range(2)]

                def stage1(i, p):
                    Qt, Kt, rows, qk, kk, vcols, orow = p["h"]
                    qsl = slice(p["qt"] * 512, (p["qt"] + 1) * 512)
                    ksl = slice(p["kb"] * 128, (p["kb"] + 1) * 128)
                    z = psZ[i % 2]
                    zk = "psZ%d" % (i % 2)
                    E, L = Eb[i % NB], Lb[i % NB]
                    ek, lk = "Eb%d" % (i % NB), "Lb%d" % (i % NB)
                    sc.add("pe", lambda e: e.matmul(z[:, :], lhsT=Kt[rows, ksl], rhs=Qt[rows, qsl], start=True, stop=True),
                           reads=[qk, kk], writes=[zk])
                    sc.add("act", lambda e: e.activation(out=E[:], in_=z[:, :], func=AF.Exp), reads=[zk], writes=[ek])
                    if p["u"] is not None:
                        u = p["u"]
                        sc.add("dve", lambda e: e.tensor_tensor(out=E[:], in0=E[:], in1=cm[:, u * 512:(u + 1) * 512],
                                                                op=ALU.mult), reads=[ek, "cm"], writes=[ek])
                    sc.add("act", lambda e: e.activation(out=L[:], in_=E[:], func=AF.Ln, bias=1.0), reads=[ek], writes=[lk])
                    if not p["last"]:
                        nxt = LsB[(p["cnt"] + 1) % 4]
                        nxtk = "LsB%d" % ((p["cnt"] + 1) % 4)
                        if p["first"]:
                            sc.add("dve", lambda e: e.tensor_copy(out=nxt[:], in_=L[:]), reads=[lk], writes=[nxtk])
                            sc.add("dve", lambda e: e.tensor_copy(out=LsF[:], in_=L[:]), reads=[lk], writes=["LsF"])
                        else:
                            sc.add("dve", lambda e: e.tensor_tensor(out=LsF[:], in0=LsF[:], in1=L[:], op=ALU.add),
                                   reads=[lk, "LsF"], writes=["LsF"])
                            sc.add("dve", lambda e: e.tensor_copy(out=nxt[:], in_=LsF[:]), reads=["LsF"], writes=[nxtk])

                def stage2(i, p):
                    Qt, Kt, rows, qk, kk, vcols, orow = p["h"]
                    qsl = slice(p["qt"] * 512, (p["qt"] + 1) * 512)
                    kb = p["kb"]
                    bq = psB[i % 2]
                    bk = "psB%d" % (i % 2)
                    E, L, A = Eb[i % NB], Lb[i % NB], Ab[i % NB]
                    ek, lk, ak = "Eb%d" % (i % NB), "Lb%d" % (i % NB), "Ab%d" % (i % NB)
                    X = Xb[i % 2]
                    xk_ = "Xb%d" % (i % 2)
                    first, last = p["first"], p["last"]
                    cur = LsB[p["cnt"] % 4]
                    curk = "LsB%d" % (p["cnt"] % 4)
                    sc.add("pe", lambda e: e.matmul(bq[:, :], lhsT=negtri, rhs=L[:], start=True, stop=first),
                           reads=[lk, "cmat"], writes=[bk])
                    if not first:
                        sc.add("pe", lambda e: e.matmul(bq[:, :], lhsT=negones, rhs=cur[:], start=False, stop=True),
                               reads=[curk, "cmat"], writes=[bk])
                    sc.add("act", lambda e: e.activation(out=X[:], in_=bq[:, :], func=AF.Exp), reads=[bk], writes=[xk_])
                    sc.add("dve", lambda e: e.tensor_tensor(out=A[:], in0=E[:], in1=X[:], op=ALU.mult),
                           reads=[ek, xk_], writes=[ak])

                def stage3(i, p):
                    Qt, Kt, rows, qk, kk, vcols, orow = p["h"]
                    qsl = slice(p["qt"] * 512, (p["qt"] + 1) * 512)
                    kb = p["kb"]
                    A = Ab[i % NB]
                    ak = "Ab%d" % (i % NB)
                    first, last = p["first"], p["last"]
                    po = psOs[p["qn"] % 2]
                    pok = "psOs%d" % (p["qn"] % 2)
                    sc.add("pe", lambda e: e.matmul(po[0:64, :], lhsT=Vs[:, kb, vcols], rhs=A[:], start=first, stop=last,
                                                    skip_group_check=True),
                           reads=[ak, "Vs"], writes=[pok])
                    if last:
                        oo = ost[p["qn"] % 2]
                        ook = "ost%d" % (p["qn"] % 2)
                        sc.add("act", lambda e: e.activation(out=oo[:], in_=po[0:64, :], func=AF.Copy), reads=[pok], writes=[ook])
                        qt_ = p["qt"]
                        sc.add("sp", lambda e: e.dma_start(
                            out=o_locq[qt_ // 4].ap()[orow:orow + 64, (qt_ % 4) * 512:(qt_ % 4 + 1) * 512], in_=oo[:]),
                               reads=[ook], writes=["o_loc_b%d_%d" % (orow, qt_)], dma=ook)

                cnt = 0
                for p in pairs:
                    p["cnt"] = cnt
                    cnt += 1
                DEPTH = 2
                for i in range(len(pairs) + 2 * DEPTH):
                    if i < len(pairs):
                        stage1(i, pairs[i])
                    if 0 <= i - DEPTH < len(pairs):
                        stage2(i - DEPTH, pairs[i - DEPTH])
                    if 0 <= i - 2 * DEPTH < len(pairs):
                        p3 = pairs[i - 2 * DEPTH]
                        stage3(i - 2 * DEPTH, p3)
                        if phases >= 4 and p3["last"] and p3["hi"] == 1 and p3["qt"] % 4 == 3:
                            add_gather(p3["qt"] // 4)
            sc.barrier()
            if phases < 4:
                sc.enabled = False

        oallk = ["o_all%d" % q for q in range(4)]
        if debug:
            for q in range(4):
                sc.add("pool", lambda e, q=q: e.dma_start(out=dbg[:, q * TOK_OWN:(q + 1) * TOK_OWN], in_=o_all.ap()[q]),
                       reads=oallk, writes=["dbg%d" % q], dma="dbg")

        if phases < 5:
            sc.enabled = False
        PC = contextlib.ExitStack()
        with PC:
            xr = sb("xr", [128, 16, D], F32, PC)
            hT2 = sb("hT2", [128, 8, TOK_OWN], BF16, PC)
            junk = sb("junkC", [128, D], BF16, PC)
            ss = sb("ssC", [128, 4], F32, PC)
            lnv = sb("lnvC", [128, 4], F32, PC)
            rstd = sb("rstdC", [128, 4], F32, PC)
            xn = sb("xnC", [128, 4, D], BF16, PC)
            xo = x_own.rearrange("(tt p) d -> p tt d", p=128)
            for q4 in range(4):
                sc.add("sp", lambda e, q4=q4: e.dma_start(out=xr[:, 4 * q4:4 * q4 + 4, :], in_=xo[:, 4 * q4:4 * q4 + 4, :]),
                       writes=["xr%d" % q4], dma="xr%d" % q4)
            psT = [ps("psTC%d" % i, [128, 1024], BF16, PC) for i in range(2)]
            psM = [ps("psMC%d" % i, [128, 512], F32, PC) for i in range(6)]
            pmc = [0]

            def nextps():
                i = pmc[0] % 6
                pmc[0] += 1
                return psM[i], "psMC%d" % i

            C1 = contextlib.ExitStack()
            with C1:
                Wg = sb("Wg", [128, 8, 2 * D], BF16, C1)
                Wud = sb("Wud", [128, 2, D], BF16, C1)
                Wus = sb("Wus", [128, 4, D], BF16, C1)
                Wo = sb("Wo", [128, 8, D], BF16, C1)
                oA = sb("oA", [128, 2, 512], BF16, C1)
                oB = sb("oB", [128, 4, 512], BF16, C1)
                hT1 = sb("hT1", [128, 8, 512], BF16, C1)
                Ga = sb("Ga", [128, 512], F32, C1)
                Gb = sb("Gb", [128, 512], F32, C1)
                t1 = sb("t1", [128, 512], F32, C1)
                t2 = sb("t2", [128, 512], F32, C1)
                mg = sb("mg", [128, 8, 512], BF16, C1)
                wgv = w_gate.rearrange("(c p) n -> p c n", p=128)
                for c0 in range(0, 8, 2):
                    sc.add("pool", lambda e, c0=c0: e.dma_start(out=Wg[:, c0:c0 + 2, :], in_=wgv[:, c0:c0 + 2, :]),
                           writes=["Wg%d" % c0], dma="wg")
                wgk = ["Wg%d" % c0 for c0 in range(0, 8, 2)]
                sc.add("pool", lambda e: e.dma_start(out=Wud[:], in_=w_ud.rearrange("(c p) n -> p c n", p=128)),
                       writes=["Wud"], dma="wud")
                sc.add("pool", lambda e: e.dma_start(out=Wus[:], in_=w_us.rearrange("(c p) n -> p c n", p=128)),
                       writes=["Wus"], dma="wus")
                wov = w_o.rearrange("(c p) n -> p c n", p=128)
                for c0 in range(0, 8, 4):
                    sc.add("pool", lambda e, c0=c0: e.dma_start(out=Wo[:, c0:c0 + 4, :], in_=wov[:, c0:c0 + 4, :]),
                           writes=["Wo%d" % c0], dma="wo")
                wok = ["Wo0", "Wo4"]
                jq_cache = {}

                def get_jq(e):
                    if "jq" not in jq_cache:
                        pid = e.partition_id()
                        jq_cache["jq"] = bass.ds(pid % 4, 1)
                    return jq_cache["jq"]

                omv = o_mine.ap()

                def mk_om(r):
                    def f(e):
                        jq = get_jq(e)
                        return e.dma_start(out=omv[r * 192:(r + 1) * 192, :].rearrange("(o f) t -> o f t", o=1),
                                           in_=o_all.ap()[jq, r * 192:(r + 1) * 192, :])
                    return f

                for r in range(4):
                    sc.add("pool", mk_om(r), reads=oallk, writes=["o_mine%d" % r], dma="omine")

                def mk_oa(cc, hh, T):
                    def f(e):
                        r = 2 * cc + hh
                        return e.dma_start(out=oA[hh * 64:(hh + 1) * 64, cc, :],
                                           in_=omv[r * 192:r * 192 + 64, T * 512:(T + 1) * 512])
                    return f

                def mk_ob(cc, T):
                    def f(e):
                        return e.dma_start(out=oB[:, cc, :], in_=omv[cc * 192 + 64:cc * 192 + 192, T * 512:(T + 1) * 512])
                    return f

                for T in range(4):
                    tsl = slice(T * 512, (T + 1) * 512)
                    for cc in range(2):
                        for hh in range(2):
                            sc.add("sp", mk_oa(cc, hh, T), reads=["o_mine%d" % r_ for r_ in range(4)], writes=["oA%d" % (2 * cc + hh)], dma="oAg")
                    for cc in range(4):
                        sc.add("sp", mk_ob(cc, T), reads=["o_mine%d" % r_ for r_ in range(4)], writes=["oB%d" % cc], dma="oBg")
                    rmsnorm_T([(xr[:, 4 * T + tt, :], "xr%d" % T) for tt in range(4)], 4,
                              lambda c: (hT1[:, c, :], "hT1"), g1, None, (ss, lnv, rstd, junk, xn), psT, "C")
                    for m in range(8):
                        msl = slice(m * 128, (m + 1) * 128)
                        pga, pgak = nextps()
                        for c in range(8):
                            sc.add("pe", lambda e, c=c, pga=pga, msl=msl: e.matmul(pga[:, :], lhsT=Wg[:, c, msl], rhs=hT1[:, c, :],
                                                                                  start=(c == 0), stop=(c == 7)),
                                   reads=wgk + ["hT1"], writes=[pgak])
                        pgb, pgbk = nextps()
                        msl2 = slice(D + m * 128, D + (m + 1) * 128)
                        for c in range(8):
                            sc.add("pe", lambda e, c=c, pgb=pgb, msl2=msl2: e.matmul(pgb[:, :], lhsT=Wg[:, c, msl2], rhs=hT1[:, c, :],
                                                                                    start=(c == 0), stop=(c == 7)),
                                   reads=wgk + ["hT1"], writes=[pgbk])
                        pua, puak = nextps()
                        for c in range(2):
                            sc.add("pe", lambda e, c=c, pua=pua, msl=msl, tsl=tsl: e.matmul(pua[:, :], lhsT=Wud[:, c, msl],
                                                                                           rhs=oA[:, c, :],
                                                                                           start=(c == 0), stop=(c == 1)),
                                   reads=["Wud", "oA0", "oA1", "oA2", "oA3"], writes=[puak])
                        pub, pubk = nextps()
                        for c in range(4):
                            sc.add("pe", lambda e, c=c, pub=pub, msl=msl, tsl=tsl: e.matmul(pub[:, :], lhsT=Wus[:, c, msl],
                                                                                           rhs=oB[:, c, :],
                                                                                           start=(c == 0), stop=(c == 3)),
                                   reads=["Wus", "oB0", "oB1", "oB2", "oB3"], writes=[pubk])
                        sc.add("act", lambda e, pga=pga, m=m: e.activation(out=Ga[:], in_=pga[:, :], func=AF.Sigmoid,
                                                                           bias=bg[:, m:m + 1]),
                               reads=[pgak, "vec"], writes=["Ga"])
                        sc.add("act", lambda e, pgb=pgb, m=m: e.activation(out=Gb[:], in_=pgb[:, :], func=AF.Sigmoid,
                                                                           bias=bg[:, 8 + m:9 + m]),
                               reads=[pgbk, "vec"], writes=["Gb"])
                        sc.add("dve", lambda e, pua=pua: e.tensor_tensor(out=t1[:], in0=Ga[:], in1=pua[:, :], op=ALU.mult),
                               reads=["Ga", puak], writes=["t1"])
                        sc.add("dve", lambda e, pub=pub: e.tensor_tensor(out=t2[:], in0=Gb[:], in1=pub[:, :], op=ALU.mult),
                               reads=["Gb", pubk], writes=["t2"])
                        sc.add("pool", lambda e, m=m: e.tensor_tensor(out=mg[:, m, :], in0=t1[:], in1=t2[:], op=ALU.add),
                               reads=["t1", "t2"], writes=["mg%d" % m])
                    mgk = ["mg%d" % m for m in range(8)]
                    for tb in range(4):
                        for half in range(2):
                            py, pyk = nextps()
                            hsl = slice(half * 512, (half + 1) * 512)
                            for c in range(8):
                                sc.add("pe", lambda e, c=c, py=py, tb=tb, hsl=hsl: e.matmul(
                                    py[:, :], lhsT=mg[:, c, tb * 128:(tb + 1) * 128], rhs=Wo[:, c, hsl],
                                    start=(c == 0), stop=(c == 7)), reads=mgk + wok, writes=[pyk])
                            xa = xr[:, 4 * T + tb, hsl]
                            sc.add("dve", lambda e, xa=xa, py=py: e.tensor_tensor(out=xa, in0=xa, in1=py[:, :], op=ALU.add),
                                   reads=[pyk, "xr%d" % T], writes=["xr%d" % T])
                    rmsnorm_T([(xr[:, 4 * T + tt, :], "xr%d" % T) for tt in range(4)], 4,
                              lambda c, tsl=tsl: (hT2[:, c, tsl], "hT2_%d" % T), g2, None,
                              (ss, lnv, rstd, junk, xn), psT, "C")
            sc.barrier()
            C2 = contextlib.ExitStack()
            with C2:
                gfb = sb("gfb", [128, D], F32, C2)
                sc.add("sp", lambda e: e.dma_start(out=gfb[:], in_=gfin[:, :]), writes=["gfb"], dma="c3")
                W1q = [sb("W1q%d" % i, [128, 8, D], BF16, C2) for i in range(2)]
                W2q = [sb("W2q%d" % i, [128, 8, D], BF16, C2) for i in range(2)]
                uT = [sb("uT%d" % i, [128, 8, 512], BF16, C2) for i in range(2)]
                rl = [sb("rl%d" % i, [128, 512], F32, C2) for i in range(2)]
                yo = [sb("yo%d" % i, [128, D], F32, C2) for i in range(2)]
                w1v = w_1.rearrange("(c p) n -> p c n", p=128)
                w2v = w_2.rearrange("(c p) n -> p c n", p=128)
                it = 0
                rc = 0
                for qf in range(4):
                    s_ = qf % 2
                    sc.add("pool", lambda e, qf=qf, s_=s_: e.dma_start(out=W1q[s_][:], in_=w1v[:, :, qf * D:(qf + 1) * D]),
                           writes=["W1q%d" % s_], dma="w1q%d" % s_)
                    sc.add("pool", lambda e, qf=qf, s_=s_: e.dma_start(out=W2q[s_][:], in_=w2v[:, 8 * qf:8 * qf + 8, :]),
                           writes=["W2q%d" % s_], dma="w2q%d" % s_)
                    for T in range(4):
                        tsl = slice(T * 512, (T + 1) * 512)
                        u = uT[it % 2]
                        uk = "uT%d" % (it % 2)
                        it += 1
                        for m in range(8):
                            pu, puk = nextps()
                            for c in range(8):
                                sc.add("pe", lambda e, c=c, pu=pu, m=m, s_=s_, tsl=tsl: e.matmul(
                                    pu[:, :], lhsT=W1q[s_][:, c, m * 128:(m + 1) * 128], rhs=hT2[:, c, tsl],
                                    start=(c == 0), stop=(c == 7)), reads=["W1q%d" % s_, "hT2_%d" % T], writes=[puk])
                            r_ = rl[rc % 2]
                            rk = "rl%d" % (rc % 2)
                            rc += 1
                            sc.add("act", lambda e, r_=r_, pu=pu: e.activation(out=r_[:], in_=pu[:, :], func=AF.Relu),
                                   reads=[puk], writes=[rk])
                            sc.add("pool", lambda e, r_=r_, u=u, m=m: e.tensor_tensor(out=u[:, m, :], in0=r_[:], in1=r_[:],
                                                                                    op=ALU.mult),
                                   reads=[rk], writes=[uk + "_%d" % m])
                        uks = [uk + "_%d" % m for m in range(8)]
                        for tb in range(4):
                            for half in range(2):
                                py, pyk = nextps()
                                hsl = slice(half * 512, (half + 1) * 512)
                                for m in range(8):
                                    sc.add("pe", lambda e, m=m, py=py, u=u, tb=tb, hsl=hsl, s_=s_: e.matmul(
                                        py[:, :], lhsT=u[:, m, tb * 128:(tb + 1) * 128], rhs=W2q[s_][:, m, hsl],
                                        start=(m == 0), stop=(m == 7)), reads=uks + ["W2q%d" % s_], writes=[pyk])
                                xa = xr[:, 4 * T + tb, hsl]
                                sc.add("dve", lambda e, xa=xa, py=py: e.tensor_tensor(out=xa, in0=xa, in1=py[:, :], op=ALU.add),
                                       reads=[pyk, "xr%d" % T], writes=["xr%d" % T])
                ov = out.rearrange("(tt p) d -> p tt d", p=128)
                for T in range(4):
                    for tt in range(4):
                        sc.add("act", lambda e, T=T, tt=tt: e.activation(out=junk[:], in_=xr[:, 4 * T + tt, :], func=AF.Square,
                                                                         accum_out=ss[:, tt:tt + 1]),
                               reads=["xr%d" % T], writes=["Fjunk", "Fss%d" % tt])
                    sc.add("act", lambda e: e.activation(out=lnv[:, 0:4], in_=ss[:, 0:4], func=AF.Ln, scale=1.0 / D,
                                                         bias=eps_sb[:, 0:1]),
                           reads=["Fss%d" % tt for tt in range(4)] + ["eps"], writes=["Flnv"])
                    sc.add("act", lambda e: e.activation(out=rstd[:, 0:4], in_=lnv[:, 0:4], func=AF.Exp, scale=-0.5),
                           reads=["Flnv"], writes=["Frstd"])
                    for tt in range(4):
                        y = yo[tt % 2]
                        yk = "yo%d" % (tt % 2)
                        sc.add("dve", lambda e, T=T, tt=tt, y=y: e.scalar_tensor_tensor(
                            out=y[:], in0=xr[:, 4 * T + tt, :], scalar=rstd[:, tt:tt + 1], in1=gfb[:],
                            op0=ALU.mult, op1=ALU.mult), reads=["xr%d" % T, "Frstd", "gfb"], writes=[yk])
                        sc.add("sp", lambda e, T=T, tt=tt, y=y: e.dma_start(out=ov[:, 4 * T + tt, :], in_=y[:]),
                               reads=[yk], writes=["out"], dma=yk)

        semstack = contextlib.ExitStack()
        with semstack:
            sc.prepare(nc, semstack)
            with nc.Block() as block:
                sc.emit(nc, block)
    return nc


def _constants(j):
    bf = ml_dtypes.bfloat16
    ident = np.eye(128, dtype=np.float32)
    jj = np.arange(128)[:, None]
    kk = np.arange(128)[None, :]
    negtri = np.where(jj >= kk, -1.0, 0.0).astype(np.float32)
    negones = -np.ones((128, 128), np.float32)
    cmat = np.concatenate([ident, negtri, negones], axis=1).astype(bf)
    p = np.arange(128)[:, None]
    c = np.arange(512)[None, :]
    cm = np.concatenate([(128 * u + p < c).astype(np.float32) for u in range(4)], axis=1).astype(bf)
    kq = np.arange(128)[:, None].astype(np.float64)
    qq = np.arange(128)[None, :].astype(np.float64)
    dms = []
    for g, d in enumerate((1, 4, 16)):
        slope = 2.0 ** (-8.0 * (4 * g + j + 1) / 12.0)
        sp = qq + 128 - kq
        prev = np.where(sp <= 128, np.exp(-slope * d * sp), 0.0)
        scur = qq - kq
        cur = np.where(scur >= 0, np.exp(-slope * d * scur), 0.0)
        dms.append(np.concatenate([prev, cur], axis=1))
    dmask = np.concatenate(dms, axis=1).astype(np.float32)
    sel = np.zeros((128, 64), np.float32)
    sel[64, :] = 1.0
    return cmat, cm, dmask, sel


def _own_cols(j):
    def dq(g):
        return (4 * g + j) * 64
    cols = {}
    qa, ka, va = 0, 768, 1536
    qb, kb, vb = 2304, 2816, 3328
    s0, s1 = 2 * j, 2 * j + 1
    r = lambda o: list(range(o, o + 64))
    cols["QA"] = r(qa + dq(0)) + r(qa + dq(1))
    cols["KA"] = r(ka + dq(0)) + r(ka + dq(1))
    cols["VA"] = r(va + dq(0)) + r(va + dq(1))
    cols["QB"] = r(qa + dq(2)) + r(qb + s0 * 64)
    cols["KB"] = r(ka + dq(2)) + r(kb + s0 * 64)
    cols["VS"] = r(vb + s0 * 64) + r(vb + s1 * 64)
    cols["QC"] = r(qb + s1 * 64)
    cols["KC"] = r(kb + s1 * 64)
    cols["VG"] = r(va + dq(2))
    idx = []
    for n in OWN_TILES:
        idx += cols[n]
    return np.array(idx)


_NC_CACHE = {}


def kernel(x, norm_mix_g, w_in, b_gate, w_up_dil, w_up_sb, w_out, norm_mlp_g, w_mlp_in, w_mlp_out, norm_final_g,
           _debug=False, _sb_limit=None, _phases=9):
    x = np.asarray(x, np.float32)
    w_in0 = np.asarray(w_in, np.float32)[0]
    key = (_debug, _sb_limit, _phases)
    if key not in _NC_CACHE:
        _NC_CACHE[key] = build_nc(debug=_debug, sb_limit=_sb_limit, phases=_phases)
    nc = _NC_CACHE[key]
    vecs = np.concatenate([
        np.asarray(norm_mix_g, np.float32)[0].reshape(8, 128).T,
        np.asarray(norm_mlp_g, np.float32)[0].reshape(8, 128).T,
        np.asarray(b_gate, np.float32)[0].reshape(16, 128).T], axis=1)
    vecs = np.ascontiguousarray(vecs)
    gfin = np.ascontiguousarray(np.broadcast_to(np.asarray(norm_final_g, np.float32)[None, :], (128, D)))
    w_gate = np.ascontiguousarray(w_in0[:, 3840:5888])
    shared = {
        "w_gate": w_gate,
        "w_ud": np.ascontiguousarray(np.asarray(w_up_dil, np.float32)[0]),
        "w_us": np.ascontiguousarray(np.asarray(w_up_sb, np.float32)[0]),
        "w_o": np.ascontiguousarray(np.asarray(w_out, np.float32)[0]),
        "w_1": np.ascontiguousarray(np.asarray(w_mlp_in, np.float32)[0]),
        "w_2": np.ascontiguousarray(np.asarray(w_mlp_out, np.float32)[0]),
        "vecs": vecs, "gfin": gfin,
    }
    in_maps = []
    for c in range(NCORES):
        b, j = c // 4, c % 4
        cmat, cm, dmask, sel = _constants(j)
        m = dict(shared)
        m["x_full"] = np.ascontiguousarray(x[b])
        m["x_own"] = np.ascontiguousarray(x[b, j * TOK_OWN:(j + 1) * TOK_OWN])
        m["w_own"] = np.ascontiguousarray(w_in0[:, _own_cols(j)])
        m["cmat"] = cmat
        m["cmask"] = cm
        m["dmask"] = dmask
        m["selm"] = sel
        in_maps.append(m)
    res = run_bass_kernel_spmd(nc, in_maps, core_ids=list(range(NCORES)))
    outp = np.empty((2, S, D), np.float32)
    for c in range(NCORES):
        b, j = c // 4, c % 4
        outp[b, j * TOK_OWN:(j + 1) * TOK_OWN] = np.asarray(res.results[c]["y_out"], np.float32)
    if _debug:
        return outp, [np.asarray(res.results[c]["dbg"]) for c in range(NCORES)]
    return outp
```

```python
import numpy as np
import ml_dtypes
import concourse.bass as bass
import concourse.mybir as mybir
from concourse.bass_utils import run_bass_kernel_spmd

F32 = mybir.dt.float32
BF16 = mybir.dt.bfloat16
AF = mybir.ActivationFunctionType
ALU = mybir.AluOpType

S = 8192
D = 1024
NCORES = 8
TOK_OWN = 2048
EPS = 1e-6
SEM_CAP = 2000
XMOD = 64

OWN_TILES = ["QA", "KA", "VA", "QB", "KB", "VS", "QC", "KC", "VG"]
OWN_W = {"QA": 128, "KA": 128, "VA": 128, "QB": 128, "KB": 128, "VS": 128, "QC": 64, "KC": 64, "VG": 64}
OWN_OFF = {}
_o = 0
for _n in OWN_TILES:
    OWN_OFF[_n] = _o
    _o += OWN_W[_n]
OWN_COLS = _o


class Op:
    __slots__ = ("eng", "fn", "dma", "deps", "signal", "num", "idx", "inc")

    def __init__(self, eng, fn, dma, inc=16):
        self.eng, self.fn, self.dma = eng, fn, dma
        self.inc = inc
        self.deps = []
        self.signal = False
        self.num = None
        self.idx = None


class Sched:
    def __init__(self):
        self.ops = []
        self.lastw = {}
        self.readers = {}
        self.floor = []
        self.last_by_src = {}
        self.enabled = True
        import os as _os
        self.maxops = int(_os.environ.get("KMAXOPS", "100000000"))

    @staticmethod
    def _src(op):
        return ("dma", op.dma) if op.dma is not None else ("eng", op.eng)

    def add(self, eng, fn, reads=(), writes=(), dma=None, inc=16):
        op = Op(eng, fn, dma, inc)
        if not self.enabled or len(self.ops) >= self.maxops:
            return op
        op.idx = len(self.ops)
        deps = {}
        for d in self.floor:
            deps[id(d)] = d
        for k in reads:
            w = self.lastw.get(k)
            if w is not None:
                deps[id(w)] = w
        for k in writes:
            w = self.lastw.get(k)
            if w is not None:
                deps[id(w)] = w
            for r in self.readers.get(k, {}).values():
                deps[id(r)] = r
        for k in reads:
            self.readers.setdefault(k, {})[self._src(op)] = op
        for k in writes:
            self.lastw[k] = op
            self.readers[k] = {}
        out = []
        for d in deps.values():
            if d is op:
                continue
            if d.dma is None and d.eng == "pe" and eng == "pe" and dma is None:
                continue
            out.append(d)
        op.deps = out
        self.ops.append(op)
        self.last_by_src[self._src(op)] = op
        return op

    def barrier(self):
        self.floor = list(self.last_by_src.values())
        self.lastw = {}
        self.readers = {}

    def prepare(self, nc, semstack):
        for op in self.ops:
            for d in op.deps:
                d.signal = True
        for d in self.last_by_src.values():
            d.signal = True
        counters = {}
        for op in self.ops:
            if op.dma is not None:
                k = ("dma", op.dma)
                counters[k] = counters.get(k, 0) + 1
                op.num = counters[k]
            elif op.signal:
                k = ("eng", op.eng)
                counters[k] = counters.get(k, 0) + 1
                op.num = counters[k]
        sems = {}
        for k, n in counters.items():
            if k[0] == "dma":
                sems[k] = [semstack.enter_context(nc.semaphore("d_%s" % str(k[1])))]
            else:
                ns = (n + SEM_CAP - 1) // SEM_CAP
                sems[k] = [semstack.enter_context(nc.semaphore("e_%s_%d" % (k[1], i))) for i in range(ns)]
        self.sems = sems

    def emit(self, nc, block):
        sems = self.sems

        def semval(op):
            k = Sched._src(op)
            if k[0] == "dma":
                return sems[k][0], op.num * op.inc
            n = op.num - 1
            return sems[k][n // SEM_CAP], n % SEM_CAP + 1

        def run(engname, e):
            waited = {}
            for op in self.ops:
                if op.eng != engname:
                    continue
                need = {}
                for d in op.deps:
                    k = Sched._src(d)
                    if d.num > need.get(k, (0, None))[0]:
                        need[k] = (d.num, d)
                for k, (n, d) in need.items():
                    if waited.get(k, 0) >= n:
                        continue
                    waited[k] = n
                    s, v = semval(d)
                    e.wait_ge(s, v)
                ins = op.fn(e)
                if op.dma is not None:
                    s, _ = semval(op)
                    ins.then_inc(s, op.inc)
                elif op.signal:
                    s, _ = semval(op)
                    ins.then_inc(s, 1)

        final = [op for op in self.last_by_src.values()]

        def runfinal(e):
            for d in final:
                s, v = semval(d) if d.num is not None else (None, None)
                if s is not None:
                    e.wait_ge(s, v)

        @block.tensor
        def _(e):
            run("pe", e)

        @block.scalar
        def _(e):
            run("act", e)

        @block.vector
        def _(e):
            run("dve", e)

        @block.gpsimd
        def _(e):
            run("pool", e)

        @block.sync
        def _(e):
            run("sp", e)
            runfinal(e)


def build_nc(debug=False, sb_limit=None, phases=9, ntA=16, tlist=None, lite=False):
    import contextlib

    nc = bass.Bass("TRN2", target_bir_lowering=False)
    dt = nc.dram_tensor
    x_full = dt("x_full", [S, D], F32, kind="ExternalInput").ap()
    x_own = dt("x_own", [TOK_OWN, D], F32, kind="ExternalInput").ap()
    w_own = dt("w_own", [D, OWN_COLS], F32, kind="ExternalInput").ap()
    if lite:
        _real_dt = dt

        def dt(name, shape, dtype, kind=None):
            if kind == "ExternalInput" and name in ("w_gate", "w_ud", "w_us", "w_o", "w_1", "w_2"):
                shape = [128, 8]
            return _real_dt(name, shape, dtype, kind=kind) if kind else _real_dt(name, shape, dtype)
    w_gate = dt("w_gate", [D, 2 * D], F32, kind="ExternalInput").ap()
    w_ud = dt("w_ud", [256, D], F32, kind="ExternalInput").ap()
    w_us = dt("w_us", [512, D], F32, kind="ExternalInput").ap()
    w_o = dt("w_o", [D, D], F32, kind="ExternalInput").ap()
    w_1 = dt("w_1", [D, 4 * D], F32, kind="ExternalInput").ap()
    w_2 = dt("w_2", [4 * D, D], F32, kind="ExternalInput").ap()
    vecs = dt("vecs", [128, 32], F32, kind="ExternalInput").ap()
    gfin = dt("gfin", [128, D], F32, kind="ExternalInput").ap()
    cmask = dt("cmask", [128, 4 * 512], BF16, kind="ExternalInput").ap()
    cmat = dt("cmat", [128, 3 * 128], BF16, kind="ExternalInput").ap()
    dmask = dt("dmask", [128, 3 * 256], F32, kind="ExternalInput").ap()
    selm = dt("selm", [128, 64], F32, kind="ExternalInput").ap()
    out = dt("y_out", [TOK_OWN, D], F32, kind="ExternalOutput").ap()
    o_locq = [dt("o_loc%d" % q, [192, TOK_OWN], BF16) for q in range(4)]
    o_all = dt("o_all", [4, 4 * 192, TOK_OWN], BF16)
    o_mine = dt("o_mine", [4 * 192, TOK_OWN], BF16)
    if debug:
        dbg = dt("dbg", [4 * 192, S], BF16, kind="ExternalOutput").ap()

    sc = Sched()
    es = contextlib.ExitStack()
    with es:
        def sb(name, shape, dtype, stack=es):
            return stack.enter_context(nc.sbuf_tensor(name, shape, dtype))

        def ps(name, shape, dtype, stack=es):
            return stack.enter_context(nc.psum_tensor(name, shape, dtype))

        vec_sb = sb("vec_sb", [128, 32], F32)
        cmat_sb = sb("cmat_sb", [128, 384], BF16)
        ident = cmat_sb[:, 0:128]
        negtri = cmat_sb[:, 128:256]
        negones = cmat_sb[:, 256:384]
        sc.add("sp", lambda e: e.dma_start(out=vec_sb[:], in_=vecs[:, :]), writes=["vec"], dma="c0a")
        sc.add("sp", lambda e: e.dma_start(out=cmat_sb[:], in_=cmat[:, :]), writes=["cmat"], dma="c0b")
        g1 = vec_sb[:, 0:8]
        g2 = vec_sb[:, 8:16]
        bg = vec_sb[:, 16:32]

        def rmsnorm_T(xsrc_tiles, ntt, hT_dst, gain, keyp, scratch, psT, tag, part=None):
            ss, lnv, rstd, junk, xn = scratch
            for tt, (xa, rk) in enumerate(xsrc_tiles if part in (None, "stats") else []):
                sc.add("act", lambda e, xa=xa, tt=tt: e.activation(out=junk[:], in_=xa, func=AF.Square,
                                                                   accum_out=ss[:, tt:tt + 1]),
                       reads=[rk], writes=[tag + "junk", tag + "ss%d" % tt])
            sskeys = [tag + "ss%d" % tt for tt in range(ntt)]
            if part in (None, "stats"):
                sc.add("act", lambda e: e.activation(out=lnv[:, 0:ntt], in_=ss[:, 0:ntt], func=AF.Ln,
                                                     scale=1.0 / D, bias=eps_sb[:, 0:1]),
                       reads=sskeys + ["eps"], writes=[tag + "lnv"])
                sc.add("act", lambda e: e.activation(out=rstd[:, 0:ntt], in_=lnv[:, 0:ntt], func=AF.Exp, scale=-0.5),
                       reads=[tag + "lnv"], writes=[tag + "rstd"])
            for tt, (xa, rk) in enumerate(xsrc_tiles if part in (None, "stats") else []):
                if tt % 2 == 0:
                    sc.add("dve", lambda e, xa=xa, tt=tt: e.tensor_scalar(out=xn[:, tt, :], in0=xa,
                                                                         scalar1=rstd[:, tt:tt + 1], scalar2=None,
                                                                         op0=ALU.mult),
                           reads=[rk, tag + "rstd"], writes=[tag + "xn%d" % tt])
                else:
                    sc.add("act", lambda e, xa=xa, tt=tt: e.activation(out=xn[:, tt, :], in_=xa, func=AF.Copy,
                                                                       scale=rstd[:, tt:tt + 1]),
                           reads=[rk, tag + "rstd"], writes=[tag + "xn%d" % tt])
            for c in range(8 if part in (None, "trans") else 0):
                pb = psT[c % 2]
                pk = tag + "psT%d" % (c % 2)
                for tt in range(ntt):
                    sc.add("pe", lambda e, pb=pb, tt=tt, c=c: e.transpose(out=pb[:, tt * 128:(tt + 1) * 128],
                                                                         in_=xn[:, tt, c * 128:(c + 1) * 128],
                                                                         identity=ident),
                           reads=[tag + "xn%d" % tt, "cmat"], writes=[pk])
                dst, dk = hT_dst(c)
                if c % 2 == 0:
                    sc.add("act", lambda e, pb=pb, dst=dst, c=c: e.activation(out=dst, in_=pb[:, 0:ntt * 128],
                                                                              func=AF.Copy, scale=gain[:, c:c + 1]),
                           reads=[pk, "vec"], writes=[dk])
                else:
                    sc.add("dve", lambda e, pb=pb, dst=dst, c=c: e.tensor_scalar(out=dst, in0=pb[:, 0:ntt * 128],
                                                                                 scalar1=gain[:, c:c + 1],
                                                                                 scalar2=None, op0=ALU.mult),
                           reads=[pk, "vec"], writes=[dk])

        eps_sb = sb("eps_sb", [128, 1], F32)
        sc.add("dve", lambda e: e.memset(eps_sb[:], EPS), writes=["eps"])

        def mk_gather(q):
            def f(e):
                return e.collective_compute("AllGather", ALU.bypass, replica_groups=[[0, 1, 2, 3], [4, 5, 6, 7]],
                                            ins=[o_locq[q].ap().opt()], outs=[o_all.ap()[q].opt()])
            return f

        def add_gather(q):
            olk = ["o_loc_a%d" % ch for ch in range(4 * q, 4 * q + 4)] + \
                  ["o_loc_b%d_%d" % (orow, qt) for orow in (64, 128) for qt in range(4 * q, 4 * q + 4)]
            sc.add("pool", mk_gather(q), reads=olk, writes=["o_all%d" % q], dma="cc", inc=1)

        AB = contextlib.ExitStack()
        with AB:
            QB = sb("QB", [128, S], BF16, AB)
            KB = sb("KB", [128, S], BF16, AB)
            QC = sb("QC", [64, S], BF16, AB)
            KC = sb("KC", [64, S], BF16, AB)
            Vs = sb("Vs", [128, 64, 128], BF16, AB)
            DIL = contextlib.ExitStack()
            with DIL:
                QA = sb("QA", [128, S], BF16, DIL)
                KA = sb("KA", [128, S], BF16, DIL)
                Vd = [sb("Vd%d" % g, [128, 64, 66], BF16, DIL) for g in range(3)]
                for g in range(3):
                    sc.add("pool", lambda e, g=g: e.memset(Vd[g][:, :, 64:65], 1.0), writes=["Vd%d_ones" % g])

                PA = contextlib.ExitStack()
                with PA:
                    Wown = sb("Wown", [128, 8, OWN_COLS], BF16, PA)
                    xs = [sb("xs%d" % i, [128, 2, D], F32, PA) for i in range(2)]
                    xn = sb("xnA", [128, 2, D], BF16, PA)
                    hT = [sb("hTA%d" % i, [128, 8, 512], BF16, PA) for i in range(2)]
                    junk = sb("junkA", [128, D], BF16, PA)
                    ss = sb("ssA", [128, 4], F32, PA)
                    lnv = sb("lnvA", [128, 4], F32, PA)
                    rstd = sb("rstdA", [128, 4], F32, PA)
                    vtA = sb("vtA", [128, 512], BF16, PA)
                    vtS = sb("vtS", [128, 512], BF16, PA)
                    vt3 = sb("vt3", [128, 2048], BF16, PA)
                    sc.add("dve", lambda e: e.memset(vt3[64:128, :], 0.0), writes=["vt3"])
                    psT = [ps("psTA%d" % i, [128, 1024], BF16, PA) for i in range(2)]
                    psP = [ps("psPA%d" % i, [128, 512], F32, PA) for i in range(3)]
                    psV = [ps("psVA%d" % i, [128, 1024], BF16, PA) for i in range(2)]

                    wv = w_own.rearrange("(c p) n -> p c n", p=128)
                    for c0 in range(0, 8, 2):
                        sc.add("pool", lambda e, c0=c0: e.dma_start(out=Wown[:, c0:c0 + 2, :], in_=wv[:, c0:c0 + 2, :]),
                               writes=["Wown%d" % c0], dma="wown")
                    wkeys = ["Wown%d" % c0 for c0 in range(0, 8, 2)]
                    xv = x_full.rearrange("(n tt p) d -> n p tt d", tt=2, p=128)

                    pcount = [0]

                    def proj_tile(t, name, hTt, hk):
                        w = OWN_W[name]
                        off = OWN_OFF[name]
                        i = pcount[0] % 3
                        pcount[0] += 1
                        pp = psP[i]
                        pk = "psPA%d" % i
                        for c in range(8):
                            sc.add("pe", lambda e, c=c, pp=pp: e.matmul(pp[0:w, :], lhsT=Wown[:, c, off:off + w],
                                                                       rhs=hTt[:, c, :], start=(c == 0), stop=(c == 7)),
                                   reads=wkeys + [hk], writes=[pk])
                        return pp, pk

                    def deint(ap_rows, d):
                        return ap_rows.rearrange("p (l r) -> p r l", r=d)

                    def norm_half(ti, t, half, part):
                        hTt = hT[ti % 2]
                        hk = "hT%d" % (ti % 2)
                        n = 2 * t + half
                        xsl = xs[n % 2]
                        xk = "xs%d" % (n % 2)
                        if part == "stats":
                            sc.add("sp", lambda e, xsl=xsl, n=n: e.dma_start(out=xsl[:], in_=xv[n]),
                                   writes=[xk], dma=xk)
                        rmsnorm_T([(xsl[:, 0, :], xk), (xsl[:, 1, :], xk)], 2,
                                  lambda c, half=half, hTt=hTt, hk=hk: (hTt[:, c, half * 256:(half + 1) * 256], hk),
                                  g1, None, (ss, lnv, rstd, junk, xn), psT, "A", part=part)

                    def proj_part1(ti, t):
                        hTt = hT[ti % 2]
                        hk = "hT%d" % (ti % 2)
                        tsl = slice(t * 512, (t + 1) * 512)
                        pp, pk = proj_tile(t, "QA", hTt, hk)
                        sc.add("act", lambda e, pp=pp, tsl=tsl: e.activation(out=QA[0:64, tsl], in_=pp[0:64, :],
                                                                             func=AF.Copy, scale=0.125),
                               reads=[pk], writes=["QA"])
                        sc.add("dve", lambda e, pp=pp, t=t: e.tensor_scalar(
                            out=QA[64:128, :].rearrange("p (r n l) -> p r n l", r=4, n=16)[:, :, t, :],
                            in0=deint(pp[64:128, :], 4), scalar1=0.125, scalar2=None, op0=ALU.mult),
                            reads=[pk], writes=["QA"])
                        pp, pk = proj_tile(t, "KA", hTt, hk)
                        sc.add("act", lambda e, pp=pp, tsl=tsl: e.activation(out=KA[0:64, tsl], in_=pp[0:64, :],
                                                                             func=AF.Copy), reads=[pk], writes=["KA"])
                        sc.add("dve", lambda e, pp=pp, t=t: e.tensor_copy(
                            out=KA[64:128, :].rearrange("p (r n l) -> p r n l", r=4, n=16)[:, :, t, :],
                            in_=deint(pp[64:128, :], 4)), reads=[pk], writes=["KA"])
                        pp, pk = proj_tile(t, "VA", hTt, hk)
                        sc.add("act", lambda e, pp=pp: e.activation(out=vtA[0:64, :], in_=pp[0:64, :], func=AF.Copy),
                               reads=[pk], writes=["vtA"])
                        sc.add("dve", lambda e, pp=pp: e.tensor_copy(
                            out=vtA[64:128, :].rearrange("p (r l) -> p r l", r=4), in_=deint(pp[64:128, :], 4)),
                            reads=[pk], writes=["vtA"])
                        pv = psV[0]
                        for bi in range(4):
                            sc.add("pe", lambda e, bi=bi, pv=pv: e.transpose(out=pv[:, bi * 128:(bi + 1) * 128],
                                                                            in_=vtA[:, bi * 128:(bi + 1) * 128],
                                                                            identity=ident),
                                   reads=["vtA", "cmat"], writes=["psVA0"])
                        for bi in range(4):
                            sc.add("act", lambda e, pv=pv, t=t, bi=bi: e.activation(
                                out=Vd[0][:, 4 * t + bi, 0:64], in_=pv[:, bi * 128:bi * 128 + 64], func=AF.Copy),
                                reads=["psVA0"], writes=["Vd0"])
                            sc.add("act", lambda e, pv=pv, t=t, bi=bi: e.activation(
                                out=Vd[1][:, bi * 16 + t, 0:64], in_=pv[:, bi * 128 + 64:bi * 128 + 128], func=AF.Copy),
                                reads=["psVA0"], writes=["Vd1"])

                    def proj_part2(ti, t):
                        hTt = hT[ti % 2]
                        hk = "hT%d" % (ti % 2)
                        tsl = slice(t * 512, (t + 1) * 512)
                        nn, qq = t // 4, t % 4
                        pp, pk = proj_tile(t, "QB", hTt, hk)
                        sc.add("dve", lambda e, pp=pp, nn=nn, qq=qq: e.tensor_scalar(
                            out=QB[0:64, :].rearrange("p (r n i) -> p r n i", r=16, n=4)[:, :, nn, 32 * qq:32 * qq + 32],
                            in0=deint(pp[0:64, :], 16), scalar1=0.125, scalar2=None, op0=ALU.mult),
                            reads=[pk], writes=["QB"])
                        sc.add("act", lambda e, pp=pp, tsl=tsl: e.activation(out=QB[64:128, tsl], in_=pp[64:128, :],
                                                                             func=AF.Copy, scale=0.125),
                               reads=[pk], writes=["QB"])
                        pp, pk = proj_tile(t, "KB", hTt, hk)
                        sc.add("dve", lambda e, pp=pp, nn=nn, qq=qq: e.tensor_copy(
                            out=KB[0:64, :].rearrange("p (r n i) -> p r n i", r=16, n=4)[:, :, nn, 32 * qq:32 * qq + 32],
                            in_=deint(pp[0:64, :], 16)), reads=[pk], writes=["KB"])
                        sc.add("act", lambda e, pp=pp, tsl=tsl: e.activation(out=KB[64:128, tsl], in_=pp[64:128, :],
                                                                             func=AF.Copy), reads=[pk], writes=["KB"])
                        pp, pk = proj_tile(t, "VS", hTt, hk)
                        sc.add("act", lambda e, pp=pp: e.activation(out=vtS[:, :], in_=pp[:, :], func=AF.Copy),
                               reads=[pk], writes=["vtS"])
                        pv = psV[1]
                        for bi in range(4):
                            sc.add("pe", lambda e, bi=bi, pv=pv: e.transpose(out=pv[:, bi * 128:(bi + 1) * 128],
                                                                            in_=vtS[:, bi * 128:(bi + 1) * 128],
                                                                            identity=ident),
                                   reads=["vtS", "cmat"], writes=["psVA1"])
                        sc.add("dve", lambda e, pv=pv, t=t: e.tensor_copy(
                            out=Vs[:, 4 * t:4 * t + 4, :].rearrange("p b d -> p (b d)"), in_=pv[:, 0:512]),
                            reads=["psVA1"], writes=["Vs"])
                        pp, pk = proj_tile(t, "QC", hTt, hk)
                        sc.add("act", lambda e, pp=pp, tsl=tsl: e.activation(out=QC[0:64, tsl], in_=pp[0:64, :],
                                                                             func=AF.Copy, scale=0.125),
                               reads=[pk], writes=["QC"])
                        pp, pk = proj_tile(t, "KC", hTt, hk)
                        sc.add("dve", lambda e, pp=pp, tsl=tsl: e.tensor_copy(out=KC[0:64, tsl], in_=pp[0:64, :]),
                               reads=[pk], writes=["KC"])
                        pp, pk = proj_tile(t, "VG", hTt, hk)
                        sc.add("act", lambda e, pp=pp, qq=qq: e.activation(
                            out=vt3[0:64, :].rearrange("p (r i) -> p r i", r=16)[:, :, 32 * qq:32 * qq + 32],
                            in_=deint(pp[0:64, :], 16), func=AF.Copy), reads=[pk], writes=["vt3"])
                        if qq == 3:
                            for r in range(16):
                                pv = psV[r // 8]
                                pvk = "psVA%d" % (r // 8)
                                rr = r % 8
                                sc.add("pe", lambda e, r=r, rr=rr, pv=pv: e.transpose(out=pv[:, rr * 128:(rr + 1) * 128],
                                                                                      in_=vt3[:, r * 128:(r + 1) * 128],
                                                                                      identity=ident),
                                       reads=["vt3", "cmat"], writes=[pvk])
                            for r in range(16):
                                pv = psV[r // 8]
                                pvk = "psVA%d" % (r // 8)
                                rr = r % 8
                                if False:
                                    sc.add("act", lambda e, pv=pv, nn=nn, r=r, rr=rr: e.activation(
                                        out=Vd[2][:, r * 4 + nn, 0:64], in_=pv[:, rr * 128:rr * 128 + 64], func=AF.Copy),
                                        reads=[pvk], writes=["Vd2"])
                                else:
                                    sc.add("dve", lambda e, pv=pv, nn=nn, r=r, rr=rr: e.tensor_copy(
                                        out=Vd[2][:, r * 4 + nn, 0:64], in_=pv[:, rr * 128:rr * 128 + 64]),
                                        reads=[pvk], writes=["Vd2"])

                    tl_ = list(tlist if tlist is not None else range(ntA))
                    for half in range(2):
                        norm_half(0, tl_[0], half, "stats")
                        norm_half(0, tl_[0], half, "trans")
                    for ti, t in enumerate(tl_):
                        nxt = ti + 1 < len(tl_)
                        if nxt:
                            norm_half(ti + 1, tl_[ti + 1], 0, "stats")
                        proj_part1(ti, t)
                        if nxt:
                            norm_half(ti + 1, tl_[ti + 1], 0, "trans")
                            norm_half(ti + 1, tl_[ti + 1], 1, "stats")
                        proj_part2(ti, t)
                        if nxt:
                            norm_half(ti + 1, tl_[ti + 1], 1, "trans")
                sc.barrier()
                if phases < 2:
                    sc.enabled = False

                PD = contextlib.ExitStack()
                with PD:
                    acc = sb("acc", [65, S], F32, PD)
                    dm = sb("dm", [128, 768], F32, PD)
                    sel_sb = sb("sel_sb", [128, 64], F32, PD)
                    sc.add("sp", lambda e: e.dma_start(out=dm[:], in_=dmask[:, :]), writes=["dm"], dma="c1a")
                    sc.add("sp", lambda e: e.dma_start(out=sel_sb[:], in_=selm[:, :]), writes=["sel"], dma="c1b")
                    pex = [sb("pex%d" % i, [128, 512], F32, PD) for i in range(2)]
                    pbf = [sb("pbf%d" % i, [128, 512], BF16, PD) for i in range(2)]
                    rec = sb("rec", [64, 512], F32, PD)
                    oa = [sb("oa%d" % i, [64, 512], BF16, PD) for i in range(2)]
                    psS = [ps("psS%d" % i, [128, 512], F32, PD) for i in range(2)]
                    psO = [ps("psOd%d" % i, [128, 512], F32, PD) for i in range(2)]
                    psD = ps("psDd", [128, 512], F32, PD)
                    groups = [
                        (0, 1, QA, KA, 0),
                        (1, 4, QA, KA, 64),
                        (2, 16, QB, KB, 0),
                    ]
                    qkey = {0: "QA", 1: "QA", 2: "QB"}
                    kkey = {0: "KA", 1: "KA", 2: "KB"}
                    pairno = 0
                    pending = []
                    import os as _os3
                    _dg = _os3.environ.get("DILG")
                    for (g, d, Qt, Kt, ro) in groups:
                        if _dg is not None and str(g) not in _dg:
                            continue
                        nb = 64 // d
                        rows = slice(ro, ro + 64)
                        if d == 1:
                            banks = [[(0, n0 + i) for i in range(4)] for n0 in range(0, 64, 4)]
                        else:
                            banks = [[(r0 + i, n) for i in range(4)] for n in range(nb) for r0 in range(0, d, 4)]
                        for bank in banks:
                            po = psO[pairno % 2]
                            pok = "psOd%d" % (pairno % 2)
                            pairno += 1
                            for half in range(2):
                                blks = bank[2 * half:2 * half + 2]
                                i2 = (pairno * 2 + half) % 2
                                pS = psS[i2]
                                psk = "psS%d" % i2
                                for bi, (r, n) in enumerate(blks):
                                    blk = r * nb + n
                                    qsl = slice(blk * 128, (blk + 1) * 128)
                                    pblk = blk - 1 if n > 0 else blk
                                    ksl_p = slice(pblk * 128, (pblk + 1) * 128)
                                    sc.add("pe", lambda e, pS=pS, bi=bi, ksl_p=ksl_p, qsl=qsl, Kt=Kt, Qt=Qt, rows=rows: e.matmul(
                                        pS[:, bi * 256:bi * 256 + 128], lhsT=Kt[rows, ksl_p], rhs=Qt[rows, qsl],
                                        start=True, stop=True), reads=[qkey[g], kkey[g]], writes=[psk])
                                    sc.add("pe", lambda e, pS=pS, bi=bi, qsl=qsl, Kt=Kt, Qt=Qt, rows=rows: e.matmul(
                                        pS[:, bi * 256 + 128:bi * 256 + 256], lhsT=Kt[rows, qsl], rhs=Qt[rows, qsl],
                                        start=True, stop=True), reads=[qkey[g], kkey[g]], writes=[psk])
                                pe_ = pex[i2]
                                pb_ = pbf[i2]
                                sc.add("act", lambda e, pe_=pe_, pS=pS: e.activation(out=pe_[:], in_=pS[:, :], func=AF.Exp),
                                       reads=[psk], writes=["pex%d" % i2])
                                for bi in range(2):
                                    sc.add("dve", lambda e, pe_=pe_, pb_=pb_, bi=bi, g=g: e.tensor_tensor(
                                        out=pb_[:, bi * 256:(bi + 1) * 256], in0=pe_[:, bi * 256:(bi + 1) * 256],
                                        in1=dm[:, g * 256:(g + 1) * 256], op=ALU.mult),
                                        reads=["pex%d" % i2, "dm"], writes=["pbf%d" % i2])
                                def make_av(blks=blks, half=half, po=po, pok=pok, pb_=pb_, i2=i2, g=g, nb=nb, d=d, bank=bank):
                                    def f():
                                        for bi, (r, n) in enumerate(blks):
                                            blk = r * nb + n
                                            slot = 2 * half + bi
                                            osl = po[0:65, slot * 128:(slot + 1) * 128]
                                            if n > 0:
                                                sc.add("pe", lambda e, osl=osl, pb_=pb_, bi=bi, blk=blk, g=g: e.matmul(
                                                    osl, lhsT=Vd[g][:, blk - 1, 0:65], rhs=pb_[:, bi * 256:bi * 256 + 128],
                                                    start=True, stop=False),
                                                    reads=["pbf%d" % i2, "Vd%d" % g, "Vd%d_ones" % g], writes=[pok])
                                            sc.add("pe", lambda e, osl=osl, pb_=pb_, bi=bi, blk=blk, g=g, n=n: e.matmul(
                                                osl, lhsT=Vd[g][:, blk, 0:65], rhs=pb_[:, bi * 256 + 128:bi * 256 + 256],
                                                start=(n == 0), stop=True),
                                                reads=["pbf%d" % i2, "Vd%d" % g, "Vd%d_ones" % g], writes=[pok])

                                        if half == 1:
                                            r0, n0 = bank[0]
                                            if d == 1:
                                                dst = acc[0:65, n0 * 128:(n0 + 4) * 128]
                                                sc.add("act", lambda e, dst=dst, po=po: e.activation(out=dst, in_=po[0:65, :], func=AF.Copy),
                                                       reads=[pok], writes=["acc"])
                                            else:
                                                base = 128 * n0 * d + r0
                                                span = acc[0:65, 128 * n0 * d:128 * (n0 + 1) * d].rearrange("p (i r) -> p r i", r=d)
                                                dst = span[:, r0:r0 + 4, :]
                                                src = po[0:65, :].rearrange("p (r i) -> p r i", r=4)
                                                sc.add("dve", lambda e, dst=dst, src=src: e.tensor_tensor(out=dst, in0=dst, in1=src, op=ALU.add),
                                                       reads=[pok, "acc"], writes=["acc"])

                                    return f

                                pending.append(make_av())
                                if len(pending) > 1:
                                    pending.pop(0)()
                    while pending:
                        pending.pop(0)()
                    for ch in range(16):
                        csl = slice(ch * 512, (ch + 1) * 512)
                        sc.add("pe", lambda e, csl=csl: e.matmul(psD[0:64, :], lhsT=sel_sb[0:65, :], rhs=acc[0:65, csl],
                                                                 start=True, stop=True),
                               reads=["acc", "sel"], writes=["psDd"])
                        sc.add("dve", lambda e: e.reciprocal(out=rec[:], in_=psD[0:64, :]), reads=["psDd"], writes=["rec"])
                        oo = oa[ch % 2]
                        sc.add("dve", lambda e, oo=oo, csl=csl: e.tensor_tensor(out=oo[:], in0=acc[0:64, csl], in1=rec[:],
                                                                                op=ALU.mult),
                               reads=["acc", "rec"], writes=["oa%d" % (ch % 2)])
                        sc.add("sp", lambda e, oo=oo, ch=ch: e.dma_start(
                            out=o_locq[ch // 4].ap()[0:64, (ch % 4) * 512:(ch % 4 + 1) * 512], in_=oo[:]),
                               reads=["oa%d" % (ch % 2)], writes=["o_loc_a%d" % ch], dma="oa%d" % (ch % 2))
                sc.barrier()
                if phases < 3:
                    sc.enabled = False

            PS_ = contextlib.ExitStack()
            with PS_:
                NB = 4
                cm = sb("cm", [128, 2048], BF16, PS_)
                sc.add("sp", lambda e: e.dma_start(out=cm[:], in_=cmask[:, :]), writes=["cm"], dma="c2")
                Eb = [sb("Eb%d" % i, [128, 512], F32, PS_) for i in range(NB)]
                Lb = [sb("Lb%d" % i, [128, 512], BF16, PS_) for i in range(NB)]
                Ab = [sb("Ab%d" % i, [128, 512], BF16, PS_) for i in range(NB)]
                LsF = sb("LsF", [128, 512], F32, PS_)
                LsB = [sb("LsB%d" % i, [128, 512], BF16, PS_) for i in range(4)]
                ost = [sb("ost%d" % i, [64, 512], BF16, PS_) for i in range(2)]
                psZ = [ps("psZ%d" % i, [128, 512], F32, PS_) for i in range(2)]
                psB = [ps("psB%d" % i, [128, 512], F32, PS_) for i in range(2)]
                psOs = [ps("psOs%d" % i, [128, 512], F32, PS_) for i in range(2)]
                heads = [
                    (QB, KB, slice(64, 128), "QB", "KB", slice(0, 64), 64),
                    (QC, KC, slice(0, 64), "QC", "KC", slice(64, 128), 128),
                ]
                pairs = []
                qn = 0
                for qt in range(16):
                    for hi, hd in enumerate(heads):
                        kmax = 4 * qt + 3
                        kmin = 0 if sb_limit is None else max(0, kmax - sb_limit + 1)
                        for kb in range(kmax, kmin - 1, -1):
                            pairs.append(dict(h=hd, hi=hi, qt=qt, kb=kb, first=(kb == kmax), last=(kb == kmin),
                                              u=(kb - 4 * qt) if kb >= 4 * qt else None, qn=qn))
                        qn += 1

                Xb = [sb("Xb%d" % i, [128, 512], F32, PS_) for i in range(2)]

                def stage1(i, p):
                    Qt, Kt, rows, qk, kk, vcols, orow = p["h"]
                    qsl = slice(p["qt"] * 512, (p["qt"] + 1) * 512)
                    ksl = slice(p["kb"] * 128, (p["kb"] + 1) * 128)
                    z = psZ[i % 2]
                    zk = "psZ%d" % (i % 2)
                    E, L = Eb[i % NB], Lb[i % NB]
                    ek, lk = "Eb%d" % (i % NB), "Lb%d" % (i % NB)
                    sc.add("pe", lambda e: e.matmul(z[:, :], lhsT=Kt[rows, ksl], rhs=Qt[rows, qsl], start=True, stop=True),
                           reads=[qk, kk], writes=[zk])
                    sc.add("act", lambda e: e.activation(out=E[:], in_=z[:, :], func=AF.Exp), reads=[zk], writes=[ek])
                    if p["u"] is not None:
                        u = p["u"]
                        sc.add("dve", lambda e: e.tensor_tensor(out=E[:], in0=E[:], in1=cm[:, u * 512:(u + 1) * 512],
                                                                op=ALU.mult), reads=[ek, "cm"], writes=[ek])
                    sc.add("act", lambda e: e.activation(out=L[:], in_=E[:], func=AF.Ln, bias=1.0), reads=[ek], writes=[lk])
                    if not p["last"]:
                        nxt = LsB[(p["cnt"] + 1) % 4]
                        nxtk = "LsB%d" % ((p["cnt"] + 1) % 4)
                        if p["first"]:
                            sc.add("dve", lambda e: e.tensor_copy(out=nxt[:], in_=L[:]), reads=[lk], writes=[nxtk])
                            sc.add("dve", lambda e: e.tensor_copy(out=LsF[:], in_=L[:]), reads=[lk], writes=["LsF"])
                        else:
                            sc.add("dve", lambda e: e.tensor_tensor(out=LsF[:], in0=LsF[:], in1=L[:], op=ALU.add),
                                   reads=[lk, "LsF"], writes=["LsF"])
                            sc.add("dve", lambda e: e.tensor_copy(out=nxt[:], in_=LsF[:]), reads=["LsF"], writes=[nxtk])

                def stage2(i, p):
                    Qt, Kt, rows, qk, kk, vcols, orow = p["h"]
                    qsl = slice(p["qt"] * 512, (p["qt"] + 1) * 512)
                    kb = p["kb"]
                    bq = psB[i % 2]
                    bk = "psB%d" % (i % 2)
                    E, L, A = Eb[i % NB], Lb[i % NB], Ab[i % NB]
                    ek, lk, ak = "Eb%d" % (i % NB), "Lb%d" % (i % NB), "Ab%d" % (i % NB)
                    X = Xb[i % 2]
                    xk_ = "Xb%d" % (i % 2)
                    first, last = p["first"], p["last"]
                    cur = LsB[p["cnt"] % 4]
                    curk = "LsB%d" % (p["cnt"] % 4)
                    sc.add("pe", lambda e: e.matmul(bq[:, :], lhsT=negtri, rhs=L[:], start=True, stop=first),
                           reads=[lk, "cmat"], writes=[bk])
                    if not first:
                        sc.add("pe", lambda e: e.matmul(bq[:, :], lhsT=negones, rhs=cur[:], start=False, stop=True),
                               reads=[curk, "cmat"], writes=[bk])
                    sc.add("act", lambda e: e.activation(out=X[:], in_=bq[:, :], func=AF.Exp), reads=[bk], writes=[xk_])
                    sc.add("dve", lambda e: e.tensor_tensor(out=A[:], in0=E[:], in1=X[:], op=ALU.mult),
                           reads=[ek, xk_], writes=[ak])

                def stage3(i, p):
                    Qt, Kt, rows, qk, kk, vcols, orow = p["h"]
                    qsl = slice(p["qt"] * 512, (p["qt"] + 1) * 512)
                    kb = p["kb"]
                    A = Ab[i % NB]
                    ak = "Ab%d" % (i % NB)
                    first, last = p["first"], p["last"]
                    po = psOs[p["qn"] % 2]
                    pok = "psOs%d" % (p["qn"] % 2)
                    sc.add("pe", lambda e: e.matmul(po[0:64, :], lhsT=Vs[:, kb, vcols], rhs=A[:], start=first, stop=last,
                                                    skip_group_check=True),
                           reads=[ak, "Vs"], writes=[pok])
                    if last:
                        oo = ost[p["qn"] % 2]
                        ook = "ost%d" % (p["qn"] % 2)
                        sc.add("act", lambda e: e.activation(out=oo[:], in_=po[0:64, :], func=AF.Copy), reads=[pok], writes=[ook])
                        qt_ = p["qt"]
                        sc.add("sp", lambda e: e.dma_start(
                            out=o_locq[qt_ // 4].ap()[orow:orow + 64, (qt_ % 4) * 512:(qt_ % 4 + 1) * 512], in_=oo[:]),
                               reads=[ook], writes=["o_loc_b%d_%d" % (orow, qt_)], dma=ook)

                cnt = 0
                for p in pairs:
                    p["cnt"] = cnt
                    cnt += 1
                DEPTH = 2
                for i in range(len(pairs) + 2 * DEPTH):
                    if i < len(pairs):
                        stage1(i, pairs[i])
                    if 0 <= i - DEPTH < len(pairs):
                        stage2(i - DEPTH, pairs[i - DEPTH])
                    if 0 <= i - 2 * DEPTH < len(pairs):
                        p3 = pairs[i - 2 * DEPTH]
                        stage3(i - 2 * DEPTH, p3)
                        if phases >= 4 and p3["last"] and p3["hi"] == 1 and p3["qt"] % 4 == 3:
                            add_gather(p3["qt"] // 4)
            sc.barrier()
            if phases < 4:
                sc.enabled = False

        oallk = ["o_all%d" % q for q in range(4)]
        if debug:
            for q in range(4):
                sc.add("pool", lambda e, q=q: e.dma_start(out=dbg[:, q * TOK_OWN:(q + 1) * TOK_OWN], in_=o_all.ap()[q]),
                       reads=oallk, writes=["dbg%d" % q], dma="dbg")

        if phases < 5:
            sc.enabled = False
        PC = contextlib.ExitStack()
        with PC:
            xr = sb("xr", [128, 16, D], F32, PC)
            hT2 = sb("hT2", [128, 8, TOK_OWN], BF16, PC)
            junk = sb("junkC", [128, D], BF16, PC)
            ss = sb("ssC", [128, 4], F32, PC)
            lnv = sb("lnvC", [128, 4], F32, PC)
            rstd = sb("rstdC", [128, 4], F32, PC)
            xn = sb("xnC", [128, 4, D], BF16, PC)
            xo = x_own.rearrange("(tt p) d -> p tt d", p=128)
            for q4 in range(4):
                sc.add("sp", lambda e, q4=q4: e.dma_start(out=xr[:, 4 * q4:4 * q4 + 4, :], in_=xo[:, 4 * q4:4 * q4 + 4, :]),
                       writes=["xr%d" % q4], dma="xr%d" % q4)
            psT = [ps("psTC%d" % i, [128, 1024], BF16, PC) for i in range(2)]
            psM = [ps("psMC%d" % i, [128, 512], F32, PC) for i in range(6)]
            pmc = [0]

            def nextps():
                i = pmc[0] % 6
                pmc[0] += 1
                return psM[i], "psMC%d" % i

            C1 = contextlib.ExitStack()
            with C1:
                Wg = sb("Wg", [128, 8, 2 * D], BF16, C1)
                Wud = sb("Wud", [128, 2, D], BF16, C1)
                Wus = sb("Wus", [128, 4, D], BF16, C1)
                Wo = sb("Wo", [128, 8, D], BF16, C1)
                oA = sb("oA", [128, 2, 512], BF16, C1)
                oB = sb("oB", [128, 4, 512], BF16, C1)
                hT1 = sb("hT1", [128, 8, 512], BF16, C1)
                Ga = sb("Ga", [128, 512], F32, C1)
                Gb = sb("Gb", [128, 512], F32, C1)
                t1 = sb("t1", [128, 512], F32, C1)
                t2 = sb("t2", [128, 512], F32, C1)
                mg = sb("mg", [128, 8, 512], BF16, C1)
                wgv = w_gate.rearrange("(c p) n -> p c n", p=128)
                for c0 in range(0, 8, 2):
                    sc.add("pool", lambda e, c0=c0: e.dma_start(out=Wg[:, c0:c0 + 2, :], in_=wgv[:, c0:c0 + 2, :]),
                           writes=["Wg%d" % c0], dma="wg")
                wgk = ["Wg%d" % c0 for c0 in range(0, 8, 2)]
                sc.add("pool", lambda e: e.dma_start(out=Wud[:], in_=w_ud.rearrange("(c p) n -> p c n", p=128)),
                       writes=["Wud"], dma="wud")
                sc.add("pool", lambda e: e.dma_start(out=Wus[:], in_=w_us.rearrange("(c p) n -> p c n", p=128)),
                       writes=["Wus"], dma="wus")
                wov = w_o.rearrange("(c p) n -> p c n", p=128)
                for c0 in range(0, 8, 4):
                    sc.add("pool", lambda e, c0=c0: e.dma_start(out=Wo[:, c0:c0 + 4, :], in_=wov[:, c0:c0 + 4, :]),
                           writes=["Wo%d" % c0], dma="wo")
                wok = ["Wo0", "Wo4"]
                jq_cache = {}

                def get_jq(e):
                    if "jq" not in jq_cache:
                        pid = e.partition_id()
                        jq_cache["jq"] = bass.ds(pid % 4, 1)
                    return jq_cache["jq"]

                omv = o_mine.ap()

                def mk_om(r):
                    def f(e):
                        jq = get_jq(e)
                        return e.dma_start(out=omv[r * 192:(r + 1) * 192, :].rearrange("(o f) t -> o f t", o=1),
                                           in_=o_all.ap()[jq, r * 192:(r + 1) * 192, :])
                    return f

                for r in range(4):
                    sc.add("pool", mk_om(r), reads=oallk, writes=["o_mine%d" % r], dma="omine")

                def mk_oa(cc, hh, T):
                    def f(e):
                        r = 2 * cc + hh
                        return e.dma_start(out=oA[hh * 64:(hh + 1) * 64, cc, :],
                                           in_=omv[r * 192:r * 192 + 64, T * 512:(T + 1) * 512])
                    return f

                def mk_ob(cc, T):
                    def f(e):
                        return e.dma_start(out=oB[:, cc, :], in_=omv[cc * 192 + 64:cc * 192 + 192, T * 512:(T + 1) * 512])
                    return f

                for T in range(4):
                    tsl = slice(T * 512, (T + 1) * 512)
                    for cc in range(2):
                        for hh in range(2):
                            sc.add("sp", mk_oa(cc, hh, T), reads=["o_mine%d" % r_ for r_ in range(4)], writes=["oA%d" % (2 * cc + hh)], dma="oAg")
                    for cc in range(4):
                        sc.add("sp", mk_ob(cc, T), reads=["o_mine%d" % r_ for r_ in range(4)], writes=["oB%d" % cc], dma="oBg")
                    rmsnorm_T([(xr[:, 4 * T + tt, :], "xr%d" % T) for tt in range(4)], 4,
                              lambda c: (hT1[:, c, :], "hT1"), g1, None, (ss, lnv, rstd, junk, xn), psT, "C")
                    for m in range(8):
                        msl = slice(m * 128, (m + 1) * 128)
                        pga, pgak = nextps()
                        for c in range(8):
                            sc.add("pe", lambda e, c=c, pga=pga, msl=msl: e.matmul(pga[:, :], lhsT=Wg[:, c, msl], rhs=hT1[:, c, :],
                                                                                  start=(c == 0), stop=(c == 7)),
                                   reads=wgk + ["hT1"], writes=[pgak])
                        pgb, pgbk = nextps()
                        msl2 = slice(D + m * 128, D + (m + 1) * 128)
                        for c in range(8):
                            sc.add("pe", lambda e, c=c, pgb=pgb, msl2=msl2: e.matmul(pgb[:, :], lhsT=Wg[:, c, msl2], rhs=hT1[:, c, :],
                                                                                    start=(c == 0), stop=(c == 7)),
                                   reads=wgk + ["hT1"], writes=[pgbk])
                        pua, puak = nextps()
                        for c in range(2):
                            sc.add("pe", lambda e, c=c, pua=pua, msl=msl, tsl=tsl: e.matmul(pua[:, :], lhsT=Wud[:, c, msl],
                                                                                           rhs=oA[:, c, :],
                                                                                           start=(c == 0), stop=(c == 1)),
                                   reads=["Wud", "oA0", "oA1", "oA2", "oA3"], writes=[puak])
                        pub, pubk = nextps()
                        for c in range(4):
                            sc.add("pe", lambda e, c=c, pub=pub, msl=msl, tsl=tsl: e.matmul(pub[:, :], lhsT=Wus[:, c, msl],
                                                                                           rhs=oB[:, c, :],
                                                                                           start=(c == 0), stop=(c == 3)),
                                   reads=["Wus", "oB0", "oB1", "oB2", "oB3"], writes=[pubk])
                        sc.add("act", lambda e, pga=pga, m=m: e.activation(out=Ga[:], in_=pga[:, :], func=AF.Sigmoid,
                                                                           bias=bg[:, m:m + 1]),
                               reads=[pgak, "vec"], writes=["Ga"])
                        sc.add("act", lambda e, pgb=pgb, m=m: e.activation(out=Gb[:], in_=pgb[:, :], func=AF.Sigmoid,
                                                                           bias=bg[:, 8 + m:9 + m]),
                               reads=[pgbk, "vec"], writes=["Gb"])
                        sc.add("dve", lambda e, pua=pua: e.tensor_tensor(out=t1[:], in0=Ga[:], in1=pua[:, :], op=ALU.mult),
                               reads=["Ga", puak], writes=["t1"])
                        sc.add("dve", lambda e, pub=pub: e.tensor_tensor(out=t2[:], in0=Gb[:], in1=pub[:, :], op=ALU.mult),
                               reads=["Gb", pubk], writes=["t2"])
                        sc.add("pool", lambda e, m=m: e.tensor_tensor(out=mg[:, m, :], in0=t1[:], in1=t2[:], op=ALU.add),
                               reads=["t1", "t2"], writes=["mg%d" % m])
                    mgk = ["mg%d" % m for m in range(8)]
                    for tb in range(4):
                        for half in range(2):
                            py, pyk = nextps()
                            hsl = slice(half * 512, (half + 1) * 512)
                            for c in range(8):
                                sc.add("pe", lambda e, c=c, py=py, tb=tb, hsl=hsl: e.matmul(
                                    py[:, :], lhsT=mg[:, c, tb * 128:(tb + 1) * 128], rhs=Wo[:, c, hsl],
                                    start=(c == 0), stop=(c == 7)), reads=mgk + wok, writes=[pyk])
                            xa = xr[:, 4 * T + tb, hsl]
                            sc.add("dve", lambda e, xa=xa, py=py: e.tensor_tensor(out=xa, in0=xa, in1=py[:, :], op=ALU.add),
                                   reads=[pyk, "xr%d" % T], writes=["xr%d" % T])
                    rmsnorm_T([(xr[:, 4 * T + tt, :], "xr%d" % T) for tt in range(4)], 4,
                              lambda c, tsl=tsl: (hT2[:, c, tsl], "hT2_%d" % T), g2, None,
                              (ss, lnv, rstd, junk, xn), psT, "C")
            sc.barrier()
            C2 = contextlib.ExitStack()
            with C2:
                gfb = sb("gfb", [128, D], F32, C2)
                sc.add("sp", lambda e: e.dma_start(out=gfb[:], in_=gfin[:, :]), writes=["gfb"], dma="c3")
                W1q = [sb("W1q%d" % i, [128, 8, D], BF16, C2) for i in range(2)]
                W2q = [sb("W2q%d" % i, [128, 8, D], BF16, C2) for i in range(2)]
                uT = [sb("uT%d" % i, [128, 8, 512], BF16, C2) for i in range(2)]
                rl = [sb("rl%d" % i, [128, 512], F32, C2) for i in range(2)]
                yo = [sb("yo%d" % i, [128, D], F32, C2) for i in range(2)]
                w1v = w_1.rearrange("(c p) n -> p c n", p=128)
                w2v = w_2.rearrange("(c p) n -> p c n", p=128)
                it = 0
                rc = 0
                for qf in range(4):
                    s_ = qf % 2
                    sc.add("pool", lambda e, qf=qf, s_=s_: e.dma_start(out=W1q[s_][:], in_=w1v[:, :, qf * D:(qf + 1) * D]),
                           writes=["W1q%d" % s_], dma="w1q%d" % s_)
                    sc.add("pool", lambda e, qf=qf, s_=s_: e.dma_start(out=W2q[s_][:], in_=w2v[:, 8 * qf:8 * qf + 8, :]),
                           writes=["W2q%d" % s_], dma="w2q%d" % s_)
                    for T in range(4):
                        tsl = slice(T * 512, (T + 1) * 512)
                        u = uT[it % 2]
                        uk = "uT%d" % (it % 2)
                        it += 1
                        for m in range(8):
                            pu, puk = nextps()
                            for c in range(8):
                                sc.add("pe", lambda e, c=c, pu=pu, m=m, s_=s_, tsl=tsl: e.matmul(
                                    pu[:, :], lhsT=W1q[s_][:, c, m * 128:(m + 1) * 128], rhs=hT2[:, c, tsl],
                                    start=(c == 0), stop=(c == 7)), reads=["W1q%d" % s_, "hT2_%d" % T], writes=[puk])
                            r_ = rl[rc % 2]
                            rk = "rl%d" % (rc % 2)
                            rc += 1
                            sc.add("act", lambda e, r_=r_, pu=pu: e.activation(out=r_[:], in_=pu[:, :], func=AF.Relu),
                                   reads=[puk], writes=[rk])
                            sc.add("pool", lambda e, r_=r_, u=u, m=m: e.tensor_tensor(out=u[:, m, :], in0=r_[:], in1=r_[:],
                                                                                    op=ALU.mult),
                                   reads=[rk], writes=[uk + "_%d" % m])
                        uks = [uk + "_%d" % m for m in range(8)]
                        for tb in range(4):
                            for half in range(2):
                                py, pyk = nextps()
                                hsl = slice(half * 512, (half + 1) * 512)
                                for m in range(8):
                                    sc.add("pe", lambda e, m=m, py=py, u=u, tb=tb, hsl=hsl, s_=s_: e.matmul(
                                        py[:, :], lhsT=u[:, m, tb * 128:(tb + 1) * 128], rhs=W2q[s_][:, m, hsl],
                                        start=(m == 0), stop=(m == 7)), reads=uks + ["W2q%d" % s_], writes=[pyk])
                                xa = xr[:, 4 * T + tb, hsl]
                                sc.add("dve", lambda e, xa=xa, py=py: e.tensor_tensor(out=xa, in0=xa, in1=py[:, :], op=ALU.add),
                                       reads=[pyk, "xr%d" % T], writes=["xr%d" % T])
                ov = out.rearrange("(tt p) d -> p tt d", p=128)
                for T in range(4):
                    for tt in range(4):
                        sc.add("act", lambda e, T=T, tt=tt: e.activation(out=junk[:], in_=xr[:, 4 * T + tt, :], func=AF.Square,
                                                                         accum_out=ss[:, tt:tt + 1]),
                               reads=["xr%d" % T], writes=["Fjunk", "Fss%d" % tt])
                    sc.add("act", lambda e: e.activation(out=lnv[:, 0:4], in_=ss[:, 0:4], func=AF.Ln, scale=1.0 / D,
                                                         bias=eps_sb[:, 0:1]),
                           reads=["Fss%d" % tt for tt in range(4)] + ["eps"], writes=["Flnv"])
                    sc.add("act", lambda e: e.activation(out=rstd[:, 0:4], in_=lnv[:, 0:4], func=AF.Exp, scale=-0.5),
                           reads=["Flnv"], writes=["Frstd"])
                    for tt in range(4):
                        y = yo[tt % 2]
                        yk = "yo%d" % (tt % 2)
                        sc.add("dve", lambda e, T=T, tt=tt, y=y: e.scalar_tensor_tensor(
                            out=y[:], in0=xr[:, 4 * T + tt, :], scalar=rstd[:, tt:tt + 1], in1=gfb[:],
                            op0=ALU.mult, op1=ALU.mult), reads=["xr%d" % T, "Frstd", "gfb"], writes=[yk])
                        sc.add("sp", lambda e, T=T, tt=tt, y=y: e.dma_start(out=ov[:, 4 * T + tt, :], in_=y[:]),
                               reads=[yk], writes=["out"], dma=yk)

        semstack = contextlib.ExitStack()
        with semstack:
            sc.prepare(nc, semstack)
            with nc.Block() as block:
                sc.emit(nc, block)
    return nc


def _constants(j):
    bf = ml_dtypes.bfloat16
    ident = np.eye(128, dtype=np.float32)
    jj = np.arange(128)[:, None]
    kk = np.arange(128)[None, :]
    negtri = np.where(jj >= kk, -1.0, 0.0).astype(np.float32)
    negones = -np.ones((128, 128), np.float32)
    cmat = np.concatenate([ident, negtri, negones], axis=1).astype(bf)
    p = np.arange(128)[:, None]
    c = np.arange(512)[None, :]
    cm = np.concatenate([(128 * u + p < c).astype(np.float32) for u in range(4)], axis=1).astype(bf)
    kq = np.arange(128)[:, None].astype(np.float64)
    qq = np.arange(128)[None, :].astype(np.float64)
    dms = []
    for g, d in enumerate((1, 4, 16)):
        slope = 2.0 ** (-8.0 * (4 * g + j + 1) / 12.0)
        sp = qq + 128 - kq
        prev = np.where(sp <= 128, np.exp(-slope * d * sp), 0.0)
        scur = qq - kq
        cur = np.where(scur >= 0, np.exp(-slope * d * scur), 0.0)
        dms.append(np.concatenate([prev, cur], axis=1))
    dmask = np.concatenate(dms, axis=1).astype(np.float32)
    sel = np.zeros((128, 64), np.float32)
    sel[64, :] = 1.0
    return cmat, cm, dmask, sel


def _own_cols(j):
    def dq(g):
        return (4 * g + j) * 64
    cols = {}
    qa, ka, va = 0, 768, 1536
    qb, kb, vb = 2304, 2816, 3328
    s0, s1 = 2 * j, 2 * j + 1
    r = lambda o: list(range(o, o + 64))
    cols["QA"] = r(qa + dq(0)) + r(qa + dq(1))
    cols["KA"] = r(ka + dq(0)) + r(ka + dq(1))
    cols["VA"] = r(va + dq(0)) + r(va + dq(1))
    cols["QB"] = r(qa + dq(2)) + r(qb + s0 * 64)
    cols["KB"] = r(ka + dq(2)) + r(kb + s0 * 64)
    cols["VS"] = r(vb + s0 * 64) + r(vb + s1 * 64)
    cols["QC"] = r(qb + s1 * 64)
    cols["KC"] = r(kb + s1 * 64)
    cols["VG"] = r(va + dq(2))
    idx = []
    for n in OWN_TILES:
        idx += cols[n]
    return np.array(idx)


_NC_CACHE = {}


def kernel(x, norm_mix_g, w_in, b_gate, w_up_dil, w_up_sb, w_out, norm_mlp_g, w_mlp_in, w_mlp_out, norm_final_g,
           _debug=False, _sb_limit=None, _phases=9):
    x = np.asarray(x, np.float32)
    w_in0 = np.asarray(w_in, np.float32)[0]
    key = (_debug, _sb_limit, _phases)
    if key not in _NC_CACHE:
        _NC_CACHE[key] = build_nc(debug=_debug, sb_limit=_sb_limit, phases=_phases)
    nc = _NC_CACHE[key]
    vecs = np.concatenate([
        np.asarray(norm_mix_g, np.float32)[0].reshape(8, 128).T,
        np.asarray(norm_mlp_g, np.float32)[0].reshape(8, 128).T,
        np.asarray(b_gate, np.float32)[0].reshape(16, 128).T], axis=1)
    vecs = np.ascontiguousarray(vecs)
    gfin = np.ascontiguousarray(np.broadcast_to(np.asarray(norm_final_g, np.float32)[None, :], (128, D)))
    w_gate = np.ascontiguousarray(w_in0[:, 3840:5888])
    shared = {
        "w_gate": w_gate,
        "w_ud": np.ascontiguousarray(np.asarray(w_up_dil, np.float32)[0]),
        "w_us": np.ascontiguousarray(np.asarray(w_up_sb, np.float32)[0]),
        "w_o": np.ascontiguousarray(np.asarray(w_out, np.float32)[0]),
        "w_1": np.ascontiguousarray(np.asarray(w_mlp_in, np.float32)[0]),
        "w_2": np.ascontiguousarray(np.asarray(w_mlp_out, np.float32)[0]),
        "vecs": vecs, "gfin": gfin,
    }
    in_maps = []
    for c in range(NCORES):
        b, j = c // 4, c % 4
        cmat, cm, dmask, sel = _constants(j)
        m = dict(shared)
        m["x_full"] = np.ascontiguousarray(x[b])
        m["x_own"] = np.ascontiguousarray(x[b, j * TOK_OWN:(j + 1) * TOK_OWN])
        m["w_own"] = np.ascontiguousarray(w_in0[:, _own_cols(j)])
        m["cmat"] = cmat
        m["cmask"] = cm
        m["dmask"] = dmask
        m["selm"] = sel
        in_maps.append(m)
    res = run_bass_kernel_spmd(nc, in_maps, core_ids=list(range(NCORES)))
    outp = np.empty((2, S, D), np.float32)
    for c in range(NCORES):
        b, j = c // 4, c % 4
        outp[b, j * TOK_OWN:(j + 1) * TOK_OWN] = np.asarray(res.results[c]["y_out"], np.float32)
    if _debug:
        return outp, [np.asarray(res.results[c]["dbg"]) for c in range(NCORES)]
    return outp
```

```python
import numpy as np
import ml_dtypes
import concourse.bass as bass
import concourse.mybir as mybir
from concourse.bass_utils import run_bass_kernel_spmd

F32 = mybir.dt.float32
BF16 = mybir.dt.bfloat16
AF = mybir.ActivationFunctionType
ALU = mybir.AluOpType

S = 8192
D = 1024
NCORES = 8
TOK_OWN = 2048
EPS = 1e-6
SEM_CAP = 2000
XMOD = 64

OWN_TILES = ["QA", "KA", "VA", "QB", "KB", "VS", "QC", "KC", "VG"]
OWN_W = {"QA": 128, "KA": 128, "VA": 128, "QB": 128, "KB": 128, "VS": 128, "QC": 64, "KC": 64, "VG": 64}
OWN_OFF = {}
_o = 0
for _n in OWN_TILES:
    OWN_OFF[_n] = _o
    _o += OWN_W[_n]
OWN_COLS = _o


class Op:
    __slots__ = ("eng", "fn", "dma", "deps", "signal", "num", "idx", "inc")

    def __init__(self, eng, fn, dma, inc=16):
        self.eng, self.fn, self.dma = eng, fn, dma
        self.inc = inc
        self.deps = []
        self.signal = False
        self.num = None
        self.idx = None


class Sched:
    def __init__(self):
        self.ops = []
        self.lastw = {}
        self.readers = {}
        self.floor = []
        self.last_by_src = {}
        self.enabled = True
        import os as _os
        self.maxops = int(_os.environ.get("KMAXOPS", "100000000"))

    @staticmethod
    def _src(op):
        return ("dma", op.dma) if op.dma is not None else ("eng", op.eng)

    def add(self, eng, fn, reads=(), writes=(), dma=None, inc=16):
        op = Op(eng, fn, dma, inc)
        if not self.enabled or len(self.ops) >= self.maxops:
            return op
        op.idx = len(self.ops)
        deps = {}
        for d in self.floor:
            deps[id(d)] = d
        for k in reads:
            w = self.lastw.get(k)
            if w is not None:
                deps[id(w)] = w
        for k in writes:
            w = self.lastw.get(k)
            if w is not None:
                deps[id(w)] = w
            for r in self.readers.get(k, {}).values():
                deps[id(r)] = r
        for k in reads:
            self.readers.setdefault(k, {})[self._src(op)] = op
        for k in writes:
            self.lastw[k] = op
            self.readers[k] = {}
        out = []
        for d in deps.values():
            if d is op:
                continue
            if d.dma is None and d.eng == "pe" and eng == "pe" and dma is None:
                continue
            out.append(d)
        op.deps = out
        self.ops.append(op)
        self.last_by_src[self._src(op)] = op
        return op

    def barrier(self, keep_prefix=None, skip_src=None):
        self.floor = [op for src, op in self.last_by_src.items() if src != skip_src]
        self.lastw = {k: v for k, v in self.lastw.items() if keep_prefix is not None and k.startswith(keep_prefix)}
        self.readers = {}

    def prepare(self, nc, semstack):
        for op in self.ops:
            for d in op.deps:
                d.signal = True
        for d in self.last_by_src.values():
            d.signal = True
        counters = {}
        for op in self.ops:
            if op.dma is not None:
                k = ("dma", op.dma)
                counters[k] = counters.get(k, 0) + 1
                op.num = counters[k]
            elif op.signal:
                k = ("eng", op.eng)
                counters[k] = counters.get(k, 0) + 1
                op.num = counters[k]
        sems = {}
        for k, n in counters.items():
            if k[0] == "dma":
                sems[k] = [semstack.enter_context(nc.semaphore("d_%s" % str(k[1])))]
            else:
                ns = (n + SEM_CAP - 1) // SEM_CAP
                sems[k] = [semstack.enter_context(nc.semaphore("e_%s_%d" % (k[1], i))) for i in range(ns)]
        self.sems = sems

    def emit(self, nc, block):
        sems = self.sems

        def semval(op):
            k = Sched._src(op)
            if k[0] == "dma":
                return sems[k][0], op.num * op.inc
            n = op.num - 1
            return sems[k][n // SEM_CAP], n % SEM_CAP + 1

        def run(engname, e):
            waited = {}
            for op in self.ops:
                if op.eng != engname:
                    continue
                need = {}
                for d in op.deps:
                    k = Sched._src(d)
                    if d.num > need.get(k, (0, None))[0]:
                        need[k] = (d.num, d)
                for k, (n, d) in need.items():
                    if waited.get(k, 0) >= n:
                        continue
                    waited[k] = n
                    s, v = semval(d)
                    e.wait_ge(s, v)
                ins = op.fn(e)
                if op.dma is not None:
                    s, _ = semval(op)
                    ins.then_inc(s, op.inc)
                elif op.signal:
                    s, _ = semval(op)
                    ins.then_inc(s, 1)

        final = [op for op in self.last_by_src.values()]

        def runfinal(e):
            for d in final:
                s, v = semval(d) if d.num is not None else (None, None)
                if s is not None:
                    e.wait_ge(s, v)

        @block.tensor
        def _(e):
            run("pe", e)

        @block.scalar
        def _(e):
            run("act", e)

        @block.vector
        def _(e):
            run("dve", e)

        @block.gpsimd
        def _(e):
            run("pool", e)

        @block.sync
        def _(e):
            run("sp", e)
            runfinal(e)


def build_nc(debug=False, sb_limit=None, phases=9, ntA=16, tlist=None, lite=False):
    import contextlib

    nc = bass.Bass("TRN2", target_bir_lowering=False)
    dt = nc.dram_tensor
    x_full = dt("x_full", [S, D], F32, kind="ExternalInput").ap()
    x_own = dt("x_own", [TOK_OWN, D], F32, kind="ExternalInput").ap()
    w_own = dt("w_own", [D, OWN_COLS], F32, kind="ExternalInput").ap()
    if lite:
        _real_dt = dt

        def dt(name, shape, dtype, kind=None):
            if kind == "ExternalInput" and name in ("w_gate", "w_ud", "w_us", "w_o", "w_1", "w_2"):
                shape = [128, 8]
            return _real_dt(name, shape, dtype, kind=kind) if kind else _real_dt(name, shape, dtype)
    w_gate = dt("w_gate", [D, 2 * D], F32, kind="ExternalInput").ap()
    w_ud = dt("w_ud", [256, D], F32, kind="ExternalInput").ap()
    w_us = dt("w_us", [512, D], F32, kind="ExternalInput").ap()
    w_o = dt("w_o", [D, D], F32, kind="ExternalInput").ap()
    w_1 = dt("w_1", [D, 4 * D], F32, kind="ExternalInput").ap()
    w_2 = dt("w_2", [4 * D, D], F32, kind="ExternalInput").ap()
    vecs = dt("vecs", [128, 32], F32, kind="ExternalInput").ap()
    gfin = dt("gfin", [128, D], F32, kind="ExternalInput").ap()
    cmask = dt("cmask", [128, 4 * 512], BF16, kind="ExternalInput").ap()
    cmat = dt("cmat", [128, 3 * 128], BF16, kind="ExternalInput").ap()
    dmask = dt("dmask", [128, 3 * 256], F32, kind="ExternalInput").ap()
    selm = dt("selm", [128, 64], F32, kind="ExternalInput").ap()
    out = dt("y_out", [TOK_OWN, D], F32, kind="ExternalOutput").ap()
    o_locq = [dt("o_loc%d" % q, [192, TOK_OWN], BF16) for q in range(4)]
    o_all = dt("o_all", [4, 4 * 192, TOK_OWN], BF16)
    o_mine = dt("o_mine", [4 * 192, TOK_OWN], BF16)
    if debug:
        dbg = dt("dbg", [4 * 192, S], BF16, kind="ExternalOutput").ap()

    sc = Sched()
    es = contextlib.ExitStack()
    with es:
        def sb(name, shape, dtype, stack=es):
            return stack.enter_context(nc.sbuf_tensor(name, shape, dtype))

        def ps(name, shape, dtype, stack=es):
            return stack.enter_context(nc.psum_tensor(name, shape, dtype))

        vec_sb = sb("vec_sb", [128, 32], F32)
        cmat_sb = sb("cmat_sb", [128, 384], BF16)
        ident = cmat_sb[:, 0:128]
        negtri = cmat_sb[:, 128:256]
        negones = cmat_sb[:, 256:384]
        sc.add("sp", lambda e: e.dma_start(out=vec_sb[:], in_=vecs[:, :]), writes=["vec"], dma="c0a")
        sc.add("sp", lambda e: e.dma_start(out=cmat_sb[:], in_=cmat[:, :]), writes=["cmat"], dma="c0b")
        g1 = vec_sb[:, 0:8]
        g2 = vec_sb[:, 8:16]
        bg = vec_sb[:, 16:32]

        def rmsnorm_T(xsrc_tiles, ntt, hT_dst, gain, keyp, scratch, psT, tag, part=None):
            ss, lnv, rstd, junk, xn = scratch
            for tt, (xa, rk) in enumerate(xsrc_tiles if part in (None, "stats") else []):
                sc.add("act", lambda e, xa=xa, tt=tt: e.activation(out=junk[:], in_=xa, func=AF.Square,
                                                                   accum_out=ss[:, tt:tt + 1]),
                       reads=[rk], writes=[tag + "junk", tag + "ss%d" % tt])
            sskeys = [tag + "ss%d" % tt for tt in range(ntt)]
            if part in (None, "stats"):
                sc.add("act", lambda e: e.activation(out=lnv[:, 0:ntt], in_=ss[:, 0:ntt], func=AF.Ln,
                                                     scale=1.0 / D, bias=eps_sb[:, 0:1]),
                       reads=sskeys + ["eps"], writes=[tag + "lnv"])
                sc.add("act", lambda e: e.activation(out=rstd[:, 0:ntt], in_=lnv[:, 0:ntt], func=AF.Exp, scale=-0.5),
                       reads=[tag + "lnv"], writes=[tag + "rstd"])
            for tt, (xa, rk) in enumerate(xsrc_tiles if part in (None, "stats") else []):
                if tt % 2 == 0:
                    sc.add("dve", lambda e, xa=xa, tt=tt: e.tensor_scalar(out=xn[:, tt, :], in0=xa,
                                                                         scalar1=rstd[:, tt:tt + 1], scalar2=None,
                                                                         op0=ALU.mult),
                           reads=[rk, tag + "rstd"], writes=[tag + "xn%d" % tt])
                else:
                    sc.add("act", lambda e, xa=xa, tt=tt: e.activation(out=xn[:, tt, :], in_=xa, func=AF.Copy,
                                                                       scale=rstd[:, tt:tt + 1]),
                           reads=[rk, tag + "rstd"], writes=[tag + "xn%d" % tt])
            for c in range(8 if part in (None, "trans") else 0):
                pb = psT[c % 2]
                pk = tag + "psT%d" % (c % 2)
                for tt in range(ntt):
                    sc.add("pe", lambda e, pb=pb, tt=tt, c=c: e.transpose(out=pb[:, tt * 128:(tt + 1) * 128],
                                                                         in_=xn[:, tt, c * 128:(c + 1) * 128],
                                                                         identity=ident),
                           reads=[tag + "xn%d" % tt, "cmat"], writes=[pk])
                dst, dk = hT_dst(c)
                if c % 2 == 0:
                    sc.add("act", lambda e, pb=pb, dst=dst, c=c: e.activation(out=dst, in_=pb[:, 0:ntt * 128],
                                                                              func=AF.Copy, scale=gain[:, c:c + 1]),
                           reads=[pk, "vec"], writes=[dk])
                else:
                    sc.add("dve", lambda e, pb=pb, dst=dst, c=c: e.tensor_scalar(out=dst, in0=pb[:, 0:ntt * 128],
                                                                                 scalar1=gain[:, c:c + 1],
                                                                                 scalar2=None, op0=ALU.mult),
                           reads=[pk, "vec"], writes=[dk])

        eps_sb = sb("eps_sb", [128, 1], F32)
        sc.add("dve", lambda e: e.memset(eps_sb[:], EPS), writes=["eps"])

        def mk_gather(q):
            def f(e):
                return e.collective_compute("AllGather", ALU.bypass, replica_groups=[[0, 1, 2, 3], [4, 5, 6, 7]],
                                            ins=[o_locq[q].ap().opt()], outs=[o_all.ap()[q].opt()])
            return f

        def add_gather(q):
            olk = ["o_loc_a%d" % ch for ch in range(4 * q, 4 * q + 4)] + \
                  ["o_loc_b%d_%d" % (orow, qt) for orow in (64, 128) for qt in range(4 * q, 4 * q + 4)]
            sc.add("pool", mk_gather(q), reads=olk, writes=["o_all%d" % q], dma="cc", inc=1)

        AB = contextlib.ExitStack()
        with AB:
            QB = sb("QB", [128, S], BF16, AB)
            KB = sb("KB", [128, S], BF16, AB)
            QC = sb("QC", [64, S], BF16, AB)
            KC = sb("KC", [64, S], BF16, AB)
            Vs = sb("Vs", [128, 64, 128], BF16, AB)
            DIL = contextlib.ExitStack()
            with DIL:
                QA = sb("QA", [128, S], BF16, DIL)
                KA = sb("KA", [128, S], BF16, DIL)
                Vd = [sb("Vd%d" % g, [128, 64, 66], BF16, DIL) for g in range(3)]
                for g in range(3):
                    sc.add("pool", lambda e, g=g: e.memset(Vd[g][:, :, 64:65], 1.0), writes=["Vd%d_ones" % g])

                PA = contextlib.ExitStack()
                with PA:
                    Wown = sb("Wown", [128, 8, OWN_COLS], BF16, PA)
                    xs = [sb("xs%d" % i, [128, 2, D], F32, PA) for i in range(2)]
                    xn = sb("xnA", [128, 2, D], BF16, PA)
                    hT = [sb("hTA%d" % i, [128, 8, 512], BF16, PA) for i in range(2)]
                    junk = sb("junkA", [128, D], BF16, PA)
                    ss = sb("ssA", [128, 4], F32, PA)
                    lnv = sb("lnvA", [128, 4], F32, PA)
                    rstd = sb("rstdA", [128, 4], F32, PA)
                    vtA = sb("vtA", [128, 512], BF16, PA)
                    vtS = sb("vtS", [128, 512], BF16, PA)
                    vt3 = sb("vt3", [128, 2048], BF16, PA)
                    sc.add("dve", lambda e: e.memset(vt3[64:128, :], 0.0), writes=["vt3"])
                    psT = [ps("psTA%d" % i, [128, 1024], BF16, PA) for i in range(2)]
                    psP = [ps("psPA%d" % i, [128, 512], F32, PA) for i in range(3)]
                    psV = [ps("psVA%d" % i, [128, 1024], BF16, PA) for i in range(2)]

                    wv = w_own.rearrange("(c p) n -> p c n", p=128)
                    for c0 in range(0, 8, 2):
                        sc.add("pool", lambda e, c0=c0: e.dma_start(out=Wown[:, c0:c0 + 2, :], in_=wv[:, c0:c0 + 2, :]),
                               writes=["Wown%d" % c0], dma="wown")
                    wkeys = ["Wown%d" % c0 for c0 in range(0, 8, 2)]
                    xv = x_full.rearrange("(n tt p) d -> n p tt d", tt=2, p=128)

                    pcount = [0]

                    def proj_tile(t, name, hTt, hk):
                        w = OWN_W[name]
                        off = OWN_OFF[name]
                        i = pcount[0] % 3
                        pcount[0] += 1
                        pp = psP[i]
                        pk = "psPA%d" % i
                        for c in range(8):
                            sc.add("pe", lambda e, c=c, pp=pp: e.matmul(pp[0:w, :], lhsT=Wown[:, c, off:off + w],
                                                                       rhs=hTt[:, c, :], start=(c == 0), stop=(c == 7)),
                                   reads=wkeys + [hk], writes=[pk])
                        return pp, pk

                    def deint(ap_rows, d):
                        return ap_rows.rearrange("p (l r) -> p r l", r=d)

                    def norm_half(ti, t, half, part):
                        hTt = hT[ti % 2]
                        hk = "hT%d" % (ti % 2)
                        n = 2 * t + half
                        xsl = xs[n % 2]
                        xk = "xs%d" % (n % 2)
                        if part == "stats":
                            sc.add("sp", lambda e, xsl=xsl, n=n: e.dma_start(out=xsl[:], in_=xv[n]),
                                   writes=[xk], dma=xk)
                        rmsnorm_T([(xsl[:, 0, :], xk), (xsl[:, 1, :], xk)], 2,
                                  lambda c, half=half, hTt=hTt, hk=hk: (hTt[:, c, half * 256:(half + 1) * 256], hk),
                                  g1, None, (ss, lnv, rstd, junk, xn), psT, "A", part=part)

                    def proj_part1(ti, t):
                        hTt = hT[ti % 2]
                        hk = "hT%d" % (ti % 2)
                        tsl = slice(t * 512, (t + 1) * 512)
                        pp, pk = proj_tile(t, "QA", hTt, hk)
                        sc.add("act", lambda e, pp=pp, tsl=tsl: e.activation(out=QA[0:64, tsl], in_=pp[0:64, :],
                                                                             func=AF.Copy, scale=0.125),
                               reads=[pk], writes=["QA"])
                        sc.add("dve", lambda e, pp=pp, t=t: e.tensor_scalar(
                            out=QA[64:128, :].rearrange("p (r n l) -> p r n l", r=4, n=16)[:, :, t, :],
                            in0=deint(pp[64:128, :], 4), scalar1=0.125, scalar2=None, op0=ALU.mult),
                            reads=[pk], writes=["QA"])
                        pp, pk = proj_tile(t, "KA", hTt, hk)
                        sc.add("act", lambda e, pp=pp, tsl=tsl: e.activation(out=KA[0:64, tsl], in_=pp[0:64, :],
                                                                             func=AF.Copy), reads=[pk], writes=["KA"])
                        sc.add("dve", lambda e, pp=pp, t=t: e.tensor_copy(
                            out=KA[64:128, :].rearrange("p (r n l) -> p r n l", r=4, n=16)[:, :, t, :],
                            in_=deint(pp[64:128, :], 4)), reads=[pk], writes=["KA"])
                        pp, pk = proj_tile(t, "VA", hTt, hk)
                        sc.add("act", lambda e, pp=pp: e.activation(out=vtA[0:64, :], in_=pp[0:64, :], func=AF.Copy),
                               reads=[pk], writes=["vtA"])
                        sc.add("dve", lambda e, pp=pp: e.tensor_copy(
                            out=vtA[64:128, :].rearrange("p (r l) -> p r l", r=4), in_=deint(pp[64:128, :], 4)),
                            reads=[pk], writes=["vtA"])
                        pv = psV[0]
                        for bi in range(4):
                            sc.add("pe", lambda e, bi=bi, pv=pv: e.transpose(out=pv[:, bi * 128:(bi + 1) * 128],
                                                                            in_=vtA[:, bi * 128:(bi + 1) * 128],
                                                                            identity=ident),
                                   reads=["vtA", "cmat"], writes=["psVA0"])
                        for bi in range(4):
                            sc.add("act", lambda e, pv=pv, t=t, bi=bi: e.activation(
                                out=Vd[0][:, 4 * t + bi, 0:64], in_=pv[:, bi * 128:bi * 128 + 64], func=AF.Copy),
                                reads=["psVA0"], writes=["Vd0"])
                            sc.add("act", lambda e, pv=pv, t=t, bi=bi: e.activation(
                                out=Vd[1][:, bi * 16 + t, 0:64], in_=pv[:, bi * 128 + 64:bi * 128 + 128], func=AF.Copy),
                                reads=["psVA0"], writes=["Vd1"])

                    def proj_part2(ti, t):
                        hTt = hT[ti % 2]
                        hk = "hT%d" % (ti % 2)
                        tsl = slice(t * 512, (t + 1) * 512)
                        nn, qq = t // 4, t % 4
                        pp, pk = proj_tile(t, "QB", hTt, hk)
                        sc.add("dve", lambda e, pp=pp, nn=nn, qq=qq: e.tensor_scalar(
                            out=QB[0:64, :].rearrange("p (r n i) -> p r n i", r=16, n=4)[:, :, nn, 32 * qq:32 * qq + 32],
                            in0=deint(pp[0:64, :], 16), scalar1=0.125, scalar2=None, op0=ALU.mult),
                            reads=[pk], writes=["QB"])
                        sc.add("act", lambda e, pp=pp, tsl=tsl: e.activation(out=QB[64:128, tsl], in_=pp[64:128, :],
                                                                             func=AF.Copy, scale=0.125),
                               reads=[pk], writes=["QB"])
                        pp, pk = proj_tile(t, "KB", hTt, hk)
                        sc.add("dve", lambda e, pp=pp, nn=nn, qq=qq: e.tensor_copy(
                            out=KB[0:64, :].rearrange("p (r n i) -> p r n i", r=16, n=4)[:, :, nn, 32 * qq:32 * qq + 32],
                            in_=deint(pp[0:64, :], 16)), reads=[pk], writes=["KB"])
                        sc.add("act", lambda e, pp=pp, tsl=tsl: e.activation(out=KB[64:128, tsl], in_=pp[64:128, :],
                                                                             func=AF.Copy), reads=[pk], writes=["KB"])
                        pp, pk = proj_tile(t, "VS", hTt, hk)
                        sc.add("act", lambda e, pp=pp: e.activation(out=vtS[:, :], in_=pp[:, :], func=AF.Copy),
                               reads=[pk], writes=["vtS"])
                        pv = psV[1]
                        for bi in range(4):
                            sc.add("pe", lambda e, bi=bi, pv=pv: e.transpose(out=pv[:, bi * 128:(bi + 1) * 128],
                                                                            in_=vtS[:, bi * 128:(bi + 1) * 128],
                                                                            identity=ident),
                                   reads=["vtS", "cmat"], writes=["psVA1"])
                        sc.add("dve", lambda e, pv=pv, t=t: e.tensor_copy(
                            out=Vs[:, 4 * t:4 * t + 4, :].rearrange("p b d -> p (b d)"), in_=pv[:, 0:512]),
                            reads=["psVA1"], writes=["Vs"])
                        pp, pk = proj_tile(t, "QC", hTt, hk)
                        sc.add("act", lambda e, pp=pp, tsl=tsl: e.activation(out=QC[0:64, tsl], in_=pp[0:64, :],
                                                                             func=AF.Copy, scale=0.125),
                               reads=[pk], writes=["QC"])
                        pp, pk = proj_tile(t, "KC", hTt, hk)
                        sc.add("dve", lambda e, pp=pp, tsl=tsl: e.tensor_copy(out=KC[0:64, tsl], in_=pp[0:64, :]),
                               reads=[pk], writes=["KC"])
                        pp, pk = proj_tile(t, "VG", hTt, hk)
                        sc.add("act", lambda e, pp=pp, qq=qq: e.activation(
                            out=vt3[0:64, :].rearrange("p (r i) -> p r i", r=16)[:, :, 32 * qq:32 * qq + 32],
                            in_=deint(pp[0:64, :], 16), func=AF.Copy), reads=[pk], writes=["vt3"])
                        if qq == 3:
                            for r in range(16):
                                pv = psV[r // 8]
                                pvk = "psVA%d" % (r // 8)
                                rr = r % 8
                                sc.add("pe", lambda e, r=r, rr=rr, pv=pv: e.transpose(out=pv[:, rr * 128:(rr + 1) * 128],
                                                                                      in_=vt3[:, r * 128:(r + 1) * 128],
                                                                                      identity=ident),
                                       reads=["vt3", "cmat"], writes=[pvk])
                            for r in range(16):
                                pv = psV[r // 8]
                                pvk = "psVA%d" % (r // 8)
                                rr = r % 8
                                if False:
                                    sc.add("act", lambda e, pv=pv, nn=nn, r=r, rr=rr: e.activation(
                                        out=Vd[2][:, r * 4 + nn, 0:64], in_=pv[:, rr * 128:rr * 128 + 64], func=AF.Copy),
                                        reads=[pvk], writes=["Vd2"])
                                else:
                                    sc.add("dve", lambda e, pv=pv, nn=nn, r=r, rr=rr: e.tensor_copy(
                                        out=Vd[2][:, r * 4 + nn, 0:64], in_=pv[:, rr * 128:rr * 128 + 64]),
                                        reads=[pvk], writes=["Vd2"])

                    tl_ = list(tlist if tlist is not None else range(ntA))
                    for half in range(2):
                        norm_half(0, tl_[0], half, "stats")
                        norm_half(0, tl_[0], half, "trans")
                    for ti, t in enumerate(tl_):
                        nxt = ti + 1 < len(tl_)
                        if nxt:
                            norm_half(ti + 1, tl_[ti + 1], 0, "stats")
                        proj_part1(ti, t)
                        if nxt:
                            norm_half(ti + 1, tl_[ti + 1], 0, "trans")
                            norm_half(ti + 1, tl_[ti + 1], 1, "stats")
                        proj_part2(ti, t)
                        if nxt:
                            norm_half(ti + 1, tl_[ti + 1], 1, "trans")
                sc.barrier()
                if phases < 2:
                    sc.enabled = False

                PD = contextlib.ExitStack()
                with PD:
                    acc = sb("acc", [65, S], F32, PD)
                    dm = sb("dm", [128, 768], F32, PD)
                    sel_sb = sb("sel_sb", [128, 64], F32, PD)
                    sc.add("sp", lambda e: e.dma_start(out=dm[:], in_=dmask[:, :]), writes=["dm"], dma="c1a")
                    sc.add("sp", lambda e: e.dma_start(out=sel_sb[:], in_=selm[:, :]), writes=["sel"], dma="c1b")
                    pex = [sb("pex%d" % i, [128, 512], F32, PD) for i in range(2)]
                    pbf = [sb("pbf%d" % i, [128, 512], BF16, PD) for i in range(2)]
                    rec = sb("rec", [64, 512], F32, PD)
                    oa = [sb("oa%d" % i, [64, 512], BF16, PD) for i in range(2)]
                    psS = [ps("psS%d" % i, [128, 512], F32, PD) for i in range(2)]
                    psO = [ps("psOd%d" % i, [128, 512], F32, PD) for i in range(2)]
                    psD = ps("psDd", [128, 512], F32, PD)
                    groups = [
                        (0, 1, QA, KA, 0),
                        (1, 4, QA, KA, 64),
                        (2, 16, QB, KB, 0),
                    ]
                    qkey = {0: "QA", 1: "QA", 2: "QB"}
                    kkey = {0: "KA", 1: "KA", 2: "KB"}
                    pairno = 0
                    pending = []
                    import os as _os3
                    _dg = _os3.environ.get("DILG")
                    for (g, d, Qt, Kt, ro) in groups:
                        if _dg is not None and str(g) not in _dg:
                            continue
                        nb = 64 // d
                        rows = slice(ro, ro + 64)
                        if d == 1:
                            banks = [[(0, n0 + i) for i in range(4)] for n0 in range(0, 64, 4)]
                        else:
                            banks = [[(r0 + i, n) for i in range(4)] for n in range(nb) for r0 in range(0, d, 4)]
                        for bank in banks:
                            po = psO[pairno % 2]
                            pok = "psOd%d" % (pairno % 2)
                            pairno += 1
                            for half in range(2):
                                blks = bank[2 * half:2 * half + 2]
                                i2 = (pairno * 2 + half) % 2
                                pS = psS[i2]
                                psk = "psS%d" % i2
                                for bi, (r, n) in enumerate(blks):
                                    blk = r * nb + n
                                    qsl = slice(blk * 128, (blk + 1) * 128)
                                    pblk = blk - 1 if n > 0 else blk
                                    ksl_p = slice(pblk * 128, (pblk + 1) * 128)
                                    sc.add("pe", lambda e, pS=pS, bi=bi, ksl_p=ksl_p, qsl=qsl, Kt=Kt, Qt=Qt, rows=rows: e.matmul(
                                        pS[:, bi * 256:bi * 256 + 128], lhsT=Kt[rows, ksl_p], rhs=Qt[rows, qsl],
                                        start=True, stop=True), reads=[qkey[g], kkey[g]], writes=[psk])
                                    sc.add("pe", lambda e, pS=pS, bi=bi, qsl=qsl, Kt=Kt, Qt=Qt, rows=rows: e.matmul(
                                        pS[:, bi * 256 + 128:bi * 256 + 256], lhsT=Kt[rows, qsl], rhs=Qt[rows, qsl],
                                        start=True, stop=True), reads=[qkey[g], kkey[g]], writes=[psk])
                                pe_ = pex[i2]
                                pb_ = pbf[i2]
                                sc.add("act", lambda e, pe_=pe_, pS=pS: e.activation(out=pe_[:], in_=pS[:, :], func=AF.Exp),
                                       reads=[psk], writes=["pex%d" % i2])
                                for bi in range(2):
                                    sc.add("dve", lambda e, pe_=pe_, pb_=pb_, bi=bi, g=g: e.tensor_tensor(
                                        out=pb_[:, bi * 256:(bi + 1) * 256], in0=pe_[:, bi * 256:(bi + 1) * 256],
                                        in1=dm[:, g * 256:(g + 1) * 256], op=ALU.mult),
                                        reads=["pex%d" % i2, "dm"], writes=["pbf%d" % i2])
                                def make_av(blks=blks, half=half, po=po, pok=pok, pb_=pb_, i2=i2, g=g, nb=nb, d=d, bank=bank):
                                    def f():
                                        for bi, (r, n) in enumerate(blks):
                                            blk = r * nb + n
                                            slot = 2 * half + bi
                                            osl = po[0:65, slot * 128:(slot + 1) * 128]
                                            if n > 0:
                                                sc.add("pe", lambda e, osl=osl, pb_=pb_, bi=bi, blk=blk, g=g: e.matmul(
                                                    osl, lhsT=Vd[g][:, blk - 1, 0:65], rhs=pb_[:, bi * 256:bi * 256 + 128],
                                                    start=True, stop=False),
                                                    reads=["pbf%d" % i2, "Vd%d" % g, "Vd%d_ones" % g], writes=[pok])
                                            sc.add("pe", lambda e, osl=osl, pb_=pb_, bi=bi, blk=blk, g=g, n=n: e.matmul(
                                                osl, lhsT=Vd[g][:, blk, 0:65], rhs=pb_[:, bi * 256 + 128:bi * 256 + 256],
                                                start=(n == 0), stop=True),
                                                reads=["pbf%d" % i2, "Vd%d" % g, "Vd%d_ones" % g], writes=[pok])

                                        if half == 1:
                                            r0, n0 = bank[0]
                                            if d == 1:
                                                dst = acc[0:65, n0 * 128:(n0 + 4) * 128]
                                                sc.add("act", lambda e, dst=dst, po=po: e.activation(out=dst, in_=po[0:65, :], func=AF.Copy),
                                                       reads=[pok], writes=["acc"])
                                            else:
                                                base = 128 * n0 * d + r0
                                                span = acc[0:65, 128 * n0 * d:128 * (n0 + 1) * d].rearrange("p (i r) -> p r i", r=d)
                                                dst = span[:, r0:r0 + 4, :]
                                                src = po[0:65, :].rearrange("p (r i) -> p r i", r=4)
                                                sc.add("dve", lambda e, dst=dst, src=src: e.tensor_tensor(out=dst, in0=dst, in1=src, op=ALU.add),
                                                       reads=[pok, "acc"], writes=["acc"])

                                    return f

                                pending.append(make_av())
                                if len(pending) > 1:
                                    pending.pop(0)()
                    while pending:
                        pending.pop(0)()
                    for ch in range(16):
                        csl = slice(ch * 512, (ch + 1) * 512)
                        sc.add("pe", lambda e, csl=csl: e.matmul(psD[0:64, :], lhsT=sel_sb[0:65, :], rhs=acc[0:65, csl],
                                                                 start=True, stop=True),
                               reads=["acc", "sel"], writes=["psDd"])
                        sc.add("dve", lambda e: e.reciprocal(out=rec[:], in_=psD[0:64, :]), reads=["psDd"], writes=["rec"])
                        oo = oa[ch % 2]
                        sc.add("dve", lambda e, oo=oo, csl=csl: e.tensor_tensor(out=oo[:], in0=acc[0:64, csl], in1=rec[:],
                                                                                op=ALU.mult),
                               reads=["acc", "rec"], writes=["oa%d" % (ch % 2)])
                        sc.add("sp", lambda e, oo=oo, ch=ch: e.dma_start(
                            out=o_locq[ch // 4].ap()[0:64, (ch % 4) * 512:(ch % 4 + 1) * 512], in_=oo[:]),
                               reads=["oa%d" % (ch % 2)], writes=["o_loc_a%d" % ch], dma="oa%d" % (ch % 2))
                sc.barrier()
                if phases < 3:
                    sc.enabled = False

            PS_ = contextlib.ExitStack()
            with PS_:
                NB = 4
                cm = sb("cm", [128, 2048], BF16, PS_)
                sc.add("sp", lambda e: e.dma_start(out=cm[:], in_=cmask[:, :]), writes=["cm"], dma="c2")
                Eb = [sb("Eb%d" % i, [128, 512], F32, PS_) for i in range(NB)]
                Lb = [sb("Lb%d" % i, [128, 512], BF16, PS_) for i in range(NB)]
                Ab = [sb("Ab%d" % i, [128, 512], BF16, PS_) for i in range(NB)]
                LsF = sb("LsF", [128, 512], F32, PS_)
                LsB = [sb("LsB%d" % i, [128, 512], BF16, PS_) for i in range(4)]
                ost = [sb("ost%d" % i, [64, 512], BF16, PS_) for i in range(2)]
                psZ = [ps("psZ%d" % i, [128, 512], F32, PS_) for i in range(2)]
                psB = [ps("psB%d" % i, [128, 512], F32, PS_) for i in range(2)]
                psOs = [ps("psOs%d" % i, [128, 512], F32, PS_) for i in range(2)]
                heads = [
                    (QB, KB, slice(64, 128), "QB", "KB", slice(0, 64), 64),
                    (QC, KC, slice(0, 64), "QC", "KC", slice(64, 128), 128),
                ]
                pairs = []
                qn = 0
                for qt in range(16):
                    for hi, hd in enumerate(heads):
                        kmax = 4 * qt + 3
                        kmin = 0 if sb_limit is None else max(0, kmax - sb_limit + 1)
                        for kb in range(kmax, kmin - 1, -1):
                            pairs.append(dict(h=hd, hi=hi, qt=qt, kb=kb, first=(kb == kmax), last=(kb == kmin),
                                              u=(kb - 4 * qt) if kb >= 4 * qt else None, qn=qn))
                        qn += 1

                Xb = [sb("Xb%d" % i, [128, 512], F32, PS_) for i in range(2)]

                def stage1(i, p):
                    Qt, Kt, rows, qk, kk, vcols, orow = p["h"]
                    qsl = slice(p["qt"] * 512, (p["qt"] + 1) * 512)
                    ksl = slice(p["kb"] * 128, (p["kb"] + 1) * 128)
                    z = psZ[i % 2]
                    zk = "psZ%d" % (i % 2)
                    E, L = Eb[i % NB], Lb[i % NB]
                    ek, lk = "Eb%d" % (i % NB), "Lb%d" % (i % NB)
                    sc.add("pe", lambda e: e.matmul(z[:, :], lhsT=Kt[rows, ksl], rhs=Qt[rows, qsl], start=True, stop=True),
                           reads=[qk, kk], writes=[zk])
                    sc.add("act", lambda e: e.activation(out=E[:], in_=z[:, :], func=AF.Exp), reads=[zk], writes=[ek])
                    if p["u"] is not None:
                        u = p["u"]
                        sc.add("dve", lambda e: e.tensor_tensor(out=E[:], in0=E[:], in1=cm[:, u * 512:(u + 1) * 512],
                                                                op=ALU.mult), reads=[ek, "cm"], writes=[ek])
                    sc.add("act", lambda e: e.activation(out=L[:], in_=E[:], func=AF.Ln, bias=1.0), reads=[ek], writes=[lk])
                    if not p["last"]:
                        nxt = LsB[(p["cnt"] + 1) % 4]
                        nxtk = "LsB%d" % ((p["cnt"] + 1) % 4)
                        if p["first"]:
                            sc.add("dve", lambda e: e.tensor_copy(out=nxt[:], in_=L[:]), reads=[lk], writes=[nxtk])
                            sc.add("dve", lambda e: e.tensor_copy(out=LsF[:], in_=L[:]), reads=[lk], writes=["LsF"])
                        else:
                            sc.add("dve", lambda e: e.tensor_tensor(out=LsF[:], in0=LsF[:], in1=L[:], op=ALU.add),
                                   reads=[lk, "LsF"], writes=["LsF"])
                            sc.add("dve", lambda e: e.tensor_copy(out=nxt[:], in_=LsF[:]), reads=["LsF"], writes=[nxtk])

                def stage2(i, p):
                    Qt, Kt, rows, qk, kk, vcols, orow = p["h"]
                    qsl = slice(p["qt"] * 512, (p["qt"] + 1) * 512)
                    kb = p["kb"]
                    bq = psB[i % 2]
                    bk = "psB%d" % (i % 2)
                    E, L, A = Eb[i % NB], Lb[i % NB], Ab[i % NB]
                    ek, lk, ak = "Eb%d" % (i % NB), "Lb%d" % (i % NB), "Ab%d" % (i % NB)
                    X = Xb[i % 2]
                    xk_ = "Xb%d" % (i % 2)
                    first, last = p["first"], p["last"]
                    cur = LsB[p["cnt"] % 4]
                    curk = "LsB%d" % (p["cnt"] % 4)
                    sc.add("pe", lambda e: e.matmul(bq[:, :], lhsT=negtri, rhs=L[:], start=True, stop=first),
                           reads=[lk, "cmat"], writes=[bk])
                    if not first:
                        sc.add("pe", lambda e: e.matmul(bq[:, :], lhsT=negones, rhs=cur[:], start=False, stop=True),
                               reads=[curk, "cmat"], writes=[bk])
                    sc.add("act", lambda e: e.activation(out=X[:], in_=bq[:, :], func=AF.Exp), reads=[bk], writes=[xk_])
                    sc.add("dve", lambda e: e.tensor_tensor(out=A[:], in0=E[:], in1=X[:], op=ALU.mult),
                           reads=[ek, xk_], writes=[ak])

                def stage3(i, p):
                    Qt, Kt, rows, qk, kk, vcols, orow = p["h"]
                    qsl = slice(p["qt"] * 512, (p["qt"] + 1) * 512)
                    kb = p["kb"]
                    A = Ab[i % NB]
                    ak = "Ab%d" % (i % NB)
                    first, last = p["first"], p["last"]
                    po = psOs[p["qn"] % 2]
                    pok = "psOs%d" % (p["qn"] % 2)
                    sc.add("pe", lambda e: e.matmul(po[0:64, :], lhsT=Vs[:, kb, vcols], rhs=A[:], start=first, stop=last,
                                                    skip_group_check=True),
                           reads=[ak, "Vs"], writes=[pok])
                    if last:
                        oo = ost[p["qn"] % 2]
                        ook = "ost%d" % (p["qn"] % 2)
                        sc.add("act", lambda e: e.activation(out=oo[:], in_=po[0:64, :], func=AF.Copy), reads=[pok], writes=[ook])
                        qt_ = p["qt"]
                        sc.add("sp", lambda e: e.dma_start(
                            out=o_locq[qt_ // 4].ap()[orow:orow + 64, (qt_ % 4) * 512:(qt_ % 4 + 1) * 512], in_=oo[:]),
                               reads=[ook], writes=["o_loc_b%d_%d" % (orow, qt_)], dma=ook)

                cnt = 0
                for p in pairs:
                    p["cnt"] = cnt
                    cnt += 1
                DEPTH = 2
                for i in range(len(pairs) + 2 * DEPTH):
                    if i < len(pairs):
                        stage1(i, pairs[i])
                    if 0 <= i - DEPTH < len(pairs):
                        stage2(i - DEPTH, pairs[i - DEPTH])
                    if 0 <= i - 2 * DEPTH < len(pairs):
                        p3 = pairs[i - 2 * DEPTH]
                        stage3(i - 2 * DEPTH, p3)
                        if phases >= 4 and p3["last"] and p3["hi"] == 1 and p3["qt"] % 4 == 3:
                            add_gather(p3["qt"] // 4)
            sc.barrier(keep_prefix="o_all", skip_src=("dma", "cc"))
            if phases < 4:
                sc.enabled = False

        oallk = ["o_all%d" % q for q in range(4)]
        if debug:
            for q in range(4):
                sc.add("pool", lambda e, q=q: e.dma_start(out=dbg[:, q * TOK_OWN:(q + 1) * TOK_OWN], in_=o_all.ap()[q]),
                       reads=oallk, writes=["dbg%d" % q], dma="dbg")

        if phases < 5:
            sc.enabled = False
        PC = contextlib.ExitStack()
        with PC:
            xr = sb("xr", [128, 16, D], F32, PC)
            hT2 = sb("hT2", [128, 8, TOK_OWN], BF16, PC)
            junk = sb("junkC", [128, D], BF16, PC)
            ss = sb("ssC", [128, 4], F32, PC)
            lnv = sb("lnvC", [128, 4], F32, PC)
            rstd = sb("rstdC", [128, 4], F32, PC)
            xn = sb("xnC", [128, 4, D], BF16, PC)
            xo = x_own.rearrange("(tt p) d -> p tt d", p=128)
            for q4 in range(4):
                sc.add("sp", lambda e, q4=q4: e.dma_start(out=xr[:, 4 * q4:4 * q4 + 4, :], in_=xo[:, 4 * q4:4 * q4 + 4, :]),
                       writes=["xr%d" % q4], dma="xr%d" % q4)
            psT = [ps("psTC%d" % i, [128, 1024], BF16, PC) for i in range(2)]
            psM = [ps("psMC%d" % i, [128, 512], F32, PC) for i in range(6)]
            pmc = [0]

            def nextps():
                i = pmc[0] % 6
                pmc[0] += 1
                return psM[i], "psMC%d" % i

            C1 = contextlib.ExitStack()
            with C1:
                Wg = sb("Wg", [128, 8, 2 * D], BF16, C1)
                Wud = sb("Wud", [128, 2, D], BF16, C1)
                Wus = sb("Wus", [128, 4, D], BF16, C1)
                Wo = sb("Wo", [128, 8, D], BF16, C1)
                oA = sb("oA", [128, 2, 512], BF16, C1)
                oB = sb("oB", [128, 4, 512], BF16, C1)
                hT1 = sb("hT1", [128, 8, 512], BF16, C1)
                Ga = sb("Ga", [128, 512], F32, C1)
                Gb = sb("Gb", [128, 512], F32, C1)
                t1 = sb("t1", [128, 512], F32, C1)
                t2 = sb("t2", [128, 512], F32, C1)
                mg = sb("mg", [128, 8, 512], BF16, C1)
                wgv = w_gate.rearrange("(c p) n -> p c n", p=128)
                for c0 in range(0, 8, 2):
                    sc.add("pool", lambda e, c0=c0: e.dma_start(out=Wg[:, c0:c0 + 2, :], in_=wgv[:, c0:c0 + 2, :]),
                           writes=["Wg%d" % c0], dma="wg")
                wgk = ["Wg%d" % c0 for c0 in range(0, 8, 2)]
                sc.add("pool", lambda e: e.dma_start(out=Wud[:], in_=w_ud.rearrange("(c p) n -> p c n", p=128)),
                       writes=["Wud"], dma="wud")
                sc.add("pool", lambda e: e.dma_start(out=Wus[:], in_=w_us.rearrange("(c p) n -> p c n", p=128)),
                       writes=["Wus"], dma="wus")
                wov = w_o.rearrange("(c p) n -> p c n", p=128)
                for c0 in range(0, 8, 4):
                    sc.add("pool", lambda e, c0=c0: e.dma_start(out=Wo[:, c0:c0 + 4, :], in_=wov[:, c0:c0 + 4, :]),
                           writes=["Wo%d" % c0], dma="wo")
                wok = ["Wo0", "Wo4"]
                jq_cache = {}

                def get_jq(e):
                    if "jq" not in jq_cache:
                        pid = e.partition_id()
                        jq_cache["jq"] = bass.ds(pid % 4, 1)
                    return jq_cache["jq"]

                omv = o_mine.ap()

                def mk_om(r):
                    def f(e):
                        jq = get_jq(e)
                        return e.dma_start(out=omv[r * 192:(r + 1) * 192, :].rearrange("(o f) t -> o f t", o=1),
                                           in_=o_all.ap()[jq, r * 192:(r + 1) * 192, :])
                    return f

                for r in range(4):
                    sc.add("pool", mk_om(r), reads=oallk, writes=["o_mine%d" % r], dma="omine")

                def mk_oa(cc, hh, T):
                    def f(e):
                        r = 2 * cc + hh
                        return e.dma_start(out=oA[hh * 64:(hh + 1) * 64, cc, :],
                                           in_=omv[r * 192:r * 192 + 64, T * 512:(T + 1) * 512])
                    return f

                def mk_ob(cc, T):
                    def f(e):
                        return e.dma_start(out=oB[:, cc, :], in_=omv[cc * 192 + 64:cc * 192 + 192, T * 512:(T + 1) * 512])
                    return f

                for T in range(4):
                    tsl = slice(T * 512, (T + 1) * 512)
                    for cc in range(2):
                        for hh in range(2):
                            sc.add("sp", mk_oa(cc, hh, T), reads=["o_mine%d" % r_ for r_ in range(4)], writes=["oA%d" % (2 * cc + hh)], dma="oAg")
                    for cc in range(4):
                        sc.add("sp", mk_ob(cc, T), reads=["o_mine%d" % r_ for r_ in range(4)], writes=["oB%d" % cc], dma="oBg")
                    rmsnorm_T([(xr[:, 4 * T + tt, :], "xr%d" % T) for tt in range(4)], 4,
                              lambda c: (hT1[:, c, :], "hT1"), g1, None, (ss, lnv, rstd, junk, xn), psT, "C")
                    for m in range(8):
                        msl = slice(m * 128, (m + 1) * 128)
                        pga, pgak = nextps()
                        for c in range(8):
                            sc.add("pe", lambda e, c=c, pga=pga, msl=msl: e.matmul(pga[:, :], lhsT=Wg[:, c, msl], rhs=hT1[:, c, :],
                                                                                  start=(c == 0), stop=(c == 7)),
                                   reads=wgk + ["hT1"], writes=[pgak])
                        pgb, pgbk = nextps()
                        msl2 = slice(D + m * 128, D + (m + 1) * 128)
                        for c in range(8):
                            sc.add("pe", lambda e, c=c, pgb=pgb, msl2=msl2: e.matmul(pgb[:, :], lhsT=Wg[:, c, msl2], rhs=hT1[:, c, :],
                                                                                    start=(c == 0), stop=(c == 7)),
                                   reads=wgk + ["hT1"], writes=[pgbk])
                        pua, puak = nextps()
                        for c in range(2):
                            sc.add("pe", lambda e, c=c, pua=pua, msl=msl, tsl=tsl: e.matmul(pua[:, :], lhsT=Wud[:, c, msl],
                                                                                           rhs=oA[:, c, :],
                                                                                           start=(c == 0), stop=(c == 1)),
                                   reads=["Wud", "oA0", "oA1", "oA2", "oA3"], writes=[puak])
                        pub, pubk = nextps()
                        for c in range(4):
                            sc.add("pe", lambda e, c=c, pub=pub, msl=msl, tsl=tsl: e.matmul(pub[:, :], lhsT=Wus[:, c, msl],
                                                                                           rhs=oB[:, c, :],
                                                                                           start=(c == 0), stop=(c == 3)),
                                   reads=["Wus", "oB0", "oB1", "oB2", "oB3"], writes=[pubk])
                        sc.add("act", lambda e, pga=pga, m=m: e.activation(out=Ga[:], in_=pga[:, :], func=AF.Sigmoid,
                                                                           bias=bg[:, m:m + 1]),
                               reads=[pgak, "vec"], writes=["Ga"])
                        sc.add("act", lambda e, pgb=pgb, m=m: e.activation(out=Gb[:], in_=pgb[:, :], func=AF.Sigmoid,
                                                                           bias=bg[:, 8 + m:9 + m]),
                               reads=[pgbk, "vec"], writes=["Gb"])
                        sc.add("dve", lambda e, pua=pua: e.tensor_tensor(out=t1[:], in0=Ga[:], in1=pua[:, :], op=ALU.mult),
                               reads=["Ga", puak], writes=["t1"])
                        sc.add("dve", lambda e, pub=pub: e.tensor_tensor(out=t2[:], in0=Gb[:], in1=pub[:, :], op=ALU.mult),
                               reads=["Gb", pubk], writes=["t2"])
                        sc.add("pool", lambda e, m=m: e.tensor_tensor(out=mg[:, m, :], in0=t1[:], in1=t2[:], op=ALU.add),
                               reads=["t1", "t2"], writes=["mg%d" % m])
                    mgk = ["mg%d" % m for m in range(8)]
                    for tb in range(4):
                        for half in range(2):
                            py, pyk = nextps()
                            hsl = slice(half * 512, (half + 1) * 512)
                            for c in range(8):
                                sc.add("pe", lambda e, c=c, py=py, tb=tb, hsl=hsl: e.matmul(
                                    py[:, :], lhsT=mg[:, c, tb * 128:(tb + 1) * 128], rhs=Wo[:, c, hsl],
                                    start=(c == 0), stop=(c == 7)), reads=mgk + wok, writes=[pyk])
                            xa = xr[:, 4 * T + tb, hsl]
                            sc.add("dve", lambda e, xa=xa, py=py: e.tensor_tensor(out=xa, in0=xa, in1=py[:, :], op=ALU.add),
                                   reads=[pyk, "xr%d" % T], writes=["xr%d" % T])
                    rmsnorm_T([(xr[:, 4 * T + tt, :], "xr%d" % T) for tt in range(4)], 4,
                              lambda c, tsl=tsl: (hT2[:, c, tsl], "hT2_%d" % T), g2, None,
                              (ss, lnv, rstd, junk, xn), psT, "C")
            sc.barrier()
            C2 = contextlib.ExitStack()
            with C2:
                gfb = sb("gfb", [128, D], F32, C2)
                sc.add("sp", lambda e: e.dma_start(out=gfb[:], in_=gfin[:, :]), writes=["gfb"], dma="c3")
                W1q = [sb("W1q%d" % i, [128, 8, D], BF16, C2) for i in range(2)]
                W2q = [sb("W2q%d" % i, [128, 8, D], BF16, C2) for i in range(2)]
                uT = [sb("uT%d" % i, [128, 8, 512], BF16, C2) for i in range(2)]
                rl = [sb("rl%d" % i, [128, 512], F32, C2) for i in range(2)]
                yo = [sb("yo%d" % i, [128, D], F32, C2) for i in range(2)]
                w1v = w_1.rearrange("(c p) n -> p c n", p=128)
                w2v = w_2.rearrange("(c p) n -> p c n", p=128)
                it = 0
                rc = 0
                for qf in range(4):
                    s_ = qf % 2
                    sc.add("pool", lambda e, qf=qf, s_=s_: e.dma_start(out=W1q[s_][:], in_=w1v[:, :, qf * D:(qf + 1) * D]),
                           writes=["W1q%d" % s_], dma="w1q%d" % s_)
                    sc.add("pool", lambda e, qf=qf, s_=s_: e.dma_start(out=W2q[s_][:], in_=w2v[:, 8 * qf:8 * qf + 8, :]),
                           writes=["W2q%d" % s_], dma="w2q%d" % s_)
                    for T in range(4):
                        tsl = slice(T * 512, (T + 1) * 512)
                        u = uT[it % 2]
                        uk = "uT%d" % (it % 2)
                        it += 1
                        for m in range(8):
                            pu, puk = nextps()
                            for c in range(8):
                                sc.add("pe", lambda e, c=c, pu=pu, m=m, s_=s_, tsl=tsl: e.matmul(
                                    pu[:, :], lhsT=W1q[s_][:, c, m * 128:(m + 1) * 128], rhs=hT2[:, c, tsl],
                                    start=(c == 0), stop=(c == 7)), reads=["W1q%d" % s_, "hT2_%d" % T], writes=[puk])
                            r_ = rl[rc % 2]
                            rk = "rl%d" % (rc % 2)
                            rc += 1
                            sc.add("act", lambda e, r_=r_, pu=pu: e.activation(out=r_[:], in_=pu[:, :], func=AF.Relu),
                                   reads=[puk], writes=[rk])
                            sc.add("pool", lambda e, r_=r_, u=u, m=m: e.tensor_tensor(out=u[:, m, :], in0=r_[:], in1=r_[:],
                                                                                    op=ALU.mult),
                                   reads=[rk], writes=[uk + "_%d" % m])
                        uks = [uk + "_%d" % m for m in range(8)]
                        for tb in range(4):
                            for half in range(2):
                                py, pyk = nextps()
                                hsl = slice(half * 512, (half + 1) * 512)
                                for m in range(8):
                                    sc.add("pe", lambda e, m=m, py=py, u=u, tb=tb, hsl=hsl, s_=s_: e.matmul(
                                        py[:, :], lhsT=u[:, m, tb * 128:(tb + 1) * 128], rhs=W2q[s_][:, m, hsl],
                                        start=(m == 0), stop=(m == 7)), reads=uks + ["W2q%d" % s_], writes=[pyk])
                                xa = xr[:, 4 * T + tb, hsl]
                                sc.add("dve", lambda e, xa=xa, py=py: e.tensor_tensor(out=xa, in0=xa, in1=py[:, :], op=ALU.add),
                                       reads=[pyk, "xr%d" % T], writes=["xr%d" % T])
                ov = out.rearrange("(tt p) d -> p tt d", p=128)
                for T in range(4):
                    for tt in range(4):
                        sc.add("act", lambda e, T=T, tt=tt: e.activation(out=junk[:], in_=xr[:, 4 * T + tt, :], func=AF.Square,
                                                                         accum_out=ss[:, tt:tt + 1]),
                               reads=["xr%d" % T], writes=["Fjunk", "Fss%d" % tt])
                    sc.add("act", lambda e: e.activation(out=lnv[:, 0:4], in_=ss[:, 0:4], func=AF.Ln, scale=1.0 / D,
                                                         bias=eps_sb[:, 0:1]),
                           reads=["Fss%d" % tt for tt in range(4)] + ["eps"], writes=["Flnv"])
                    sc.add("act", lambda e: e.activation(out=rstd[:, 0:4], in_=lnv[:, 0:4], func=AF.Exp, scale=-0.5),
                           reads=["Flnv"], writes=["Frstd"])
                    for tt in range(4):
                        y = yo[tt % 2]
                        yk = "yo%d" % (tt % 2)
                        sc.add("dve", lambda e, T=T, tt=tt, y=y: e.scalar_tensor_tensor(
                            out=y[:], in0=xr[:, 4 * T + tt, :], scalar=rstd[:, tt:tt + 1], in1=gfb[:],
                            op0=ALU.mult, op1=ALU.mult), reads=["xr%d" % T, "Frstd", "gfb"], writes=[yk])
                        sc.add("sp", lambda e, T=T, tt=tt, y=y: e.dma_start(out=ov[:, 4 * T + tt, :], in_=y[:]),
                               reads=[yk], writes=["out"], dma=yk)

        semstack = contextlib.ExitStack()
        with semstack:
            sc.prepare(nc, semstack)
            with nc.Block() as block:
                sc.emit(nc, block)
    return nc


def _constants(j):
    bf = ml_dtypes.bfloat16
    ident = np.eye(128, dtype=np.float32)
    jj = np.arange(128)[:, None]
    kk = np.arange(128)[None, :]
    negtri = np.where(jj >= kk, -1.0, 0.0).astype(np.float32)
    negones = -np.ones((128, 128), np.float32)
    cmat = np.concatenate([ident, negtri, negones], axis=1).astype(bf)
    p = np.arange(128)[:, None]
    c = np.arange(512)[None, :]
    cm = np.concatenate([(128 * u + p < c).astype(np.float32) for u in range(4)], axis=1).astype(bf)
    kq = np.arange(128)[:, None].astype(np.float64)
    qq = np.arange(128)[None, :].astype(np.float64)
    dms = []
    for g, d in enumerate((1, 4, 16)):
        slope = 2.0 ** (-8.0 * (4 * g + j + 1) / 12.0)
        sp = qq + 128 - kq
        prev = np.where(sp <= 128, np.exp(-slope * d * sp), 0.0)
        scur = qq - kq
        cur = np.where(scur >= 0, np.exp(-slope * d * scur), 0.0)
        dms.append(np.concatenate([prev, cur], axis=1))
    dmask = np.concatenate(dms, axis=1).astype(np.float32)
    sel = np.zeros((128, 64), np.float32)
    sel[64, :] = 1.0
    return cmat, cm, dmask, sel


def _own_cols(j):
    def dq(g):
        return (4 * g + j) * 64
    cols = {}
    qa, ka, va = 0, 768, 1536
    qb, kb, vb = 2304, 2816, 3328
    s0, s1 = 2 * j, 2 * j + 1
    r = lambda o: list(range(o, o + 64))
    cols["QA"] = r(qa + dq(0)) + r(qa + dq(1))
    cols["KA"] = r(ka + dq(0)) + r(ka + dq(1))
    cols["VA"] = r(va + dq(0)) + r(va + dq(1))
    cols["QB"] = r(qa + dq(2)) + r(qb + s0 * 64)
    cols["KB"] = r(ka + dq(2)) + r(kb + s0 * 64)
    cols["VS"] = r(vb + s0 * 64) + r(vb + s1 * 64)
    cols["QC"] = r(qb + s1 * 64)
    cols["KC"] = r(kb + s1 * 64)
    cols["VG"] = r(va + dq(2))
    idx = []
    for n in OWN_TILES:
        idx += cols[n]
    return np.array(idx)


_NC_CACHE = {}


def kernel(x, norm_mix_g, w_in, b_gate, w_up_dil, w_up_sb, w_out, norm_mlp_g, w_mlp_in, w_mlp_out, norm_final_g,
           _debug=False, _sb_limit=None, _phases=9):
    x = np.asarray(x, np.float32)
    w_in0 = np.asarray(w_in, np.float32)[0]
    key = (_debug, _sb_limit, _phases)
    if key not in _NC_CACHE:
        _NC_CACHE[key] = build_nc(debug=_debug, sb_limit=_sb_limit, phases=_phases)
    nc = _NC_CACHE[key]
    vecs = np.concatenate([
        np.asarray(norm_mix_g, np.float32)[0].reshape(8, 128).T,
        np.asarray(norm_mlp_g, np.float32)[0].reshape(8, 128).T,
        np.asarray(b_gate, np.float32)[0].reshape(16, 128).T], axis=1)
    vecs = np.ascontiguousarray(vecs)
    gfin = np.ascontiguousarray(np.broadcast_to(np.asarray(norm_final_g, np.float32)[None, :], (128, D)))
    w_gate = np.ascontiguousarray(w_in0[:, 3840:5888])
    shared = {
        "w_gate": w_gate,
        "w_ud": np.ascontiguousarray(np.asarray(w_up_dil, np.float32)[0]),
        "w_us": np.ascontiguousarray(np.asarray(w_up_sb, np.float32)[0]),
        "w_o": np.ascontiguousarray(np.asarray(w_out, np.float32)[0]),
        "w_1": np.ascontiguousarray(np.asarray(w_mlp_in, np.float32)[0]),
        "w_2": np.ascontiguousarray(np.asarray(w_mlp_out, np.float32)[0]),
        "vecs": vecs, "gfin": gfin,
    }
    in_maps = []
    for c in range(NCORES):
        b, j = c // 4, c % 4
        cmat, cm, dmask, sel = _constants(j)
        m = dict(shared)
        m["x_full"] = np.ascontiguousarray(x[b])
        m["x_own"] = np.ascontiguousarray(x[b, j * TOK_OWN:(j + 1) * TOK_OWN])
        m["w_own"] = np.ascontiguousarray(w_in0[:, _own_cols(j)])
        m["cmat"] = cmat
        m["cmask"] = cm
        m["dmask"] = dmask
        m["selm"] = sel
        in_maps.append(m)
    res = run_bass_kernel_spmd(nc, in_maps, core_ids=list(range(NCORES)))
    outp = np.empty((2, S, D), np.float32)
    for c in range(NCORES):
        b, j = c // 4, c % 4
        outp[b, j * TOK_OWN:(j + 1) * TOK_OWN] = np.asarray(res.results[c]["y_out"], np.float32)
    if _debug:
        return outp, [np.asarray(res.results[c]["dbg"]) for c in range(NCORES)]
    return outp
```
